# Optimizing a Trainium2 kernel written in Bass

```python
import math
import jax, jax.numpy as jnp
from jax import lax
import numpy as np

D_MODEL = 1024
BATCH = 16
SEQ = 2048
DEPTH = 2

N_MIXERS = 2
N_ATTN_LAYERS = (DEPTH + 1) // 2
N_MLSTM_LAYERS = DEPTH // 2

DA_HEADS = 8
DA_HEAD_DIM = 64
DA_V_DIM = 2 * DA_HEAD_DIM
DA_PROJ = 3 * DA_HEADS * 2 * DA_HEAD_DIM
Q_BLOCK = 128

REL_BUCKETS = 32
REL_MAX_DIST = 128

ML_HEADS = 4
ML_V_DIM = D_MODEL // ML_HEADS
ML_QK_DIM = ML_V_DIM // 2
ML_CHUNK = 64
CONV_WIDTH = 4
ML_QK_WIDTH = 2 * ML_HEADS * ML_QK_DIM
ML_V_WIDTH = ML_HEADS * ML_V_DIM
ML_PROJ = ML_QK_WIDTH + 2 * ML_V_WIDTH + 2 * ML_HEADS

D_FF = 4 * D_MODEL

EPS = 1e-6

kernel_name = "hybrid_diffattn_mlstm_sqrelu"


def rmsnorm(x, g):
    xf = x.astype(jnp.float32)
    y = xf * lax.rsqrt(jnp.mean(xf * xf, axis=-1, keepdims=True) + EPS)
    return (y * g.astype(jnp.float32)).astype(x.dtype)


def t5_causal_bucket(dist):
    max_exact = REL_BUCKETS // 2
    d = jnp.maximum(dist, 1).astype(jnp.float32)
    large = max_exact + (jnp.log(d / max_exact) / math.log(REL_MAX_DIST / max_exact)
                         * (REL_BUCKETS - max_exact)).astype(jnp.int32)
    large = jnp.minimum(large, REL_BUCKETS - 1)
    return jnp.where(dist < max_exact, dist, large)


def diff_attention(h, w_in, lq1, lk1, lq2, lk2, subln_w, w_out, rel_table, lambda_init):
    B, S, _ = h.shape
    qkv = h @ w_in
    q, k, v = jnp.split(qkv, 3, axis=-1)
    q = q.reshape(B, S, DA_HEADS, 2, DA_HEAD_DIM)
    k = k.reshape(B, S, DA_HEADS, 2, DA_HEAD_DIM)
    v = v.reshape(B, S, DA_HEADS, DA_V_DIM)
    lam = (jnp.exp(jnp.sum(lq1.astype(jnp.float32) * lk1.astype(jnp.float32)))
           - jnp.exp(jnp.sum(lq2.astype(jnp.float32) * lk2.astype(jnp.float32)))
           + lambda_init)
    scale = DA_HEAD_DIM ** -0.5
    outs = []
    for blk in range(S // Q_BLOCK):
        q0 = blk * Q_BLOCK
        kend = q0 + Q_BLOCK
        qb = q[:, q0:kend]
        kb = k[:, :kend]
        vb = v[:, :kend]
        dist = (q0 + jnp.arange(Q_BLOCK))[:, None] - jnp.arange(kend)[None, :]
        bias = rel_table[t5_causal_bucket(jnp.maximum(dist, 0))]
        bias = jnp.transpose(bias, (2, 0, 1)).astype(jnp.float32)
        logits = jnp.einsum('bqhmd,bkhmd->bmhqk', qb, kb).astype(jnp.float32) * scale + bias
        logits = jnp.where(dist >= 0, logits, -jnp.inf)
        p = jax.nn.softmax(logits, axis=-1)
        attn = p[:, 0] - lam * p[:, 1]
        outs.append(jnp.einsum('bhqk,bkhd->bqhd', attn.astype(vb.dtype), vb))
    o = jnp.concatenate(outs, axis=1)
    o = rmsnorm(o, subln_w) * (1.0 - lambda_init)
    return o.reshape(B, S, DA_HEADS * DA_V_DIM) @ w_out


def causal_depthwise_conv(x, w, b):
    C = x.shape[-1]
    y = lax.conv_general_dilated(x, w[:, None, :].astype(x.dtype), window_strides=(1,),
                                 padding=[(CONV_WIDTH - 1, 0)],
                                 dimension_numbers=('NWC', 'WIO', 'NWC'),
                                 feature_group_count=C)
    return y + b.astype(x.dtype)


def mlstm_chunkwise(q, k, v, i_pre, log_f):
    B, S, H, dk = q.shape
    dv = v.shape[-1]
    L = ML_CHUNK
    NC = S // L
    f32 = jnp.float32
    q = (q.astype(f32) * dk ** -0.5).reshape(B, NC, L, H, dk).transpose(0, 3, 1, 2, 4)
    k = k.astype(f32).reshape(B, NC, L, H, dk).transpose(0, 3, 1, 2, 4)
    v = v.astype(f32).reshape(B, NC, L, H, dv).transpose(0, 3, 1, 2, 4)
    ig = i_pre.astype(f32).reshape(B, NC, L, H).transpose(0, 3, 1, 2)
    lf = log_f.astype(f32).reshape(B, NC, L, H).transpose(0, 3, 1, 2)
    b = jnp.cumsum(lf, axis=-1)
    b_last = b[..., -1]

    a = b_last[..., None] - b + ig
    m_loc = jnp.max(a, axis=-1)
    wgt = jnp.exp(a - m_loc[..., None])
    C_loc = jnp.einsum('bhcsv,bhcsk->bhcvk', wgt[..., None] * v, k)
    n_loc = jnp.einsum('bhcs,bhcsk->bhck', wgt, k)

    def step(carry, inp):
        C, n, m = carry
        bl, Cl, nl, ml = inp
        m_new = jnp.maximum(bl + m, ml)
        decay = jnp.exp(bl + m - m_new)
        gain = jnp.exp(ml - m_new)
        C_new = decay[..., None, None] * C + gain[..., None, None] * Cl
        n_new = decay[..., None] * n + gain[..., None] * nl
        return (C_new, n_new, m_new), (C, n, m)

    init = (jnp.zeros((B, H, dv, dk), f32), jnp.zeros((B, H, dk), f32), jnp.zeros((B, H), f32))
    xs = (jnp.moveaxis(b_last, 2, 0), jnp.moveaxis(C_loc, 2, 0),
          jnp.moveaxis(n_loc, 2, 0), jnp.moveaxis(m_loc, 2, 0))
    _, (C_prev, n_prev, m_prev) = lax.scan(step, init, xs)
    C_prev = jnp.moveaxis(C_prev, 0, 2)
    n_prev = jnp.moveaxis(n_prev, 0, 2)
    m_prev = jnp.moveaxis(m_prev, 0, 2)

    causal = jnp.tril(jnp.ones((L, L), dtype=bool))
    D = b[..., :, None] - b[..., None, :] + ig[..., None, :]
    D = jnp.where(causal, D, -jnp.inf)
    m_inter = b + m_prev[..., None]
    m_j = jnp.maximum(m_inter, jnp.max(D, axis=-1))
    Sc = jnp.einsum('bhcjk,bhcsk->bhcjs', q, k) * jnp.exp(D - m_j[..., None])
    inter = jnp.exp(m_inter - m_j)
    num = (jnp.einsum('bhcjs,bhcsv->bhcjv', Sc, v)
           + inter[..., None] * jnp.einsum('bhcjk,bhcvk->bhcjv', q, C_prev))
    den = jnp.sum(Sc, axis=-1) + inter * jnp.einsum('bhcjk,bhck->bhcj', q, n_prev)
    h = num / jnp.maximum(jnp.abs(den), jnp.exp(-m_j))[..., None]
    return h.transpose(0, 2, 3, 1, 4).reshape(B, S, H, dv)


def mlstm_mixer(h, w_in, b_i, b_f, conv_w, conv_b, head_norm_w, w_out):
    B, S, _ = h.shape
    proj = h @ w_in
    qk = proj[..., :ML_QK_WIDTH]
    v = proj[..., ML_QK_WIDTH:ML_QK_WIDTH + ML_V_WIDTH]
    o_pre = proj[..., ML_QK_WIDTH + ML_V_WIDTH:ML_QK_WIDTH + 2 * ML_V_WIDTH]
    gates = proj[..., ML_QK_WIDTH + 2 * ML_V_WIDTH:].astype(jnp.float32)
    qk = jax.nn.silu(causal_depthwise_conv(qk, conv_w, conv_b))
    q = qk[..., :ML_QK_WIDTH // 2].reshape(B, S, ML_HEADS, ML_QK_DIM)
    k = qk[..., ML_QK_WIDTH // 2:].reshape(B, S, ML_HEADS, ML_QK_DIM)
    v = v.reshape(B, S, ML_HEADS, ML_V_DIM)
    i_pre = gates[..., :ML_HEADS] + b_i.astype(jnp.float32)
    log_f = jax.nn.log_sigmoid(gates[..., ML_HEADS:] + b_f.astype(jnp.float32))
    ht = mlstm_chunkwise(q, k, v, i_pre, log_f)
    ht = rmsnorm(ht, head_norm_w.reshape(ML_HEADS, ML_V_DIM))
    out = jax.nn.sigmoid(o_pre.astype(jnp.float32)) * ht.reshape(B, S, ML_V_WIDTH)
    return out.astype(h.dtype) @ w_out


def sqrelu_mlp(h, w1, w2):
    return jnp.square(jax.nn.relu(h @ w1)) @ w2


def setup_inputs(seed: int = 0) -> dict:
    key = jax.random.key(seed)
    ks = jax.random.split(key, 24)
    f32 = jnp.float32
    nA, nM = N_ATTN_LAYERS, N_MLSTM_LAYERS

    def nrm(k, shape, scale):
        return jax.random.normal(k, shape, f32) * scale

    def gain(k, shape):
        return 1.0 + 0.05 * jax.random.normal(k, shape, f32)

    return {
        "x": jax.random.normal(ks[0], (BATCH, SEQ, D_MODEL), f32),
        "rel_bias": nrm(ks[1], (REL_BUCKETS, DA_HEADS), 0.5),
        "attn_norm": gain(ks[2], (nA, D_MODEL)),
        "attn_w_in": nrm(ks[3], (nA, D_MODEL, DA_PROJ), D_MODEL ** -0.5),
        "attn_lambda_q1": nrm(ks[4], (nA, DA_HEAD_DIM), 0.1),
        "attn_lambda_k1": nrm(ks[5], (nA, DA_HEAD_DIM), 0.1),
        "attn_lambda_q2": nrm(ks[6], (nA, DA_HEAD_DIM), 0.1),
        "attn_lambda_k2": nrm(ks[7], (nA, DA_HEAD_DIM), 0.1),
        "attn_subln": gain(ks[8], (nA, DA_V_DIM)),
        "attn_w_out": nrm(ks[9], (nA, DA_HEADS * DA_V_DIM, D_MODEL), (DA_HEADS * DA_V_DIM) ** -0.5),
        "mlstm_norm": gain(ks[10], (nM, D_MODEL)),
        "mlstm_w_in": nrm(ks[11], (nM, D_MODEL, ML_PROJ), D_MODEL ** -0.5),
        "mlstm_b_i": nrm(ks[12], (nM, ML_HEADS), 0.1),
        "mlstm_b_f": jnp.broadcast_to(jnp.linspace(3.0, 6.0, ML_HEADS, dtype=f32), (nM, ML_HEADS))
                     + nrm(ks[13], (nM, ML_HEADS), 0.01),
        "mlstm_conv_w": nrm(ks[14], (nM, CONV_WIDTH, ML_QK_WIDTH), CONV_WIDTH ** -0.5),
        "mlstm_conv_b": nrm(ks[15], (nM, ML_QK_WIDTH), 0.02),
        "mlstm_head_norm": gain(ks[16], (nM, ML_V_WIDTH)),
        "mlstm_w_out": nrm(ks[17], (nM, ML_V_WIDTH, D_MODEL), ML_V_WIDTH ** -0.5),
        "mlp_norm": gain(ks[18], (DEPTH, D_MODEL)),
        "mlp_w1": nrm(ks[19], (DEPTH, D_MODEL, D_FF), D_MODEL ** -0.5),
        "mlp_w2": nrm(ks[20], (DEPTH, D_FF, D_MODEL), D_FF ** -0.5),
        "final_norm": gain(ks[21], (D_MODEL,)),
    }


def reference(x, rel_bias, attn_norm, attn_w_in, attn_lambda_q1, attn_lambda_k1,
              attn_lambda_q2, attn_lambda_k2, attn_subln, attn_w_out,
              mlstm_norm, mlstm_w_in, mlstm_b_i, mlstm_b_f, mlstm_conv_w, mlstm_conv_b,
              mlstm_head_norm, mlstm_w_out, mlp_norm, mlp_w1, mlp_w2, final_norm):
    h = x
    for layer in range(DEPTH):
        j = layer // N_MIXERS
        if layer % N_MIXERS == 0:
            lambda_init = 0.8 - 0.6 * math.exp(-0.3 * layer)
            mix = diff_attention(rmsnorm(h, attn_norm[j]), attn_w_in[j],
                                 attn_lambda_q1[j], attn_lambda_k1[j],
                                 attn_lambda_q2[j], attn_lambda_k2[j],
                                 attn_subln[j], attn_w_out[j], rel_bias, lambda_init)
        else:
            mix = mlstm_mixer(rmsnorm(h, mlstm_norm[j]), mlstm_w_in[j], mlstm_b_i[j],
                              mlstm_b_f[j], mlstm_conv_w[j], mlstm_conv_b[j],
                              mlstm_head_norm[j], mlstm_w_out[j])
        h = h + mix.astype(h.dtype)
        h = h + sqrelu_mlp(rmsnorm(h, mlp_norm[layer]), mlp_w1[layer], mlp_w2[layer]).astype(h.dtype)
    return rmsnorm(h, final_norm)
```

```python
import math
from contextlib import ExitStack

import numpy as np
import ml_dtypes

import concourse.bass as bass
import concourse.mybir as mybir
from concourse.bass_utils import run_bass_kernel_spmd

F32 = mybir.dt.float32
BF16 = mybir.dt.bfloat16
AF = mybir.ActivationFunctionType
ALU = mybir.AluOpType
AX = mybir.AxisListType

NCORES = 8
SEQ = 2048
D = 1024
NT = SEQ // 128
EPS = 1e-6
NEG = -30000.0
LAMBDA_INIT = 0.8 - 0.6 * math.exp(-0.3 * 0)

C_MASK = 0
C_ONES = 128
C_CB = 256
C_LQK = 264
C_SUBW = 520
C_GB = 648
C_CW = 656
C_CBIAS = 688
C_EPS = 696
C_ONE1 = 697
NCF = 704


class Op:
    __slots__ = ("eng", "fn", "deps", "dma_key", "dma_cnt", "inc", "ticket", "idx")

    def __init__(self, eng, fn, dma_key):
        self.eng = eng
        self.fn = fn
        self.deps = []
        self.dma_key = dma_key
        self.dma_cnt = 0
        self.inc = False
        self.ticket = 0
        self.idx = 0


class Sched:
    ENGS = ("pe", "act", "dve", "pool", "sp")

    def __init__(self):
        self.ops = {e: [] for e in self.ENGS}
        self.res = {}
        self.dma_counts = {}
        self.barrier_ops = None
        self.passed = set()

    def _st(self, k):
        st = self.res.get(k)
        if st is None:
            st = self.res[k] = [None, {}]
        return st

    def snapshot(self, keys):
        out = []
        for k in keys:
            st = self._st(k)
            if st[0] is not None:
                out.append(st[0])
            out.extend(st[1].values())
        return out

    def add(self, eng, fn, reads=(), writes=(), war=(), dma_key=None, after=()):
        op = Op(eng, fn, dma_key)
        op.idx = len(self.ops[eng])
        deps = {}

        def dep(o):
            if o is None or o is op:
                return
            key = ("dma", id(o)) if o.dma_key is not None else o.eng
            cur = deps.get(key)
            if cur is None or o.idx > cur.idx:
                deps[key] = o

        for r in reads:
            st = self._st(r)
            dep(st[0])
        for w in writes:
            st = self._st(w)
            dep(st[0])
            for o in st[1].values():
                dep(o)
        for w in war:
            st = self._st(w)
            for o in st[1].values():
                dep(o)
        for o in after:
            dep(o)
        if self.barrier_ops is not None and eng not in self.passed:
            for o in self.barrier_ops:
                dep(o)
            self.passed.add(eng)
        for r in reads:
            st = self._st(r)
            rk = ("dma", id(op)) if dma_key is not None else eng
            st[1][rk] = op
        for w in writes:
            self.res[w] = [op, {}]
        if dma_key is not None:
            c = self.dma_counts.get(dma_key, 0) + 1
            self.dma_counts[dma_key] = c
            op.dma_cnt = c
        op.deps = list(deps.values())
        self.ops[eng].append(op)
        return op

    def barrier(self):
        ops = []
        for e in ("pe", "act", "dve"):
            if self.ops[e]:
                ops.append(self.ops[e][-1])
        self.barrier_ops = ops
        self.passed = set()

    def finalize(self, nc, stack, final_waits):
        for e in self.ENGS:
            for op in self.ops[e]:
                for d in op.deps:
                    if d.dma_key is None:
                        if not (d.eng == "pe" and op.eng == "pe"):
                            d.inc = True
        for e in self.ENGS:
            n = 0
            for op in self.ops[e]:
                if op.inc and op.dma_key is None:
                    n += 1
                    op.ticket = n
        esem = {e: stack.enter_context(nc.semaphore("s_" + e)) for e in self.ENGS}
        dsem = {}
        for k in self.dma_counts:
            dsem[k] = stack.enter_context(nc.semaphore("d_" + "_".join(str(x) for x in k)))
        block = stack.enter_context(nc.Block())
        sched = self

        def replay(e, eng):
            waited = {}
            for op in sched.ops[e]:
                for d in op.deps:
                    if d.dma_key is not None:
                        sem, val, sk = dsem[d.dma_key], 16 * d.dma_cnt, ("d", d.dma_key)
                    else:
                        if d.eng == "pe" and e == "pe":
                            continue
                        sem, val, sk = esem[d.eng], d.ticket, ("e", d.eng)
                    if waited.get(sk, 0) >= val:
                        continue
                    waited[sk] = val
                    eng.wait_ge(sem, val)
                ins = op.fn(eng)
                if op.dma_key is not None:
                    ins.then_inc(dsem[op.dma_key], 16)
                elif op.inc:
                    ins.then_inc(esem[e], 1)
            for k in final_waits.get(e, ()):
                eng.wait_ge(dsem[k], 16 * sched.dma_counts[k])

        @block.tensor
        def _(eng):
            replay("pe", eng)

        @block.scalar
        def _(eng):
            replay("act", eng)

        @block.vector
        def _(eng):
            replay("dve", eng)

        @block.gpsimd
        def _(eng):
            replay("pool", eng)

        @block.sync
        def _(eng):
            replay("sp", eng)


def build(nseq=2, phases=("attn", "mlp0", "mlstm", "mlp1", "final")):
    nc = bass.Bass("TRN2", target_bir_lowering=False)
    x_d = nc.dram_tensor("x", [nseq, SEQ, D], F32, kind="ExternalInput").ap()
    out_d = nc.dram_tensor("out", [nseq, SEQ, D], F32, kind="ExternalOutput").ap()
    a_w_in = nc.dram_tensor("a_w_in", [D, 3072], F32, kind="ExternalInput").ap()
    a_w_out = nc.dram_tensor("a_w_out", [D, D], F32, kind="ExternalInput").ap()
    m_w_in = nc.dram_tensor("m_w_in", [D, 3080], F32, kind="ExternalInput").ap()
    m_w_out = nc.dram_tensor("m_w_out", [D, D], F32, kind="ExternalInput").ap()
    w1_d = nc.dram_tensor("w1", [2, D, 4096], F32, kind="ExternalInput").ap()
    w2_d = nc.dram_tensor("w2", [2, 4096, D], F32, kind="ExternalInput").ap()
    cf_d = nc.dram_tensor("cf32", [128, NCF], F32, kind="ExternalInput").ap()
    id_d = nc.dram_tensor("ident", [128, 128], BF16, kind="ExternalInput").ap()
    tt_d = nc.dram_tensor("tt", [128, 8 * 256], F32, kind="ExternalInput").ap()
    gbc_d = nc.dram_tensor("gbc", [5, 128, D], F32, kind="ExternalInput").ap()
    hw_d = nc.dram_tensor("hwbc", [128, D], F32, kind="ExternalInput").ap()
    dbg_d = nc.dram_tensor("dbg", [128, 8 * SEQ], BF16, kind="ExternalOutput").ap() if "dbg_hnT" in phases else None
    dbgh = [int(p[8:]) for p in phases if p.startswith("dbg_otok")]
    dbo_d = nc.dram_tensor("dbo", [128, NT * 128], BF16, kind="ExternalOutput").ap() if dbgh else None
    dbq_d = nc.dram_tensor("dbq", [128, 3 * SEQ], BF16, kind="ExternalOutput").ap() if dbgh else None

    S = Sched()
    stack = ExitStack()
    sb = lambda name, shape, dt: stack.enter_context(nc.sbuf_tensor(name, shape, dt))
    X = sb("X", [128, NT, D], F32)
    hnT = sb("hnT", [128, 8, SEQ], BF16)
    WS = sb("WS", [128, 4, 4096], BF16)
    ARENA_B = 60 * 1024
    arena = sb("arena", [128, ARENA_B // 2], BF16)
    CF = sb("CF", [128, NCF], F32)
    ident = sb("identsb", [128, 128], BF16)
    xn = sb("xn", [128, 2, D], BF16)
    junk = sb("junk", [128, D], BF16)
    gbc = sb("gbcs", [128, 2, D], F32)
    ssq = sb("ssq", [128, NT], F32)
    rstd = sb("rstd", [128, NT], F32)
    lnt = sb("lnt", [128, NT], F32)
    PS = [stack.enter_context(nc.psum_tensor(f"ps{i}", [128, 512], F32)) for i in range(8)]

    def carve(off, dt, *dims):
        n = 1
        for d_ in dims:
            n *= d_
        nb = n * (4 if dt == F32 else 2)
        assert off % 4 == 0 and off + nb <= ARENA_B, (off, nb)
        ap = arena[:, off // 2:(off + nb) // 2]
        if dt == F32:
            ap = ap.bitcast(F32)
        if len(dims) == 2:
            ap = ap.rearrange("p (a b) -> p a b", a=dims[0])
        elif len(dims) == 3:
            ap = ap.rearrange("p (a b c) -> p a b c", a=dims[0], b=dims[1])
        return ap, off + ((nb + 3) // 4) * 4

    PE = lambda fn, r=(), w=(): S.add("pe", fn, r, w)
    ACT = lambda fn, r=(), w=(): S.add("act", fn, r, w)
    DVE = lambda fn, r=(), w=(): S.add("dve", fn, r, w)

    def SPDMA(out, in_, r=(), w=(), key=None):
        return S.add("sp", lambda e: e.dma_start(out=out, in_=in_), r, w, dma_key=key)

    cfc = lambda c0, n: CF[:, c0:c0 + n]

    plan = []
    for s in range(nseq):
        if "attn" in phases:
            for h in range(8):
                plan.append(("a_qkv", h))
                if h % 2 == 1:
                    plan.append(("a_wo", h // 2))
        if "mlp0" in phases:
            for g in range(8):
                plan.append(("w1", 0, g))
                plan.append(("w2", 0, g))
        if "mlstm" in phases:
            plan.append(("m_g",))
            for hd in range(4):
                plan.append(("m_qk", hd))
                plan.append(("m_v", hd))
                plan.append(("m_o", hd))
                plan.append(("m_wo", hd))
        if "mlp1" in phases:
            for g in range(8):
                plan.append(("w1", 1, g))
                plan.append(("w2", 1, g))

    wstate = {"issued": 0, "next": 0}

    def w_issue(i):
        tag = plan[i]
        slot = i % 4
        dst = WS[:, slot, :]
        kind = tag[0]
        parts = []
        if kind == "a_qkv":
            h = tag[1]
            for t_ in range(3):
                src = a_w_in[:, t_ * 1024 + h * 128:t_ * 1024 + (h + 1) * 128].rearrange("(c p) n -> p c n", p=128)
                o = dst[:, 0:8 * 384].rearrange("p (c t n) -> p c t n", c=8, t=3)[:, :, t_, :]
                parts.append((o, src))
        elif kind == "a_wo":
            hp = tag[1]
            src = a_w_out[hp * 256:(hp + 1) * 256, :].rearrange("(c p) n -> p c n", p=128)
            parts.append((dst[:, 0:2048].rearrange("p (c n) -> p c n", c=2), src))
        elif kind == "w1":
            _, l, g = tag
            src = w1_d[l, :, g * 512:(g + 1) * 512].rearrange("(c p) n -> p c n", p=128)
            parts.append((dst[:, 0:4096].rearrange("p (c n) -> p c n", c=8), src))
        elif kind == "w2":
            _, l, g = tag
            src = w2_d[l, g * 512:(g + 1) * 512, :].rearrange("(c p) n -> p c n", p=128)
            parts.append((dst[:, 0:4096].rearrange("p (c n) -> p c n", c=4), src))
        elif kind == "m_g":
            src = m_w_in[:, 3072:3080].rearrange("(c p) n -> p c n", p=128)
            parts.append((dst[:, 0:64].rearrange("p (c n) -> p c n", c=8), src))
        elif kind == "m_qk":
            hd = tag[1]
            for t_ in range(2):
                src = m_w_in[:, t_ * 512 + hd * 128:t_ * 512 + (hd + 1) * 128].rearrange("(c p) n -> p c n", p=128)
                o = dst[:, 0:2048].rearrange("p (c t n) -> p c t n", c=8, t=2)[:, :, t_, :]
                parts.append((o, src))
        elif kind == "m_v":
            hd = tag[1]
            src = m_w_in[:, 1024 + hd * 256:1024 + (hd + 1) * 256].rearrange("(c p) n -> p c n", p=128)
            parts.append((dst[:, 0:2048].rearrange("p (c n) -> p c n", c=8), src))
        elif kind == "m_o":
            hd = tag[1]
            src = m_w_in[:, 2048 + hd * 256:2048 + (hd + 1) * 256].rearrange("(c p) n -> p c n", p=128)
            parts.append((dst[:, 0:2048].rearrange("p (c n) -> p c n", c=8), src))
        elif kind == "m_wo":
            hd = tag[1]
            src = m_w_out[hd * 256:(hd + 1) * 256, :].rearrange("(c p) n -> p c n", p=128)
            parts.append((dst[:, 0:2048].rearrange("p (c n) -> p c n", c=2), src))
        else:
            raise ValueError(kind)
        snap = S.snapshot([("ws", slot, p_) for p_ in range(3)])
        for p_, (o, src) in enumerate(parts):
            S.add("pool", lambda e, o=o, src=src: e.dma_start(out=o, in_=src), (), [("ws", slot, p_)],
                  dma_key=("ws", slot, p_), after=snap)

    def w_next(tag):
        i = wstate["next"]
        assert plan[i] == tag, (plan[i], tag)
        while wstate["issued"] < min(len(plan), i + 3):
            w_issue(wstate["issued"])
            wstate["issued"] += 1
        wstate["next"] = i + 1
        slot = i % 4
        return WS[:, slot, :], ("ws", slot, 0)

    SPDMA(CF[:], cf_d, w=["CF"], key=("c", 0))
    SPDMA(ident[:], id_d, w=["ident"], key=("c", 1))
    gbstate = {"n": 0}

    def load_gbc(idx):
        b = gbstate["n"] % 2
        gbstate["n"] += 1
        SPDMA(gbc[:, b, :], gbc_d[idx], w=[("gbc", b)], key=("g", b))
        return b

    def emit_stats():
        for t in range(NT):
            ACT(lambda e, t=t: e.activation(out=junk[:], in_=X[:, t, :], func=AF.Square,
                                            accum_out=ssq[:, t:t + 1]),
                r=[("X", t)], w=[("ssq", t)])
        ACT(lambda e: e.activation(out=lnt[:], in_=ssq[:], func=AF.Ln, scale=1.0 / D, bias=cfc(C_EPS, 1)),
            r=[("ssq", t) for t in range(NT)] + ["CF"], w=["lnt"])
        ACT(lambda e: e.activation(out=rstd[:], in_=lnt[:], func=AF.Exp, scale=-0.5),
            r=["lnt"], w=["rstd"])

    def emit_norm(gidx):
        gb = load_gbc(gidx)
        emit_stats()
        for t in range(NT):
            b = t % 2
            DVE(lambda e, t=t, b=b: e.scalar_tensor_tensor(out=xn[:, b, :], in0=X[:, t, :],
                                                           scalar=rstd[:, t:t + 1], in1=gbc[:, gb, :],
                                                           op0=ALU.mult, op1=ALU.mult),
                r=[("X", t), "rstd", ("gbc", gb)], w=[("xn", b)])
            bank = 6 + (t % 2)
            pv = PS[bank][:].bitcast(BF16)
            for c in range(8):
                PE(lambda e, c=c, b=b, pv=pv: e.transpose(out=pv[:, c * 128:(c + 1) * 128],
                                                          in_=xn[:, b, c * 128:(c + 1) * 128],
                                                          identity=ident[:]),
                   r=[("xn", b), "ident"], w=[("ps", bank)])
            o = hnT[:, :, t * 128:(t + 1) * 128]
            i_ = pv.rearrange("p (c n) -> p c n", c=8)
            if t % 2 == 0:
                ACT(lambda e, o=o, i_=i_: e.activation(out=o, in_=i_, func=AF.Copy),
                    r=[("ps", bank)], w=[("hnT", t)])
            else:
                DVE(lambda e, o=o, i_=i_: e.tensor_copy(out=o, in_=i_),
                    r=[("ps", bank)], w=[("hnT", t)])

    HN_ALL = [("hnT", t) for t in range(NT)]

    def emit_mlp(layer):
        S.barrier()
        off = 0
        h1T, off = carve(off, BF16, 2, 4, SEQ)
        rtmp, off = carve(off, F32, 2, 512)
        cnt = {"a": 0, "b": 0, "r": 0}

        def g1(g, tq, w1v, wk1):
            hb = g % 2
            for c in range(4):
                bank = cnt["a"] % 4
                cnt["a"] += 1
                for d_ in range(8):
                    PE(lambda e, bank=bank, c=c, d_=d_: e.matmul(
                        PS[bank][:], lhsT=w1v[:, d_, c * 128:(c + 1) * 128],
                        rhs=hnT[:, d_, tq * 512:(tq + 1) * 512], start=(d_ == 0), stop=(d_ == 7)),
                       r=[wk1] + [("hnT", t) for t in range(tq * 4, tq * 4 + 4)], w=[("ps", bank)])
                rb = cnt["r"] % 2
                cnt["r"] += 1
                ACT(lambda e, bank=bank, rb=rb: e.activation(out=rtmp[:, rb, :], in_=PS[bank][:], func=AF.Relu),
                    r=[("ps", bank)], w=[("rtmp", rb)])
                ACT(lambda e, rb=rb, c=c, hb=hb: e.activation(out=h1T[:, hb, c, tq * 512:(tq + 1) * 512],
                                                              in_=rtmp[:, rb, :], func=AF.Square),
                    r=[("rtmp", rb)], w=[("h1T", hb, c, tq)])

        def g2(g, tq, w2v, wk2):
            hb = g % 2
            for t in range(tq * 4, tq * 4 + 4):
                for n in range(2):
                    bank = 4 + cnt["b"] % 4
                    cnt["b"] += 1
                    for c in range(4):
                        PE(lambda e, bank=bank, c=c, t=t, n=n: e.matmul(
                            PS[bank][:], lhsT=h1T[:, hb, c, t * 128:(t + 1) * 128],
                            rhs=w2v[:, c, n * 512:(n + 1) * 512], start=(c == 0), stop=(c == 3)),
                           r=[wk2, ("h1T", hb, c, tq)], w=[("ps", bank)])
                    DVE(lambda e, bank=bank, t=t, n=n: e.tensor_tensor(
                        out=X[:, t, n * 512:(n + 1) * 512], in0=PS[bank][:],
                        in1=X[:, t, n * 512:(n + 1) * 512], op=ALU.add),
                        r=[("ps", bank), ("X", t)], w=[("X", t)])

        for g in range(8):
            w1s, wk1 = w_next(("w1", layer, g))
            w2s, wk2 = w_next(("w2", layer, g))
            w1v = w1s[:, 0:4096].rearrange("p (c n) -> p c n", c=8)
            w2v = w2s[:, 0:4096].rearrange("p (c n) -> p c n", c=4)
            g1(g, 0, w1v, wk1)
            for tq in range(1, 4):
                g1(g, tq, w1v, wk1)
                g2(g, tq - 1, w2v, wk2)
            g2(g, 3, w2v, wk2)

    def emit_attn():
        S.barrier()
        off = 0
        TT, off = carve(off, F32, 8, 256)
        qT, off = carve(off, BF16, SEQ)
        kT, off = carve(off, BF16, SEQ)
        V, off = carve(off, BF16, NT, 130)
        PT, off = carve(off, BF16, 4, 512)
        ntmp, off = carve(off, F32, 2, 256)
        osb, off = carve(off, F32, 4, 128)
        t1sb, off = carve(off, F32, 4, 128)
        otok, off = carve(off, BF16, NT, 128)
        oTp, off = carve(off, BF16, 2, SEQ)
        subw, off = carve(off, F32, 128)
        sm, off = carve(off, F32, 64)
        lqk, off = carve(off, F32, 2, 64)
        SPDMA(TT.rearrange("p a b -> p (a b)"), tt_d, w=["TT"], key=("c", 2))
        for k in range(2):
            DVE(lambda e, k=k: e.tensor_tensor(out=lqk[:, k, :], in0=cfc(C_LQK + 128 * k, 64),
                                               in1=cfc(C_LQK + 128 * k + 64, 64), op=ALU.mult),
                r=["CF"], w=[("lqk", k)])
            DVE(lambda e, k=k: e.reduce_sum(out=sm[:, k:k + 1], in_=lqk[:, k, :], axis=AX.X),
                r=[("lqk", k)], w=[("sm", k)])
        ACT(lambda e: e.activation(out=sm[:, 2:4], in_=sm[:, 0:2], func=AF.Exp),
            r=[("sm", 0), ("sm", 1)], w=[("sm", 2)])
        DVE(lambda e: e.scalar_tensor_tensor(out=sm[:, 4:5], in0=sm[:, 2:3], scalar=float(LAMBDA_INIT),
                                             in1=sm[:, 3:4], op0=ALU.add, op1=ALU.subtract),
            r=[("sm", 2)], w=["lam"])
        DVE(lambda e: e.tensor_scalar(out=subw, in0=cfc(C_SUBW, 128), scalar1=float(1.0 - LAMBDA_INIT),
                                      scalar2=None, op0=ALU.mult),
            r=["CF"], w=["subw"])
        DVE(lambda e: e.memset(V[:, :, 128:130], 1.0), w=["Vones"])
        lam = sm[:, 4:5]
        cnt = {"st": 0, "pt": 0, "nt": 0, "pj": 0}
        STB = (0, 1, 2)
        ACCA = (3, 4)
        ACCB = 5

        def acc(m, qs):
            if qs < 3:
                return PS[ACCA[m]][:, qs * 129:(qs + 1) * 129], ("ps", ACCA[m])
            return PS[ACCB][:, m * 129:(m + 1) * 129], ("ps", ACCB)

        def attn_head(h):
            wsl, wk = w_next(("a_qkv", h))
            Wv_ = wsl[:, 0:8 * 384].rearrange("p (c t n) -> p c t n", c=8, t=3)
            for which, dst, sc in ((0, qT, 0.125), (1, kT, 1.0)):
                for tq in range(4):
                    bank = 6 + cnt["pj"] % 2
                    cnt["pj"] += 1
                    for d_ in range(8):
                        PE(lambda e, bank=bank, d_=d_, which=which, tq=tq: e.matmul(
                            PS[bank][:], lhsT=Wv_[:, d_, which, :], rhs=hnT[:, d_, tq * 512:(tq + 1) * 512],
                            start=(d_ == 0), stop=(d_ == 7)),
                           r=[(wk[0], wk[1], which)] + [("hnT", t) for t in range(tq * 4, tq * 4 + 4)], w=[("ps", bank)])
                    ACT(lambda e, bank=bank, dst=dst, sc=sc, tq=tq: e.activation(
                        out=dst[:, tq * 512:(tq + 1) * 512], in_=PS[bank][:], func=AF.Copy, scale=sc),
                        r=[("ps", bank)], w=[("qk", which)])
            for t0 in range(0, NT, 4):
                bank = 6 + cnt["pj"] % 2
                cnt["pj"] += 1
                for tt_ in range(4):
                    t = t0 + tt_
                    for d_ in range(8):
                        PE(lambda e, bank=bank, d_=d_, t=t, tt_=tt_: e.matmul(
                            PS[bank][:, tt_ * 128:(tt_ + 1) * 128], lhsT=hnT[:, d_, t * 128:(t + 1) * 128],
                            rhs=Wv_[:, d_, 2, :], start=(d_ == 0), stop=(d_ == 7)),
                           r=[(wk[0], wk[1], 2), ("hnT", t)], w=[("ps", bank)])
                DVE(lambda e, bank=bank, t0=t0: e.tensor_copy(
                    out=V[:, t0:t0 + 4, 0:128], in_=PS[bank][:].rearrange("p (a b) -> p a b", a=4)),
                    r=[("ps", bank)], w=["V"])
            for j in range(4):
                nk = 4 * j + 4
                for i in range(nk):
                    qb0 = max(4 * j, i)
                    q0 = qb0 * 128
                    N = (4 * j + 4) * 128 - q0
                    d0 = qb0 - i
                    for m in range(2):
                        bank = STB[cnt["st"] % 3]
                        cnt["st"] += 1
                        PE(lambda e, bank=bank, m=m, i=i, q0=q0, N=N: e.matmul(
                            PS[bank][:, 0:N], lhsT=kT[m * 64:(m + 1) * 64, i * 128:(i + 1) * 128],
                            rhs=qT[m * 64:(m + 1) * 64, q0:q0 + N], start=True, stop=True),
                           r=[("qk", 0), ("qk", 1)], w=[("ps", bank)])
                        pb = cnt["pt"] % 4
                        cnt["pt"] += 1
                        if d0 >= 2:
                            n1 = 0
                        else:
                            tc0 = 0 if d0 == 0 else 128
                            n1 = min(256 - tc0, N)
                            nb = cnt["nt"] % 2
                            cnt["nt"] += 1
                            DVE(lambda e, bank=bank, n1=n1, tc0=tc0, nb=nb, h=h: e.tensor_tensor(
                                out=ntmp[:, nb, 0:n1], in0=PS[bank][:, 0:n1], in1=TT[:, h, tc0:tc0 + n1],
                                op=ALU.add),
                                r=[("ps", bank), "TT"], w=[("ntmp", nb)])
                            ACT(lambda e, pb=pb, n1=n1, nb=nb: e.activation(
                                out=PT[:, pb, 0:n1], in_=ntmp[:, nb, 0:n1], func=AF.Exp),
                                r=[("ntmp", nb)], w=[("PT", pb)])
                        if N > n1:
                            ACT(lambda e, pb=pb, n1=n1, N=N, bank=bank, h=h: e.activation(
                                out=PT[:, pb, n1:N], in_=PS[bank][:, n1:N], func=AF.Exp,
                                bias=cfc(C_CB + h, 1)),
                                r=[("ps", bank), "CF"], w=[("PT", pb)])
                        for qs in range(qb0 - 4 * j, 4):
                            qb = 4 * j + qs
                            col = qb * 128 - q0
                            a_ap, a_key = acc(m, qs)
                            first = (i == 0) and ((qs == 0) or (qs == 3 and m == 0))
                            PE(lambda e, a_ap=a_ap, pb=pb, col=col, i=i, first=first, qb=qb: e.matmul(
                                a_ap, lhsT=PT[:, pb, col:col + 128], rhs=V[:, i, 0:129],
                                start=first, stop=(i == qb), skip_group_check=True),
                               r=[("PT", pb), "V", "Vones"], w=[a_key])
                for qs in range(4):
                    t = 4 * j + qs
                    a0, k0 = acc(0, qs)
                    a1, k1 = acc(1, qs)
                    c0 = 8 + qs * 4
                    DVE(lambda e, a0=a0, c0=c0: e.reciprocal(out=sm[:, c0:c0 + 1], in_=a0[:, 128:129]),
                        r=[k0], w=[("smq", qs, 0)])
                    DVE(lambda e, a1=a1, c0=c0: e.reciprocal(out=sm[:, c0 + 1:c0 + 2], in_=a1[:, 128:129]),
                        r=[k1], w=[("smq", qs, 1)])
                    DVE(lambda e, c0=c0: e.tensor_tensor(out=sm[:, c0 + 2:c0 + 3], in0=sm[:, c0 + 1:c0 + 2],
                                                         in1=lam, op=ALU.mult),
                        r=[("smq", qs, 1), "lam"], w=[("smq", qs, 2)])
                    ACT(lambda e, a1=a1, c0=c0, qs=qs: e.activation(out=t1sb[:, qs, :], in_=a1[:, 0:128],
                                                                    func=AF.Copy, scale=sm[:, c0 + 2:c0 + 3]),
                        r=[k1, ("smq", qs, 2)], w=[("t1", qs)])
                    DVE(lambda e, a0=a0, c0=c0, qs=qs: e.scalar_tensor_tensor(
                        out=osb[:, qs, :], in0=a0[:, 0:128], scalar=sm[:, c0:c0 + 1], in1=t1sb[:, qs, :],
                        op0=ALU.mult, op1=ALU.subtract),
                        r=[k0, ("smq", qs, 0), ("t1", qs)], w=[("osb", qs)])
                    ACT(lambda e, qs=qs: e.activation(out=junk[:, 0:128], in_=osb[:, qs, :], func=AF.Square,
                                                      accum_out=sm[:, 32 + qs:33 + qs]),
                        r=[("osb", qs)], w=[("ss2", qs)])
                ACT(lambda e: e.activation(out=sm[:, 36:40], in_=sm[:, 32:36], func=AF.Ln, scale=1.0 / 128,
                                           bias=cfc(C_EPS, 1)),
                    r=[("ss2", q_) for q_ in range(4)] + ["CF"], w=["ln2"])
                ACT(lambda e: e.activation(out=sm[:, 40:44], in_=sm[:, 36:40], func=AF.Exp, scale=-0.5),
                    r=["ln2"], w=["rs2"])
                for qs in range(4):
                    t = 4 * j + qs
                    DVE(lambda e, qs=qs, t=t: e.scalar_tensor_tensor(
                        out=otok[:, t, :], in0=osb[:, qs, :], scalar=sm[:, 40 + qs:41 + qs], in1=subw,
                        op0=ALU.mult, op1=ALU.mult),
                        r=[("osb", qs), "rs2", "subw"], w=[("otok", t)])
            if dbgh and h == dbgh[0]:
                SPDMA(dbo_d, otok.rearrange("p a b -> p (a b)"), r=[("otok", t) for t in range(NT)], key=("c", 6))
                SPDMA(dbq_d[:, 0:SEQ], qT, r=[("qk", 0)], key=("c", 7))
                SPDMA(dbq_d[:, SEQ:2 * SEQ], kT, r=[("qk", 1)], key=("c", 8))
                SPDMA(dbq_d[:, 2 * SEQ:2 * SEQ + NT * 128].rearrange("p (a b) -> p a b", a=NT), V[:, :, 0:128], r=["V"], key=("c", 9))
            hh = h % 2
            for t0 in range(0, NT, 4):
                bank = 6 + cnt["pj"] % 2
                cnt["pj"] += 1
                pv = PS[bank][:].bitcast(BF16)
                for tt_ in range(4):
                    t = t0 + tt_
                    PE(lambda e, pv=pv, t=t, tt_=tt_: e.transpose(out=pv[:, tt_ * 128:(tt_ + 1) * 128],
                                                                  in_=otok[:, t, :], identity=ident[:]),
                       r=[("otok", t), "ident"], w=[("ps", bank)])
                ACT(lambda e, pv=pv, t0=t0, hh=hh: e.activation(out=oTp[:, hh, t0 * 128:(t0 + 4) * 128],
                                                                in_=pv[:, 0:512], func=AF.Copy),
                    r=[("ps", bank)], w=[("oTp", hh, t0)])
            if hh == 1:
                wsl2, wk2 = w_next(("a_wo", h // 2))
                wo = wsl2[:, 0:2048].rearrange("p (c n) -> p c n", c=2)
                for t in range(NT):
                    for n in range(2):
                        bank = 6 + cnt["pj"] % 2
                        cnt["pj"] += 1
                        for c in range(2):
                            PE(lambda e, bank=bank, c=c, t=t, n=n: e.matmul(
                                PS[bank][:], lhsT=oTp[:, c, t * 128:(t + 1) * 128],
                                rhs=wo[:, c, n * 512:(n + 1) * 512], start=(c == 0), stop=(c == 1)),
                               r=[wk2, ("oTp", c, (t // 4) * 4)], w=[("ps", bank)])
                        DVE(lambda e, bank=bank, t=t, n=n: e.tensor_tensor(
                            out=X[:, t, n * 512:(n + 1) * 512], in0=PS[bank][:],
                            in1=X[:, t, n * 512:(n + 1) * 512], op=ALU.add),
                            r=[("ps", bank), ("X", t)], w=[("X", t)])

        for h in range(8):
            attn_head(h)

    def emit_mlstm():
        S.barrier()
        off = 0
        A1_off = off
        raw, off = carve(off, F32, SEQ + 16)
        cacc, off = carve(off, F32, SEQ)
        numsb, _ = carve(A1_off, F32, NT, 257)
        assert NT * 257 * 4 <= off - A1_off
        qT, off = carve(off, BF16, SEQ)
        kT, off = carve(off, BF16, SEQ)
        vaug, off = carve(off, BF16, NT, 258)
        og, off = carve(off, BF16, NT, 256)
        htT, off = carve(off, BF16, 2, SEQ)
        hwbc, off = carve(off, F32, 256)
        G, off = carve(off, F32, NT, 8)
        nl, off = carve(off, F32, NT, 4)
        fq, off = carve(off, F32, NT, 4)
        fk, off = carve(off, F32, NT, 4)
        gg, off = carve(off, F32, NT, 4)
        eb, off = carve(off, F32, NT, 4)
        tmpa, off = carve(off, F32, NT, 4)
        tmpb, off = carve(off, F32, NT, 4)
        CTf, off = carve(off, F32, 258)
        CTb, off = carve(off, BF16, 2, 258)
        kpp, off = carve(off, BF16, 2, 128)
        scsb, off = carve(off, BF16, 2, 128)
        dd, off = carve(off, F32, NT)
        r2, off = carve(off, F32, NT)
        ss3, off = carve(off, F32, NT)
        ln3, off = carve(off, F32, NT)
        r3, off = carve(off, F32, NT)
        maskT = cfc(C_MASK, 128)
        ones = cfc(C_ONES, 128)
        cnt = {"pj": 0, "k": 0}
        wsl, wk = w_next(("m_g",))
        wg = wsl[:, 0:64].rearrange("p (c n) -> p c n", c=8)
        for t in range(NT):
            for d_ in range(8):
                PE(lambda e, t=t, d_=d_: e.matmul(PS[0][:, t * 8:(t + 1) * 8], lhsT=hnT[:, d_, t * 128:(t + 1) * 128],
                                                  rhs=wg[:, d_, :], start=(d_ == 0), stop=(d_ == 7)),
                   r=[wk, ("hnT", t)], w=[("ps", 0)])
        Gv = PS[0][:, 0:128].rearrange("p (t n) -> p t n", t=NT)
        for t in range(NT):
            pass
        DVE(lambda e: e.tensor_tensor(out=G, in0=Gv, in1=cfc(C_GB, 8).unsqueeze(1).to_broadcast([128, NT, 8]),
                                      op=ALU.add),
            r=[("ps", 0), "CF"], w=["G"])
        ACT(lambda e: e.activation(out=tmpa, in_=G[:, :, 4:8], func=AF.Exp, scale=-1.0), r=["G"], w=["tmpa"])
        ACT(lambda e: e.activation(out=nl, in_=tmpa, func=AF.Ln, bias=cfc(C_ONE1, 1)), r=["tmpa", "CF"], w=["nl"])
        nlf = nl.rearrange("p t h -> p (t h)")
        PE(lambda e: e.matmul(PS[1][:, 0:64], lhsT=maskT, rhs=nlf, start=True, stop=True),
           r=["nl", "CF"], w=[("ps", 1)])
        PE(lambda e: e.matmul(PS[1][:, 64:128], lhsT=ones, rhs=nlf, start=True, stop=True),
           r=["nl", "CF"], w=[("ps", 1)])
        cum = PS[1][:, 0:64].rearrange("p (t h) -> p t h", t=NT)
        tot = PS[1][:, 64:128].rearrange("p (t h) -> p t h", t=NT)
        ACT(lambda e: e.activation(out=fq, in_=cum, func=AF.Exp, scale=-1.0), r=[("ps", 1)], w=["fq0"])
        DVE(lambda e: e.tensor_scalar(out=fq, in0=fq, scalar1=float(128 ** -0.5), scalar2=None, op0=ALU.mult),
            r=["fq0"], w=["fq"])
        DVE(lambda e: e.tensor_tensor(out=tmpa, in0=cum, in1=G[:, :, 0:4], op=ALU.add),
            r=[("ps", 1), "G", "nl"], w=["tmpa2"])
        ACT(lambda e: e.activation(out=fk, in_=tmpa, func=AF.Exp), r=["tmpa2"], w=["fk"])
        DVE(lambda e: e.tensor_tensor(out=tmpb, in0=tmpa, in1=tot, op=ALU.subtract),
            r=["tmpa2", ("ps", 1)], w=["tmpb"])
        ACT(lambda e: e.activation(out=gg, in_=tmpb, func=AF.Exp), r=["tmpb"], w=["gg"])
        ACT(lambda e: e.activation(out=eb, in_=tot, func=AF.Exp, scale=-1.0), r=[("ps", 1)], w=["eb"])
        DVE(lambda e: e.memset(vaug[:, :, 256:258], 1.0), w=["vones"])

        def m_head(hd):
            wqk_s, wkq = w_next(("m_qk", hd))
            wqk = wqk_s[:, 0:2048].rearrange("p (c t n) -> p c t n", c=8, t=2)
            S.add("sp", lambda e, hd=hd: e.dma_start(out=hwbc, in_=hw_d[:, hd * 256:(hd + 1) * 256]),
                  (), ["hwbc"], dma_key=("c", 3))
            DVE(lambda e: e.memset(raw[:, 0:3], 0.0), r=[], w=["A1", "rawz"])
            for which, dst in ((0, qT), (1, kT)):
                ch = which * 4 + hd
                for tq in range(4):
                    bank = 6 + cnt["pj"] % 2
                    cnt["pj"] += 1
                    for d_ in range(8):
                        PE(lambda e, bank=bank, d_=d_, which=which, tq=tq: e.matmul(
                            PS[bank][:], lhsT=wqk[:, d_, which, :], rhs=hnT[:, d_, tq * 512:(tq + 1) * 512],
                            start=(d_ == 0), stop=(d_ == 7)),
                           r=[(wkq[0], wkq[1], which)] + [("hnT", t) for t in range(tq * 4, tq * 4 + 4)], w=[("ps", bank)])
                    ACT(lambda e, bank=bank, tq=tq: e.activation(
                        out=raw[:, 3 + tq * 512:3 + (tq + 1) * 512], in_=PS[bank][:], func=AF.Copy),
                        r=[("ps", bank), "rawz"], w=["A1"])
                cw = lambda k_, ch=ch: cfc(C_CW + ch * 4 + k_, 1)
                DVE(lambda e, cw=cw: e.tensor_scalar(out=cacc, in0=raw[:, 3:3 + SEQ], scalar1=cw(3), scalar2=None,
                                                     op0=ALU.mult),
                    r=["A1", "rawz", "CF"], w=["cacc"])
                for k_ in (2, 1, 0):
                    DVE(lambda e, cw=cw, k_=k_: e.scalar_tensor_tensor(
                        out=cacc, in0=raw[:, k_:k_ + SEQ], scalar=cw(k_), in1=cacc, op0=ALU.mult, op1=ALU.add),
                        r=["A1", "rawz", "cacc"], w=["cacc"])
                ACT(lambda e, dst=dst, ch=ch: e.activation(out=dst, in_=cacc, func=AF.Silu,
                                                           bias=cfc(C_CBIAS + ch, 1)),
                    r=["cacc", "CF"], w=[("mqk", which)])
            wv_s, wkv = w_next(("m_v", hd))
            wv = wv_s[:, 0:2048].rearrange("p (c n) -> p c n", c=8)
            for t0 in range(0, NT, 2):
                bank = 6 + cnt["pj"] % 2
                cnt["pj"] += 1
                for tt_ in range(2):
                    t = t0 + tt_
                    for d_ in range(8):
                        PE(lambda e, bank=bank, d_=d_, t=t, tt_=tt_: e.matmul(
                            PS[bank][:, tt_ * 256:(tt_ + 1) * 256], lhsT=hnT[:, d_, t * 128:(t + 1) * 128],
                            rhs=wv[:, d_, :], start=(d_ == 0), stop=(d_ == 7)),
                           r=[wkv, ("hnT", t)], w=[("ps", bank)])
                DVE(lambda e, bank=bank, t0=t0: e.tensor_copy(
                    out=vaug[:, t0:t0 + 2, 0:256], in_=PS[bank][:].rearrange("p (a b) -> p a b", a=2)),
                    r=[("ps", bank)], w=["vaug"])
            wo_s, wko = w_next(("m_o", hd))
            wo_ = wo_s[:, 0:2048].rearrange("p (c n) -> p c n", c=8)
            for t0 in range(0, NT, 2):
                bank = 6 + cnt["pj"] % 2
                cnt["pj"] += 1
                for tt_ in range(2):
                    t = t0 + tt_
                    for d_ in range(8):
                        PE(lambda e, bank=bank, d_=d_, t=t, tt_=tt_: e.matmul(
                            PS[bank][:, tt_ * 256:(tt_ + 1) * 256], lhsT=hnT[:, d_, t * 128:(t + 1) * 128],
                            rhs=wo_[:, d_, :], start=(d_ == 0), stop=(d_ == 7)),
                           r=[wko, ("hnT", t)], w=[("ps", bank)])
                ACT(lambda e, bank=bank, t0=t0: e.activation(
                    out=og[:, t0:t0 + 2, :], in_=PS[bank][:].rearrange("p (a b) -> p a b", a=2), func=AF.Sigmoid),
                    r=[("ps", bank)], w=[("og", t0)])
            for c in range(NT):
                cs = slice(c * 128, (c + 1) * 128)
                sb_ = cnt["k"] % 2
                cnt["k"] += 1
                PE(lambda e, cs=cs: e.matmul(PS[2][:, 0:128], lhsT=kT[:, cs], rhs=qT[:, cs], start=True, stop=True),
                   r=[("mqk", 0), ("mqk", 1)], w=[("ps", 2)])
                DVE(lambda e, c=c, sb_=sb_, hd=hd: e.scalar_tensor_tensor(
                    out=scsb[:, sb_, :], in0=PS[2][:, 0:128], scalar=fk[:, c, hd:hd + 1], in1=maskT,
                    op0=ALU.mult, op1=ALU.mult),
                    r=[("ps", 2), "fk", "CF"], w=[("scsb", sb_)])
                cb_ = c % 2
                PE(lambda e, c=c, sb_=sb_: e.matmul(PS[3][:, 0:257], lhsT=scsb[:, sb_, :], rhs=vaug[:, c, 0:257],
                                                    start=True, stop=(c == 0)),
                   r=[("scsb", sb_), "vaug", "vones"], w=[("ps", 3)])
                if c > 0:
                    PE(lambda e, cs=cs, cb_=cb_: e.matmul(PS[3][:, 0:257], lhsT=qT[:, cs], rhs=CTb[:, cb_, 0:257],
                                                          start=False, stop=True),
                       r=[("mqk", 0), ("CTb", cb_)], w=[("ps", 3)])
                ACT(lambda e, c=c: e.activation(out=numsb[:, c, :], in_=PS[3][:, 0:257], func=AF.Copy),
                    r=[("ps", 3)], w=["A1"])
                if c < NT - 1:
                    pvb = PS[4][:].bitcast(BF16)
                    PE(lambda e, cs=cs, pvb=pvb: e.transpose(out=pvb[:, 0:128], in_=kT[:, cs], identity=ident[:]),
                       r=[("mqk", 1), "ident"], w=[("ps", 4)])
                    ACT(lambda e, c=c, sb_=sb_, pvb=pvb, hd=hd: e.activation(
                        out=kpp[:, sb_, :], in_=pvb[:, 0:128], func=AF.Copy, scale=gg[:, c, hd:hd + 1]),
                        r=[("ps", 4), "gg"], w=[("kpp", sb_)])
                    PE(lambda e, c=c, sb_=sb_: e.matmul(PS[5][:, 0:257], lhsT=kpp[:, sb_, :], rhs=vaug[:, c, 0:257],
                                                        start=True, stop=True),
                       r=[("kpp", sb_), "vaug", "vones"], w=[("ps", 5)])
                    nb_ = (c + 1) % 2
                    if c == 0:
                        DVE(lambda e: e.tensor_copy(out=CTf[:, 0:257], in_=PS[5][:, 0:257]),
                            r=[("ps", 5)], w=["CTf"])
                    else:
                        DVE(lambda e, c=c, hd=hd: e.scalar_tensor_tensor(
                            out=CTf[:, 0:257], in0=CTf[:, 0:257], scalar=eb[:, c, hd:hd + 1], in1=PS[5][:, 0:257],
                            op0=ALU.mult, op1=ALU.add),
                            r=[("ps", 5), "CTf", "eb"], w=["CTf"])
                    DVE(lambda e, nb_=nb_: e.tensor_copy(out=CTb[:, nb_, 0:257], in_=CTf[:, 0:257]),
                        r=["CTf"], w=[("CTb", nb_)])
            den = numsb[:, :, 256]
            DVE(lambda e, hd=hd: e.tensor_tensor(out=dd, in0=den, in1=fq[:, :, hd], op=ALU.mult),
                r=["A1", "fq"], w=["dd"])
            DVE(lambda e: e.scalar_tensor_tensor(out=dd, in0=dd, scalar=-1.0, in1=dd, op0=ALU.mult, op1=ALU.max),
                r=["dd"], w=["dd"])
            DVE(lambda e: e.tensor_scalar(out=dd, in0=dd, scalar1=1.0, scalar2=None, op0=ALU.max),
                r=["dd"], w=["dd"])
            DVE(lambda e: e.reciprocal(out=dd, in_=dd), r=["dd"], w=["dd"])
            DVE(lambda e, hd=hd: e.tensor_tensor(out=r2, in0=dd, in1=fq[:, :, hd], op=ALU.mult),
                r=["dd", "fq"], w=["r2"])
            for c in range(NT):
                ACT(lambda e, c=c: e.activation(out=junk[:, 0:256], in_=numsb[:, c, 0:256], func=AF.Square,
                                                scale=r2[:, c:c + 1], accum_out=ss3[:, c:c + 1]),
                    r=["A1", "r2"], w=[("ss3", c)])
            ACT(lambda e: e.activation(out=ln3, in_=ss3, func=AF.Ln, scale=1.0 / 256, bias=cfc(C_EPS, 1)),
                r=[("ss3", c) for c in range(NT)] + ["CF"], w=["ln3"])
            ACT(lambda e: e.activation(out=r3, in_=ln3, func=AF.Exp, scale=-0.5), r=["ln3"], w=["r3a"])
            DVE(lambda e: e.tensor_tensor(out=r3, in0=r3, in1=r2, op=ALU.mult), r=["r3a", "r2"], w=["r3"])
            for c in range(NT):
                DVE(lambda e, c=c: e.scalar_tensor_tensor(
                    out=numsb[:, c, 0:256], in0=numsb[:, c, 0:256], scalar=r3[:, c:c + 1], in1=hwbc,
                    op0=ALU.mult, op1=ALU.mult),
                    r=["A1", "r3", "hwbc"], w=[("hn", c)])
                DVE(lambda e, c=c: e.tensor_tensor(out=og[:, c, :], in0=numsb[:, c, 0:256], in1=og[:, c, :],
                                                   op=ALU.mult),
                    r=[("hn", c), ("og", (c // 2) * 2)], w=[("gated", c)])
            for cc in range(2):
                for t0 in range(0, NT, 4):
                    bank = 6 + cnt["pj"] % 2
                    cnt["pj"] += 1
                    pv = PS[bank][:].bitcast(BF16)
                    for tt_ in range(4):
                        t = t0 + tt_
                        PE(lambda e, pv=pv, t=t, tt_=tt_, cc=cc: e.transpose(
                            out=pv[:, tt_ * 128:(tt_ + 1) * 128], in_=og[:, t, cc * 128:(cc + 1) * 128],
                            identity=ident[:]),
                           r=[("gated", t), "ident"], w=[("ps", bank)])
                    ACT(lambda e, pv=pv, t0=t0, cc=cc: e.activation(out=htT[:, cc, t0 * 128:(t0 + 4) * 128],
                                                                    in_=pv[:, 0:512], func=AF.Copy),
                        r=[("ps", bank)], w=[("htT", cc, t0)])
            wout_s, wkout = w_next(("m_wo", hd))
            wout = wout_s[:, 0:2048].rearrange("p (c n) -> p c n", c=2)
            for t in range(NT):
                for n in range(2):
                    bank = 6 + cnt["pj"] % 2
                    cnt["pj"] += 1
                    for cc in range(2):
                        PE(lambda e, bank=bank, cc=cc, t=t, n=n: e.matmul(
                            PS[bank][:], lhsT=htT[:, cc, t * 128:(t + 1) * 128],
                            rhs=wout[:, cc, n * 512:(n + 1) * 512], start=(cc == 0), stop=(cc == 1)),
                           r=[wkout, ("htT", cc, (t // 4) * 4)], w=[("ps", bank)])
                    DVE(lambda e, bank=bank, t=t, n=n: e.tensor_tensor(
                        out=X[:, t, n * 512:(n + 1) * 512], in0=PS[bank][:],
                        in1=X[:, t, n * 512:(n + 1) * 512], op=ALU.add),
                        r=[("ps", bank), ("X", t)], w=[("X", t)])

        for hd in range(4):
            m_head(hd)

    def emit_final(s, do_norm):
        if do_norm:
            gb = load_gbc(4)
            emit_stats()
            for t in range(NT):
                DVE(lambda e, t=t: e.scalar_tensor_tensor(out=X[:, t, :], in0=X[:, t, :], scalar=rstd[:, t:t + 1],
                                                          in1=gbc[:, gb, :], op0=ALU.mult, op1=ALU.mult),
                    r=[("X", t), "rstd", ("gbc", gb)], w=[("X", t)])
        for t0 in range(0, NT, 4):
            SPDMA(out_d[s, t0 * 128:(t0 + 4) * 128, :].rearrange("(t p) d -> p t d", p=128), X[:, t0:t0 + 4, :],
                  r=[("X", t) for t in range(t0, t0 + 4)], key=("o", t0 // 4))

    for s in range(nseq):
        for t0 in range(0, NT, 4):
            SPDMA(X[:, t0:t0 + 4, :], x_d[s, t0 * 128:(t0 + 4) * 128, :].rearrange("(t p) d -> p t d", p=128),
                  w=[("X", t) for t in range(t0, t0 + 4)], key=("x", t0 // 4))
        if "attn" in phases:
            emit_norm(0)
            emit_attn()
        if "mlp0" in phases:
            emit_norm(1)
            emit_mlp(0)
        if "mlstm" in phases:
            emit_norm(2)
            emit_mlstm()
        if "mlp1" in phases:
            emit_norm(3)
            emit_mlp(1)
        if "dbg_hnT" in phases:
            emit_norm(1)
            SPDMA(dbg_d, hnT[:].rearrange("p a b -> p (a b)"), r=HN_ALL, key=("c", 5))
        emit_final(s, "final" in phases)

    S.finalize(nc, stack, {"sp": [("o", i) for i in range(4)] + ([("c", 5)] if "dbg_hnT" in phases else []) + ([("c", 6), ("c", 7), ("c", 8), ("c", 9)] if dbgh else [])})
    stack.close()
    return nc


def _t5_bucket_table():
    d = np.arange(256)
    max_exact = 16
    dd = np.maximum(d, 1).astype(np.float32)
    large = max_exact + (np.log(dd / max_exact) / math.log(128 / max_exact) * (32 - max_exact)).astype(np.int32)
    large = np.minimum(large, 31)
    return np.where(d < max_exact, d, large)


def host_consts(inp):
    f32 = np.float32
    cf = np.zeros((128, NCF), f32)
    s_idx = np.arange(128)[:, None]
    j_idx = np.arange(128)[None, :]
    cf[:, C_MASK:C_MASK + 128] = (s_idx <= j_idx).astype(f32)
    cf[:, C_ONES:C_ONES + 128] = 1.0
    rel = np.asarray(inp["rel_bias"], f32)
    cf[:, C_CB:C_CB + 8] = rel[31][None, :]
    for k, nm in enumerate(("attn_lambda_q1", "attn_lambda_k1", "attn_lambda_q2", "attn_lambda_k2")):
        cf[:, C_LQK + 64 * k:C_LQK + 64 * (k + 1)] = np.asarray(inp[nm], f32)[0][None, :]
    cf[:, C_SUBW:C_SUBW + 128] = np.asarray(inp["attn_subln"], f32)[0][None, :]
    cf[:, C_GB:C_GB + 4] = np.asarray(inp["mlstm_b_i"], f32)[0][None, :]
    cf[:, C_GB + 4:C_GB + 8] = np.asarray(inp["mlstm_b_f"], f32)[0][None, :]
    cw = np.asarray(inp["mlstm_conv_w"], f32)[0]
    cf[:, C_CW:C_CW + 32] = cw.reshape(4, 8, 128).transpose(2, 1, 0).reshape(128, 32)
    cf[:, C_CBIAS:C_CBIAS + 8] = np.asarray(inp["mlstm_conv_b"], f32)[0].reshape(8, 128).T
    cf[:, C_EPS] = EPS
    cf[:, C_ONE1] = 1.0
    bt = _t5_bucket_table()
    kk = np.arange(128)[:, None]
    qq = np.arange(256)[None, :]
    dist = qq - kk
    idx = bt[np.clip(dist, 0, 255)]
    tt = rel[idx]
    tt = np.where((dist >= 0)[:, :, None], tt, f32(NEG))
    tt = np.ascontiguousarray(tt.transpose(0, 2, 1)).reshape(128, 8 * 256).astype(f32)
    gains = np.stack([np.asarray(inp["attn_norm"], f32)[0], np.asarray(inp["mlp_norm"], f32)[0],
                      np.asarray(inp["mlstm_norm"], f32)[0], np.asarray(inp["mlp_norm"], f32)[1],
                      np.asarray(inp["final_norm"], f32)], 0)
    gbc = np.ascontiguousarray(np.broadcast_to(gains[:, None, :], (5, 128, D))).astype(f32)
    hwbc = np.ascontiguousarray(np.broadcast_to(np.asarray(inp["mlstm_head_norm"], f32)[0][None, :], (128, D)))
    ident = np.eye(128, dtype=f32).astype(ml_dtypes.bfloat16)
    return dict(cf32=cf, tt=tt, gbc=gbc, hwbc=hwbc, ident=ident)


def make_in_maps(inp, ncores, nseq):
    c = host_consts(inp)
    f32 = np.float32
    shared = dict(
        a_w_in=np.ascontiguousarray(np.asarray(inp["attn_w_in"], f32)[0]),
        a_w_out=np.ascontiguousarray(np.asarray(inp["attn_w_out"], f32)[0]),
        m_w_in=np.ascontiguousarray(np.asarray(inp["mlstm_w_in"], f32)[0]),
        m_w_out=np.ascontiguousarray(np.asarray(inp["mlstm_w_out"], f32)[0]),
        w1=np.ascontiguousarray(np.asarray(inp["mlp_w1"], f32)),
        w2=np.ascontiguousarray(np.asarray(inp["mlp_w2"], f32)),
        **c,
    )
    x = np.asarray(inp["x"], f32)
    maps = []
    for i in range(ncores):
        m = dict(shared)
        m["x"] = np.ascontiguousarray(x[i * nseq:(i + 1) * nseq])
        maps.append(m)
    return maps


_NC_CACHE = {}


def kernel(**inputs):
    nseq = 16 // NCORES
    if "full" not in _NC_CACHE:
        _NC_CACHE["full"] = build(nseq=nseq)
    nc = _NC_CACHE["full"]
    in_maps = make_in_maps(inputs, NCORES, nseq)
    res = run_bass_kernel_spmd(nc, in_maps, core_ids=list(range(NCORES)))
    out = np.concatenate([np.asarray(r["out"]) for r in res.results], axis=0)
    return out.astype(np.float32)
```

```python
import math
from contextlib import ExitStack

import numpy as np
import ml_dtypes

import concourse.bass as bass
import concourse.mybir as mybir
from concourse.bass_utils import run_bass_kernel_spmd

F32 = mybir.dt.float32
BF16 = mybir.dt.bfloat16
AF = mybir.ActivationFunctionType
ALU = mybir.AluOpType
AX = mybir.AxisListType

NCORES = 8
SEQ = 2048
D = 1024
NT = SEQ // 128
EPS = 1e-6
NEG = -30000.0
LAMBDA_INIT = 0.8 - 0.6 * math.exp(-0.3 * 0)

C_MASK = 0
C_ONES = 128
C_CB = 256
C_LQK = 264
C_SUBW = 520
C_GB = 648
C_CW = 656
C_CBIAS = 688
C_EPS = 696
C_ONE1 = 697
NCF = 704


class Op:
    __slots__ = ("eng", "fn", "deps", "dma_key", "dma_cnt", "inc", "ticket", "idx")

    def __init__(self, eng, fn, dma_key):
        self.eng = eng
        self.fn = fn
        self.deps = []
        self.dma_key = dma_key
        self.dma_cnt = 0
        self.inc = False
        self.ticket = 0
        self.idx = 0


class Sched:
    ENGS = ("pe", "act", "dve", "pool", "sp")

    def __init__(self):
        self.ops = {e: [] for e in self.ENGS}
        self.res = {}
        self.dma_counts = {}
        self.barrier_ops = None
        self.passed = set()

    def _st(self, k):
        st = self.res.get(k)
        if st is None:
            st = self.res[k] = [None, {}]
        return st

    def snapshot(self, keys):
        out = []
        for k in keys:
            st = self._st(k)
            if st[0] is not None:
                out.append(st[0])
            out.extend(st[1].values())
        return out

    def add(self, eng, fn, reads=(), writes=(), war=(), dma_key=None, after=()):
        op = Op(eng, fn, dma_key)
        op.idx = len(self.ops[eng])
        deps = {}

        def dep(o):
            if o is None or o is op:
                return
            key = ("dma", id(o)) if o.dma_key is not None else o.eng
            cur = deps.get(key)
            if cur is None or o.idx > cur.idx:
                deps[key] = o

        for r in reads:
            st = self._st(r)
            dep(st[0])
        for w in writes:
            st = self._st(w)
            dep(st[0])
            for o in st[1].values():
                dep(o)
        for w in war:
            st = self._st(w)
            for o in st[1].values():
                dep(o)
        for o in after:
            dep(o)
        if self.barrier_ops is not None and eng not in self.passed:
            for o in self.barrier_ops:
                dep(o)
            self.passed.add(eng)
        for r in reads:
            st = self._st(r)
            rk = ("dma", id(op)) if dma_key is not None else eng
            st[1][rk] = op
        for w in writes:
            self.res[w] = [op, {}]
        if dma_key is not None:
            c = self.dma_counts.get(dma_key, 0) + 1
            self.dma_counts[dma_key] = c
            op.dma_cnt = c
        op.deps = list(deps.values())
        self.ops[eng].append(op)
        return op

    def barrier(self):
        ops = []
        for e in ("pe", "act", "dve"):
            if self.ops[e]:
                ops.append(self.ops[e][-1])
        self.barrier_ops = ops
        self.passed = set()

    def simulate(self):
        done = set()
        ptr = {e: 0 for e in self.ENGS}
        total = sum(len(v) for v in self.ops.values())
        n = 0
        while n < total:
            progressed = False
            for e in self.ENGS:
                while ptr[e] < len(self.ops[e]):
                    op = self.ops[e][ptr[e]]
                    if all(id(d) in done for d in op.deps):
                        done.add(id(op))
                        ptr[e] += 1
                        n += 1
                        progressed = True
                    else:
                        break
            if not progressed:
                return {e: (ptr[e], len(self.ops[e])) for e in self.ENGS}
        return None

    def finalize(self, nc, stack, final_waits):
        for e in self.ENGS:
            for op in self.ops[e]:
                for d in op.deps:
                    if d.dma_key is None:
                        if not (d.eng == "pe" and op.eng == "pe"):
                            d.inc = True
        for e in self.ENGS:
            n = 0
            for op in self.ops[e]:
                if op.inc and op.dma_key is None:
                    n += 1
                    op.ticket = n
        esem = {e: stack.enter_context(nc.semaphore("s_" + e)) for e in self.ENGS}
        dsem = {}
        for k in self.dma_counts:
            dsem[k] = stack.enter_context(nc.semaphore("d_" + "_".join(str(x) for x in k)))
        block = stack.enter_context(nc.Block())
        sched = self

        def replay(e, eng):
            waited = {}
            for op in sched.ops[e]:
                for d in op.deps:
                    if d.dma_key is not None:
                        sem, val, sk = dsem[d.dma_key], 16 * d.dma_cnt, ("d", d.dma_key)
                    else:
                        if d.eng == "pe" and e == "pe":
                            continue
                        sem, val, sk = esem[d.eng], d.ticket, ("e", d.eng)
                    if waited.get(sk, 0) >= val:
                        continue
                    waited[sk] = val
                    eng.wait_ge(sem, val)
                ins = op.fn(eng)
                if op.dma_key is not None:
                    ins.then_inc(dsem[op.dma_key], 16)
                elif op.inc:
                    ins.then_inc(esem[e], 1)
            for k in final_waits.get(e, ()):
                eng.wait_ge(dsem[k], 16 * sched.dma_counts[k])

        @block.tensor
        def _(eng):
            replay("pe", eng)

        @block.scalar
        def _(eng):
            replay("act", eng)

        @block.vector
        def _(eng):
            replay("dve", eng)

        @block.gpsimd
        def _(eng):
            replay("pool", eng)

        @block.sync
        def _(eng):
            replay("sp", eng)


def build(nseq=2, phases=("attn", "mlp0", "mlstm", "mlp1", "final")):
    nc = bass.Bass("TRN2", target_bir_lowering=False)
    x_d = nc.dram_tensor("x", [nseq, SEQ, D], F32, kind="ExternalInput").ap()
    out_d = nc.dram_tensor("out", [nseq, SEQ, D], F32, kind="ExternalOutput").ap()
    a_w_in = nc.dram_tensor("a_w_in", [D, 3072], F32, kind="ExternalInput").ap()
    a_w_out = nc.dram_tensor("a_w_out", [D, D], F32, kind="ExternalInput").ap()
    m_w_in = nc.dram_tensor("m_w_in", [D, 3080], F32, kind="ExternalInput").ap()
    m_w_out = nc.dram_tensor("m_w_out", [D, D], F32, kind="ExternalInput").ap()
    w1_d = nc.dram_tensor("w1", [2, D, 4096], F32, kind="ExternalInput").ap()
    w2_d = nc.dram_tensor("w2", [2, 4096, D], F32, kind="ExternalInput").ap()
    cf_d = nc.dram_tensor("cf32", [128, NCF], F32, kind="ExternalInput").ap()
    id_d = nc.dram_tensor("ident", [128, 128], BF16, kind="ExternalInput").ap()
    tt_d = nc.dram_tensor("tt", [128, 8 * 256], F32, kind="ExternalInput").ap()
    gbc_d = nc.dram_tensor("gbc", [5, 128, D], F32, kind="ExternalInput").ap()
    hw_d = nc.dram_tensor("hwbc", [128, D], F32, kind="ExternalInput").ap()
    dbg_d = nc.dram_tensor("dbg", [128, 8 * SEQ], BF16, kind="ExternalOutput").ap() if "dbg_hnT" in phases else None
    dbgh = [int(p[8:]) for p in phases if p.startswith("dbg_otok")]
    dbo_d = nc.dram_tensor("dbo", [128, NT * 128], BF16, kind="ExternalOutput").ap() if dbgh else None
    dbq_d = nc.dram_tensor("dbq", [128, 3 * SEQ], BF16, kind="ExternalOutput").ap() if dbgh else None

    S = Sched()
    stack = ExitStack()
    sb = lambda name, shape, dt: stack.enter_context(nc.sbuf_tensor(name, shape, dt))
    X = sb("X", [128, NT, D], F32)
    hnT = sb("hnT", [128, 8, SEQ], BF16)
    WS = sb("WS", [128, 4, 4096], BF16)
    ARENA_B = 60 * 1024
    arena = sb("arena", [128, ARENA_B // 2], BF16)
    CF = sb("CF", [128, NCF], F32)
    ident = sb("identsb", [128, 128], BF16)
    xn = sb("xn", [128, 2, D], BF16)
    junk = sb("junk", [128, D], BF16)
    gbc = sb("gbcs", [128, 2, D], F32)
    ssq = sb("ssq", [128, NT], F32)
    rstd = sb("rstd", [128, NT], F32)
    lnt = sb("lnt", [128, NT], F32)
    PS = [stack.enter_context(nc.psum_tensor(f"ps{i}", [128, 512], F32)) for i in range(8)]

    def carve(off, dt, *dims):
        n = 1
        for d_ in dims:
            n *= d_
        nb = n * (4 if dt == F32 else 2)
        assert off % 4 == 0 and off + nb <= ARENA_B, (off, nb)
        ap = arena[:, off // 2:(off + nb) // 2]
        if dt == F32:
            ap = ap.bitcast(F32)
        if len(dims) == 2:
            ap = ap.rearrange("p (a b) -> p a b", a=dims[0])
        elif len(dims) == 3:
            ap = ap.rearrange("p (a b c) -> p a b c", a=dims[0], b=dims[1])
        return ap, off + ((nb + 3) // 4) * 4

    PE = lambda fn, r=(), w=(): S.add("pe", fn, r, w)
    ACT = lambda fn, r=(), w=(): S.add("act", fn, r, w)
    DVE = lambda fn, r=(), w=(): S.add("dve", fn, r, w)

    def SPDMA(out, in_, r=(), w=(), key=None):
        return S.add("sp", lambda e: e.dma_start(out=out, in_=in_), r, w, dma_key=key)

    cfc = lambda c0, n: CF[:, c0:c0 + n]

    plan = []
    for s in range(nseq):
        if "attn" in phases:
            for h in range(8):
                plan.append(("a_qkv", h))
                if h % 2 == 1:
                    plan.append(("a_wo", h // 2))
        if "mlp0" in phases:
            for g in range(8):
                plan.append(("w1", 0, g))
                plan.append(("w2", 0, g))
        if "mlstm" in phases:
            plan.append(("m_g",))
            for hd in range(4):
                plan.append(("m_qk", hd))
                plan.append(("m_v", hd))
                plan.append(("m_o", hd))
                plan.append(("m_wo", hd))
        if "mlp1" in phases:
            for g in range(8):
                plan.append(("w1", 1, g))
                plan.append(("w2", 1, g))

    wstate = {"issued": 0, "next": 0}

    def w_issue(i):
        tag = plan[i]
        slot = i % 4
        dst = WS[:, slot, :]
        kind = tag[0]
        parts = []
        if kind == "a_qkv":
            h = tag[1]
            for t_ in range(3):
                src = a_w_in[:, t_ * 1024 + h * 128:t_ * 1024 + (h + 1) * 128].rearrange("(c p) n -> p c n", p=128)
                o = dst[:, 0:8 * 384].rearrange("p (c t n) -> p c t n", c=8, t=3)[:, :, t_, :]
                parts.append((o, src))
        elif kind == "a_wo":
            hp = tag[1]
            src = a_w_out[hp * 256:(hp + 1) * 256, :].rearrange("(c p) n -> p c n", p=128)
            parts.append((dst[:, 0:2048].rearrange("p (c n) -> p c n", c=2), src))
        elif kind == "w1":
            _, l, g = tag
            src = w1_d[l, :, g * 512:(g + 1) * 512].rearrange("(c p) n -> p c n", p=128)
            parts.append((dst[:, 0:4096].rearrange("p (c n) -> p c n", c=8), src))
        elif kind == "w2":
            _, l, g = tag
            src = w2_d[l, g * 512:(g + 1) * 512, :].rearrange("(c p) n -> p c n", p=128)
            parts.append((dst[:, 0:4096].rearrange("p (c n) -> p c n", c=4), src))
        elif kind == "m_g":
            src = m_w_in[:, 3072:3080].rearrange("(c p) n -> p c n", p=128)
            parts.append((dst[:, 0:64].rearrange("p (c n) -> p c n", c=8), src))
        elif kind == "m_qk":
            hd = tag[1]
            for t_ in range(2):
                src = m_w_in[:, t_ * 512 + hd * 128:t_ * 512 + (hd + 1) * 128].rearrange("(c p) n -> p c n", p=128)
                o = dst[:, 0:2048].rearrange("p (c t n) -> p c t n", c=8, t=2)[:, :, t_, :]
                parts.append((o, src))
        elif kind == "m_v":
            hd = tag[1]
            src = m_w_in[:, 1024 + hd * 256:1024 + (hd + 1) * 256].rearrange("(c p) n -> p c n", p=128)
            parts.append((dst[:, 0:2048].rearrange("p (c n) -> p c n", c=8), src))
        elif kind == "m_o":
            hd = tag[1]
            src = m_w_in[:, 2048 + hd * 256:2048 + (hd + 1) * 256].rearrange("(c p) n -> p c n", p=128)
            parts.append((dst[:, 0:2048].rearrange("p (c n) -> p c n", c=8), src))
        elif kind == "m_wo":
            hd = tag[1]
            src = m_w_out[hd * 256:(hd + 1) * 256, :].rearrange("(c p) n -> p c n", p=128)
            parts.append((dst[:, 0:2048].rearrange("p (c n) -> p c n", c=2), src))
        else:
            raise ValueError(kind)
        snap = S.snapshot([("ws", slot, p_) for p_ in range(3)])
        for p_, (o, src) in enumerate(parts):
            S.add("pool", lambda e, o=o, src=src: e.dma_start(out=o, in_=src), (), [("ws", slot, p_)],
                  dma_key=("ws", slot, p_), after=snap)

    def w_next(tag):
        i = wstate["next"]
        assert plan[i] == tag, (plan[i], tag)
        while wstate["issued"] < min(len(plan), i + 3):
            w_issue(wstate["issued"])
            wstate["issued"] += 1
        wstate["next"] = i + 1
        slot = i % 4
        return WS[:, slot, :], ("ws", slot, 0)

    SPDMA(CF[:], cf_d, w=["CF"], key=("c", 0))
    SPDMA(ident[:], id_d, w=["ident"], key=("c", 1))
    gbstate = {"n": 0}

    def load_gbc(idx):
        b = gbstate["n"] % 2
        gbstate["n"] += 1
        SPDMA(gbc[:, b, :], gbc_d[idx], w=[("gbc", b)], key=("g", b))
        return b

    def emit_stats():
        for t in range(NT):
            ACT(lambda e, t=t: e.activation(out=junk[:], in_=X[:, t, :], func=AF.Square,
                                            accum_out=ssq[:, t:t + 1]),
                r=[("X", t)], w=[("ssq", t)])
        ACT(lambda e: e.activation(out=lnt[:], in_=ssq[:], func=AF.Ln, scale=1.0 / D, bias=cfc(C_EPS, 1)),
            r=[("ssq", t) for t in range(NT)] + ["CF"], w=["lnt"])
        ACT(lambda e: e.activation(out=rstd[:], in_=lnt[:], func=AF.Exp, scale=-0.5),
            r=["lnt"], w=["rstd"])

    def emit_norm(gidx):
        gb = load_gbc(gidx)
        emit_stats()
        for t in range(NT):
            b = t % 2
            DVE(lambda e, t=t, b=b: e.scalar_tensor_tensor(out=xn[:, b, :], in0=X[:, t, :],
                                                           scalar=rstd[:, t:t + 1], in1=gbc[:, gb, :],
                                                           op0=ALU.mult, op1=ALU.mult),
                r=[("X", t), "rstd", ("gbc", gb)], w=[("xn", b)])
            bank = 6 + (t % 2)
            pv = PS[bank][:].bitcast(BF16)
            for c in range(8):
                PE(lambda e, c=c, b=b, pv=pv: e.transpose(out=pv[:, c * 128:(c + 1) * 128],
                                                          in_=xn[:, b, c * 128:(c + 1) * 128],
                                                          identity=ident[:]),
                   r=[("xn", b), "ident"], w=[("ps", bank)])
            o = hnT[:, :, t * 128:(t + 1) * 128]
            i_ = pv.rearrange("p (c n) -> p c n", c=8)
            if t % 2 == 0:
                ACT(lambda e, o=o, i_=i_: e.activation(out=o, in_=i_, func=AF.Copy),
                    r=[("ps", bank)], w=[("hnT", t)])
            else:
                DVE(lambda e, o=o, i_=i_: e.tensor_copy(out=o, in_=i_),
                    r=[("ps", bank)], w=[("hnT", t)])

    HN_ALL = [("hnT", t) for t in range(NT)]

    def emit_mlp(layer):
        S.barrier()
        off = 0
        h1T, off = carve(off, BF16, 2, 4, SEQ)
        rtmp, off = carve(off, F32, 2, 512)
        cnt = {"a": 0, "b": 0, "r": 0}

        def g1(g, tq, w1v, wk1):
            hb = g % 2
            for c in range(4):
                bank = cnt["a"] % 4
                cnt["a"] += 1
                for d_ in range(8):
                    PE(lambda e, bank=bank, c=c, d_=d_: e.matmul(
                        PS[bank][:], lhsT=w1v[:, d_, c * 128:(c + 1) * 128],
                        rhs=hnT[:, d_, tq * 512:(tq + 1) * 512], start=(d_ == 0), stop=(d_ == 7)),
                       r=[wk1] + [("hnT", t) for t in range(tq * 4, tq * 4 + 4)], w=[("ps", bank)])
                rb = cnt["r"] % 2
                cnt["r"] += 1
                ACT(lambda e, bank=bank, rb=rb: e.activation(out=rtmp[:, rb, :], in_=PS[bank][:], func=AF.Relu),
                    r=[("ps", bank)], w=[("rtmp", rb)])
                ACT(lambda e, rb=rb, c=c, hb=hb: e.activation(out=h1T[:, hb, c, tq * 512:(tq + 1) * 512],
                                                              in_=rtmp[:, rb, :], func=AF.Square),
                    r=[("rtmp", rb)], w=[("h1T", hb, c, tq)])

        def g2(g, tq, w2v, wk2):
            hb = g % 2
            for t in range(tq * 4, tq * 4 + 4):
                for n in range(2):
                    bank = 4 + cnt["b"] % 4
                    cnt["b"] += 1
                    for c in range(4):
                        PE(lambda e, bank=bank, c=c, t=t, n=n: e.matmul(
                            PS[bank][:], lhsT=h1T[:, hb, c, t * 128:(t + 1) * 128],
                            rhs=w2v[:, c, n * 512:(n + 1) * 512], start=(c == 0), stop=(c == 3)),
                           r=[wk2, ("h1T", hb, c, tq)], w=[("ps", bank)])
                    DVE(lambda e, bank=bank, t=t, n=n: e.tensor_tensor(
                        out=X[:, t, n * 512:(n + 1) * 512], in0=PS[bank][:],
                        in1=X[:, t, n * 512:(n + 1) * 512], op=ALU.add),
                        r=[("ps", bank), ("X", t)], w=[("X", t)])

        for g in range(8):
            w1s, wk1 = w_next(("w1", layer, g))
            w2s, wk2 = w_next(("w2", layer, g))
            w1v = w1s[:, 0:4096].rearrange("p (c n) -> p c n", c=8)
            w2v = w2s[:, 0:4096].rearrange("p (c n) -> p c n", c=4)
            g1(g, 0, w1v, wk1)
            for tq in range(1, 4):
                g1(g, tq, w1v, wk1)
                g2(g, tq - 1, w2v, wk2)
            g2(g, 3, w2v, wk2)

    def emit_attn():
        S.barrier()
        off = 0
        TT, off = carve(off, F32, 8, 256)
        qTm, off = carve(off, BF16, 2, SEQ)
        kT, off = carve(off, BF16, SEQ)
        V, off = carve(off, BF16, NT, 130)
        PT, off = carve(off, BF16, 4, 512)
        ntmp, off = carve(off, F32, 2, 256)
        osb, off = carve(off, F32, 4, 128)
        t1sb, off = carve(off, F32, 4, 128)
        otok, off = carve(off, BF16, NT, 128)
        oTp, off = carve(off, BF16, 2, SEQ)
        subw, off = carve(off, F32, 128)
        sm, off = carve(off, F32, 64)
        lqk, off = carve(off, F32, 2, 64)
        accsb, off = carve(off, F32, 2, 1032)
        SPDMA(TT.rearrange("p a b -> p (a b)"), tt_d, w=["TT"], key=("c", 2))
        for k in range(2):
            DVE(lambda e, k=k: e.tensor_tensor(out=lqk[:, k, :], in0=cfc(C_LQK + 128 * k, 64),
                                               in1=cfc(C_LQK + 128 * k + 64, 64), op=ALU.mult),
                r=["CF"], w=[("lqk", k)])
            DVE(lambda e, k=k: e.reduce_sum(out=sm[:, k:k + 1], in_=lqk[:, k, :], axis=AX.X),
                r=[("lqk", k)], w=[("sm", k)])
        ACT(lambda e: e.activation(out=sm[:, 2:4], in_=sm[:, 0:2], func=AF.Exp),
            r=[("sm", 0), ("sm", 1)], w=[("sm", 2)])
        DVE(lambda e: e.scalar_tensor_tensor(out=sm[:, 4:5], in0=sm[:, 2:3], scalar=float(LAMBDA_INIT),
                                             in1=sm[:, 3:4], op0=ALU.add, op1=ALU.subtract),
            r=[("sm", 2)], w=["lam"])
        DVE(lambda e: e.tensor_scalar(out=subw, in0=cfc(C_SUBW, 128), scalar1=float(1.0 - LAMBDA_INIT),
                                      scalar2=None, op0=ALU.mult),
            r=["CF"], w=["subw"])
        DVE(lambda e: e.memset(V[:, :, 128:130], 1.0), w=["Vones"])
        DVE(lambda e: e.memset(qTm[64:128, 0, :], 0.0), w=["qz0"])
        DVE(lambda e: e.memset(qTm[0:64, 1, :], 0.0), w=["qz1"])
        lam = sm[:, 4:5]
        cnt = {"st": 0, "pt": 0, "nt": 0, "pj": 0, "ab": 0}
        STB = (0, 1, 2)
        ACCA = (3, 4)
        ACCB = 5

        def acc(m, qs):
            if qs < 3:
                return PS[ACCA[m]][:, qs * 129:(qs + 1) * 129], ("ps", ACCA[m])
            return PS[ACCB][:, m * 129:(m + 1) * 129], ("ps", ACCB)

        def attn_head(h):
            wsl, wk = w_next(("a_qkv", h))
            Wv_ = wsl[:, 0:8 * 384].rearrange("p (c t n) -> p c t n", c=8, t=3)
            for which in (0, 1):
                for tq in range(4):
                    bank = 6 + cnt["pj"] % 2
                    cnt["pj"] += 1
                    for d_ in range(8):
                        PE(lambda e, bank=bank, d_=d_, which=which, tq=tq: e.matmul(
                            PS[bank][:], lhsT=Wv_[:, d_, which, :], rhs=hnT[:, d_, tq * 512:(tq + 1) * 512],
                            start=(d_ == 0), stop=(d_ == 7)),
                           r=[(wk[0], wk[1], which)] + [("hnT", t) for t in range(tq * 4, tq * 4 + 4)], w=[("ps", bank)])
                    if which == 0:
                        ACT(lambda e, bank=bank, tq=tq: e.activation(
                            out=qTm[0:64, 0, tq * 512:(tq + 1) * 512], in_=PS[bank][0:64, :], func=AF.Copy, scale=0.125),
                            r=[("ps", bank)], w=[("qk", 0)])
                        ACT(lambda e, bank=bank, tq=tq: e.activation(
                            out=qTm[64:128, 1, tq * 512:(tq + 1) * 512], in_=PS[bank][64:128, :], func=AF.Copy,
                            scale=0.125),
                            r=[("ps", bank), ("qk", 0)], w=[("qk", 0)])
                    else:
                        ACT(lambda e, bank=bank, tq=tq: e.activation(
                            out=kT[:, tq * 512:(tq + 1) * 512], in_=PS[bank][:], func=AF.Copy),
                            r=[("ps", bank)], w=[("qk", 1)])
            for t0 in range(0, NT, 4):
                bank = 6 + cnt["pj"] % 2
                cnt["pj"] += 1
                for tt_ in range(4):
                    t = t0 + tt_
                    for d_ in range(8):
                        PE(lambda e, bank=bank, d_=d_, t=t, tt_=tt_: e.matmul(
                            PS[bank][:, tt_ * 128:(tt_ + 1) * 128], lhsT=hnT[:, d_, t * 128:(t + 1) * 128],
                            rhs=Wv_[:, d_, 2, :], start=(d_ == 0), stop=(d_ == 7)),
                           r=[(wk[0], wk[1], 2), ("hnT", t)], w=[("ps", bank)])
                DVE(lambda e, bank=bank, t0=t0: e.tensor_copy(
                    out=V[:, t0:t0 + 4, 0:128], in_=PS[bank][:].rearrange("p (a b) -> p a b", a=4)),
                    r=[("ps", bank)], w=["V"])
            tiles = [(j, i, m) for j in range(4) for i in range(4 * j + 4) for m in range(2)]
            LA = 2
            info = {}

            def stage_a(n):
                j, i, m = tiles[n]
                qb0 = max(4 * j, i)
                q0 = qb0 * 128
                N = (4 * j + 4) * 128 - q0
                d0 = qb0 - i
                bank = STB[cnt["st"] % 3]
                cnt["st"] += 1
                PE(lambda e: e.matmul(
                    PS[bank][:, 0:N], lhsT=kT[:, i * 128:(i + 1) * 128],
                    rhs=qTm[:, m, q0:q0 + N], start=True, stop=True),
                   r=[("qk", 0), ("qk", 1), "qz0", "qz1"], w=[("ps", bank)])
                pb = cnt["pt"] % 4
                cnt["pt"] += 1
                if d0 >= 2:
                    n1 = 0
                else:
                    tc0 = 0 if d0 == 0 else 128
                    n1 = min(256 - tc0, N)
                    nb = cnt["nt"] % 2
                    cnt["nt"] += 1
                    DVE(lambda e: e.tensor_tensor(
                        out=ntmp[:, nb, 0:n1], in0=PS[bank][:, 0:n1], in1=TT[:, h, tc0:tc0 + n1], op=ALU.add),
                        r=[("ps", bank), "TT"], w=[("ntmp", nb)])
                    ACT(lambda e: e.activation(out=PT[:, pb, 0:n1], in_=ntmp[:, nb, 0:n1], func=AF.Exp),
                        r=[("ntmp", nb)], w=[("PT", pb)])
                if N > n1:
                    ACT(lambda e: e.activation(out=PT[:, pb, n1:N], in_=PS[bank][:, n1:N], func=AF.Exp,
                                               bias=cfc(C_CB + h, 1)),
                        r=[("ps", bank), "CF"], w=[("PT", pb)])
                info[n] = (pb, q0, qb0)

            def stage_b(n):
                j, i, m = tiles[n]
                pb, q0, qb0 = info[n]
                for qs in range(qb0 - 4 * j, 4):
                    qb = 4 * j + qs
                    col = qb * 128 - q0
                    a_ap, a_key = acc(m, qs)
                    first = (i == 0) and ((qs == 0) or (qs == 3 and m == 0))
                    PE(lambda e, a_ap=a_ap, col=col, first=first, qb=qb: e.matmul(
                        a_ap, lhsT=PT[:, pb, col:col + 128], rhs=V[:, i, 0:129],
                        start=first, stop=(i == qb), skip_group_check=True),
                       r=[("PT", pb), "V", "Vones"], w=[a_key])
                if i == 4 * j + 3 and m == 1:
                    finalize(j)

            def finalize(j):
                ab = cnt["ab"] % 2
                cnt["ab"] += 1
                DVE(lambda e: e.tensor_copy(out=accsb[:, ab, 0:387], in_=PS[ACCA[0]][:, 0:387]),
                    r=[("ps", ACCA[0])], w=[("accsb", ab, 0)])
                ACT(lambda e: e.activation(out=accsb[:, ab, 387:774], in_=PS[ACCA[1]][:, 0:387], func=AF.Copy),
                    r=[("ps", ACCA[1])], w=[("accsb", ab, 1)])
                DVE(lambda e: e.tensor_copy(out=accsb[:, ab, 774:1032], in_=PS[ACCB][:, 0:258]),
                    r=[("ps", ACCB)], w=[("accsb", ab, 2)])
                akeys = [("accsb", ab, k_) for k_ in range(3)]

                def sacc(m, qs):
                    o_ = (m * 387 + qs * 129) if qs < 3 else (774 + m * 129)
                    return accsb[:, ab, o_:o_ + 129]

                for qs in range(4):
                    a0 = sacc(0, qs)
                    a1 = sacc(1, qs)
                    c0 = 8 + qs * 4
                    DVE(lambda e, a0=a0, c0=c0: e.reciprocal(out=sm[:, c0:c0 + 1], in_=a0[:, 128:129]),
                        r=akeys, w=[("smq", qs, 0)])
                    DVE(lambda e, a1=a1, c0=c0: e.reciprocal(out=sm[:, c0 + 1:c0 + 2], in_=a1[:, 128:129]),
                        r=akeys, w=[("smq", qs, 1)])
                    DVE(lambda e, c0=c0: e.tensor_tensor(out=sm[:, c0 + 2:c0 + 3], in0=sm[:, c0 + 1:c0 + 2],
                                                         in1=lam, op=ALU.mult),
                        r=[("smq", qs, 1), "lam"], w=[("smq", qs, 2)])
                    DVE(lambda e, a1=a1, c0=c0, qs=qs: e.tensor_scalar(
                        out=t1sb[:, qs, :], in0=a1[:, 0:128], scalar1=sm[:, c0 + 2:c0 + 3], scalar2=None,
                        op0=ALU.mult),
                        r=akeys + [("smq", qs, 2)], w=[("t1", qs)])
                    DVE(lambda e, a0=a0, c0=c0, qs=qs: e.scalar_tensor_tensor(
                        out=osb[:, qs, :], in0=a0[:, 0:128], scalar=sm[:, c0:c0 + 1], in1=t1sb[:, qs, :],
                        op0=ALU.mult, op1=ALU.subtract),
                        r=akeys + [("smq", qs, 0), ("t1", qs)], w=[("osb", qs)])
                    ACT(lambda e, qs=qs: e.activation(out=junk[:, 0:128], in_=osb[:, qs, :], func=AF.Square,
                                                      accum_out=sm[:, 32 + qs:33 + qs]),
                        r=[("osb", qs)], w=[("ss2", qs)])
                ACT(lambda e: e.activation(out=sm[:, 36:40], in_=sm[:, 32:36], func=AF.Ln, scale=1.0 / 128,
                                           bias=cfc(C_EPS, 1)),
                    r=[("ss2", q_) for q_ in range(4)] + ["CF"], w=["ln2"])
                ACT(lambda e: e.activation(out=sm[:, 40:44], in_=sm[:, 36:40], func=AF.Exp, scale=-0.5),
                    r=["ln2"], w=["rs2"])
                for qs in range(4):
                    t = 4 * j + qs
                    DVE(lambda e, qs=qs, t=t: e.scalar_tensor_tensor(
                        out=otok[:, t, :], in0=osb[:, qs, :], scalar=sm[:, 40 + qs:41 + qs], in1=subw,
                        op0=ALU.mult, op1=ALU.mult),
                        r=[("osb", qs), "rs2", "subw"], w=[("otok", t)])

            for n in range(len(tiles) + LA):
                if n < len(tiles):
                    stage_a(n)
                if n >= LA:
                    stage_b(n - LA)
            if dbgh and h == dbgh[0]:
                SPDMA(dbo_d, otok.rearrange("p a b -> p (a b)"), r=[("otok", t) for t in range(NT)], key=("c", 6))
                SPDMA(dbq_d[:, 0:SEQ], qTm[:, 0, :], r=[("qk", 0)], key=("c", 7))
                SPDMA(dbq_d[:, SEQ:2 * SEQ], kT, r=[("qk", 1)], key=("c", 8))
                SPDMA(dbq_d[:, 2 * SEQ:2 * SEQ + NT * 128].rearrange("p (a b) -> p a b", a=NT), V[:, :, 0:128], r=["V"], key=("c", 9))
            hh = h % 2
            for t0 in range(0, NT, 4):
                bank = 6 + cnt["pj"] % 2
                cnt["pj"] += 1
                pv = PS[bank][:].bitcast(BF16)
                for tt_ in range(4):
                    t = t0 + tt_
                    PE(lambda e, pv=pv, t=t, tt_=tt_: e.transpose(out=pv[:, tt_ * 128:(tt_ + 1) * 128],
                                                                  in_=otok[:, t, :], identity=ident[:]),
                       r=[("otok", t), "ident"], w=[("ps", bank)])
                ACT(lambda e, pv=pv, t0=t0, hh=hh: e.activation(out=oTp[:, hh, t0 * 128:(t0 + 4) * 128],
                                                                in_=pv[:, 0:512], func=AF.Copy),
                    r=[("ps", bank)], w=[("oTp", hh, t0)])
            if hh == 1:
                wsl2, wk2 = w_next(("a_wo", h // 2))
                wo = wsl2[:, 0:2048].rearrange("p (c n) -> p c n", c=2)
                for t in range(NT):
                    for n in range(2):
                        bank = 6 + cnt["pj"] % 2
                        cnt["pj"] += 1
                        for c in range(2):
                            PE(lambda e, bank=bank, c=c, t=t, n=n: e.matmul(
                                PS[bank][:], lhsT=oTp[:, c, t * 128:(t + 1) * 128],
                                rhs=wo[:, c, n * 512:(n + 1) * 512], start=(c == 0), stop=(c == 1)),
                               r=[wk2, ("oTp", c, (t // 4) * 4)], w=[("ps", bank)])
                        DVE(lambda e, bank=bank, t=t, n=n: e.tensor_tensor(
                            out=X[:, t, n * 512:(n + 1) * 512], in0=PS[bank][:],
                            in1=X[:, t, n * 512:(n + 1) * 512], op=ALU.add),
                            r=[("ps", bank), ("X", t)], w=[("X", t)])

        for h in range(8):
            attn_head(h)

    def emit_mlstm():
        S.barrier()
        off = 0
        A1_off = off
        raw, off = carve(off, F32, SEQ + 16)
        cacc, off = carve(off, F32, SEQ)
        numsb, _ = carve(A1_off, F32, NT, 257)
        assert NT * 257 * 4 <= off - A1_off
        qT, off = carve(off, BF16, SEQ)
        kT, off = carve(off, BF16, SEQ)
        vaug, off = carve(off, BF16, NT, 258)
        og, off = carve(off, BF16, NT, 256)
        htT, off = carve(off, BF16, 2, SEQ)
        hwbc, off = carve(off, F32, 256)
        G, off = carve(off, F32, NT, 8)
        nl, off = carve(off, F32, NT, 4)
        fq, off = carve(off, F32, NT, 4)
        fk, off = carve(off, F32, NT, 4)
        gg, off = carve(off, F32, NT, 4)
        eb, off = carve(off, F32, NT, 4)
        tmpa, off = carve(off, F32, NT, 4)
        tmpb, off = carve(off, F32, NT, 4)
        CTf, off = carve(off, F32, 258)
        CTb, off = carve(off, BF16, 2, 258)
        kpp, off = carve(off, BF16, 2, 128)
        scsb, off = carve(off, BF16, 2, 128)
        dd, off = carve(off, F32, NT)
        r2, off = carve(off, F32, NT)
        ss3, off = carve(off, F32, NT)
        ln3, off = carve(off, F32, NT)
        r3, off = carve(off, F32, NT)
        maskT = cfc(C_MASK, 128)
        ones = cfc(C_ONES, 128)
        cnt = {"pj": 0, "k": 0}
        wsl, wk = w_next(("m_g",))
        wg = wsl[:, 0:64].rearrange("p (c n) -> p c n", c=8)
        for t in range(NT):
            for d_ in range(8):
                PE(lambda e, t=t, d_=d_: e.matmul(PS[0][:, t * 8:(t + 1) * 8], lhsT=hnT[:, d_, t * 128:(t + 1) * 128],
                                                  rhs=wg[:, d_, :], start=(d_ == 0), stop=(d_ == 7)),
                   r=[wk, ("hnT", t)], w=[("ps", 0)])
        Gv = PS[0][:, 0:128].rearrange("p (t n) -> p t n", t=NT)
        for t in range(NT):
            pass
        DVE(lambda e: e.tensor_tensor(out=G, in0=Gv, in1=cfc(C_GB, 8).unsqueeze(1).to_broadcast([128, NT, 8]),
                                      op=ALU.add),
            r=[("ps", 0), "CF"], w=["G"])
        ACT(lambda e: e.activation(out=tmpa, in_=G[:, :, 4:8], func=AF.Exp, scale=-1.0), r=["G"], w=["tmpa"])
        ACT(lambda e: e.activation(out=nl, in_=tmpa, func=AF.Ln, bias=cfc(C_ONE1, 1)), r=["tmpa", "CF"], w=["nl"])
        nlf = nl.rearrange("p t h -> p (t h)")
        PE(lambda e: e.matmul(PS[1][:, 0:64], lhsT=maskT, rhs=nlf, start=True, stop=True),
           r=["nl", "CF"], w=[("ps", 1)])
        PE(lambda e: e.matmul(PS[1][:, 64:128], lhsT=ones, rhs=nlf, start=True, stop=True),
           r=["nl", "CF"], w=[("ps", 1)])
        cum = PS[1][:, 0:64].rearrange("p (t h) -> p t h", t=NT)
        tot = PS[1][:, 64:128].rearrange("p (t h) -> p t h", t=NT)
        ACT(lambda e: e.activation(out=fq, in_=cum, func=AF.Exp, scale=-1.0), r=[("ps", 1)], w=["fq0"])
        DVE(lambda e: e.tensor_scalar(out=fq, in0=fq, scalar1=float(128 ** -0.5), scalar2=None, op0=ALU.mult),
            r=["fq0"], w=["fq"])
        DVE(lambda e: e.tensor_tensor(out=tmpa, in0=cum, in1=G[:, :, 0:4], op=ALU.add),
            r=[("ps", 1), "G", "nl"], w=["tmpa2"])
        ACT(lambda e: e.activation(out=fk, in_=tmpa, func=AF.Exp), r=["tmpa2"], w=["fk"])
        DVE(lambda e: e.tensor_tensor(out=tmpb, in0=tmpa, in1=tot, op=ALU.subtract),
            r=["tmpa2", ("ps", 1)], w=["tmpb"])
        ACT(lambda e: e.activation(out=gg, in_=tmpb, func=AF.Exp), r=["tmpb"], w=["gg"])
        ACT(lambda e: e.activation(out=eb, in_=tot, func=AF.Exp, scale=-1.0), r=[("ps", 1)], w=["eb"])
        DVE(lambda e: e.memset(vaug[:, :, 256:258], 1.0), w=["vones"])

        def m_head(hd):
            wqk_s, wkq = w_next(("m_qk", hd))
            wqk = wqk_s[:, 0:2048].rearrange("p (c t n) -> p c t n", c=8, t=2)
            S.add("sp", lambda e, hd=hd: e.dma_start(out=hwbc, in_=hw_d[:, hd * 256:(hd + 1) * 256]),
                  (), ["hwbc"], dma_key=("c", 3))
            DVE(lambda e: e.memset(raw[:, 0:3], 0.0), r=[], w=["A1", "rawz"])
            for which, dst in ((0, qT), (1, kT)):
                ch = which * 4 + hd
                for tq in range(4):
                    bank = 6 + cnt["pj"] % 2
                    cnt["pj"] += 1
                    for d_ in range(8):
                        PE(lambda e, bank=bank, d_=d_, which=which, tq=tq: e.matmul(
                            PS[bank][:], lhsT=wqk[:, d_, which, :], rhs=hnT[:, d_, tq * 512:(tq + 1) * 512],
                            start=(d_ == 0), stop=(d_ == 7)),
                           r=[(wkq[0], wkq[1], which)] + [("hnT", t) for t in range(tq * 4, tq * 4 + 4)], w=[("ps", bank)])
                    ACT(lambda e, bank=bank, tq=tq: e.activation(
                        out=raw[:, 3 + tq * 512:3 + (tq + 1) * 512], in_=PS[bank][:], func=AF.Copy),
                        r=[("ps", bank), "rawz"], w=["A1"])
                cw = lambda k_, ch=ch: cfc(C_CW + ch * 4 + k_, 1)
                DVE(lambda e, cw=cw: e.tensor_scalar(out=cacc, in0=raw[:, 3:3 + SEQ], scalar1=cw(3), scalar2=None,
                                                     op0=ALU.mult),
                    r=["A1", "rawz", "CF"], w=["cacc"])
                for k_ in (2, 1, 0):
                    DVE(lambda e, cw=cw, k_=k_: e.scalar_tensor_tensor(
                        out=cacc, in0=raw[:, k_:k_ + SEQ], scalar=cw(k_), in1=cacc, op0=ALU.mult, op1=ALU.add),
                        r=["A1", "rawz", "cacc"], w=["cacc"])
                ACT(lambda e, dst=dst, ch=ch: e.activation(out=dst, in_=cacc, func=AF.Silu,
                                                           bias=cfc(C_CBIAS + ch, 1)),
                    r=["cacc", "CF"], w=[("mqk", which)])
            wv_s, wkv = w_next(("m_v", hd))
            wv = wv_s[:, 0:2048].rearrange("p (c n) -> p c n", c=8)
            for t0 in range(0, NT, 2):
                bank = 6 + cnt["pj"] % 2
                cnt["pj"] += 1
                for tt_ in range(2):
                    t = t0 + tt_
                    for d_ in range(8):
                        PE(lambda e, bank=bank, d_=d_, t=t, tt_=tt_: e.matmul(
                            PS[bank][:, tt_ * 256:(tt_ + 1) * 256], lhsT=hnT[:, d_, t * 128:(t + 1) * 128],
                            rhs=wv[:, d_, :], start=(d_ == 0), stop=(d_ == 7)),
                           r=[wkv, ("hnT", t)], w=[("ps", bank)])
                DVE(lambda e, bank=bank, t0=t0: e.tensor_copy(
                    out=vaug[:, t0:t0 + 2, 0:256], in_=PS[bank][:].rearrange("p (a b) -> p a b", a=2)),
                    r=[("ps", bank)], w=["vaug"])
            wo_s, wko = w_next(("m_o", hd))
            wo_ = wo_s[:, 0:2048].rearrange("p (c n) -> p c n", c=8)
            for t0 in range(0, NT, 2):
                bank = 6 + cnt["pj"] % 2
                cnt["pj"] += 1
                for tt_ in range(2):
                    t = t0 + tt_
                    for d_ in range(8):
                        PE(lambda e, bank=bank, d_=d_, t=t, tt_=tt_: e.matmul(
                            PS[bank][:, tt_ * 256:(tt_ + 1) * 256], lhsT=hnT[:, d_, t * 128:(t + 1) * 128],
                            rhs=wo_[:, d_, :], start=(d_ == 0), stop=(d_ == 7)),
                           r=[wko, ("hnT", t)], w=[("ps", bank)])
                ACT(lambda e, bank=bank, t0=t0: e.activation(
                    out=og[:, t0:t0 + 2, :], in_=PS[bank][:].rearrange("p (a b) -> p a b", a=2), func=AF.Sigmoid),
                    r=[("ps", bank)], w=[("og", t0)])
            for c in range(NT):
                cs = slice(c * 128, (c + 1) * 128)
                sb_ = cnt["k"] % 2
                cnt["k"] += 1
                PE(lambda e, cs=cs: e.matmul(PS[2][:, 0:128], lhsT=kT[:, cs], rhs=qT[:, cs], start=True, stop=True),
                   r=[("mqk", 0), ("mqk", 1)], w=[("ps", 2)])
                DVE(lambda e, c=c, sb_=sb_, hd=hd: e.scalar_tensor_tensor(
                    out=scsb[:, sb_, :], in0=PS[2][:, 0:128], scalar=fk[:, c, hd:hd + 1], in1=maskT,
                    op0=ALU.mult, op1=ALU.mult),
                    r=[("ps", 2), "fk", "CF"], w=[("scsb", sb_)])
                cb_ = c % 2
                PE(lambda e, c=c, sb_=sb_: e.matmul(PS[3][:, 0:257], lhsT=scsb[:, sb_, :], rhs=vaug[:, c, 0:257],
                                                    start=True, stop=(c == 0)),
                   r=[("scsb", sb_), "vaug", "vones"], w=[("ps", 3)])
                if c > 0:
                    PE(lambda e, cs=cs, cb_=cb_: e.matmul(PS[3][:, 0:257], lhsT=qT[:, cs], rhs=CTb[:, cb_, 0:257],
                                                          start=False, stop=True),
                       r=[("mqk", 0), ("CTb", cb_)], w=[("ps", 3)])
                ACT(lambda e, c=c: e.activation(out=numsb[:, c, :], in_=PS[3][:, 0:257], func=AF.Copy),
                    r=[("ps", 3)], w=["A1"])
                if c < NT - 1:
                    pvb = PS[4][:].bitcast(BF16)
                    PE(lambda e, cs=cs, pvb=pvb: e.transpose(out=pvb[:, 0:128], in_=kT[:, cs], identity=ident[:]),
                       r=[("mqk", 1), "ident"], w=[("ps", 4)])
                    ACT(lambda e, c=c, sb_=sb_, pvb=pvb, hd=hd: e.activation(
                        out=kpp[:, sb_, :], in_=pvb[:, 0:128], func=AF.Copy, scale=gg[:, c, hd:hd + 1]),
                        r=[("ps", 4), "gg"], w=[("kpp", sb_)])
                    PE(lambda e, c=c, sb_=sb_: e.matmul(PS[5][:, 0:257], lhsT=kpp[:, sb_, :], rhs=vaug[:, c, 0:257],
                                                        start=True, stop=True),
                       r=[("kpp", sb_), "vaug", "vones"], w=[("ps", 5)])
                    nb_ = (c + 1) % 2
                    if c == 0:
                        DVE(lambda e: e.tensor_copy(out=CTf[:, 0:257], in_=PS[5][:, 0:257]),
                            r=[("ps", 5)], w=["CTf"])
                    else:
                        DVE(lambda e, c=c, hd=hd: e.scalar_tensor_tensor(
                            out=CTf[:, 0:257], in0=CTf[:, 0:257], scalar=eb[:, c, hd:hd + 1], in1=PS[5][:, 0:257],
                            op0=ALU.mult, op1=ALU.add),
                            r=[("ps", 5), "CTf", "eb"], w=["CTf"])
                    DVE(lambda e, nb_=nb_: e.tensor_copy(out=CTb[:, nb_, 0:257], in_=CTf[:, 0:257]),
                        r=["CTf"], w=[("CTb", nb_)])
            den = numsb[:, :, 256]
            DVE(lambda e, hd=hd: e.tensor_tensor(out=dd, in0=den, in1=fq[:, :, hd], op=ALU.mult),
                r=["A1", "fq"], w=["dd"])
            DVE(lambda e: e.scalar_tensor_tensor(out=dd, in0=dd, scalar=-1.0, in1=dd, op0=ALU.mult, op1=ALU.max),
                r=["dd"], w=["dd"])
            DVE(lambda e: e.tensor_scalar(out=dd, in0=dd, scalar1=1.0, scalar2=None, op0=ALU.max),
                r=["dd"], w=["dd"])
            DVE(lambda e: e.reciprocal(out=dd, in_=dd), r=["dd"], w=["dd"])
            DVE(lambda e, hd=hd: e.tensor_tensor(out=r2, in0=dd, in1=fq[:, :, hd], op=ALU.mult),
                r=["dd", "fq"], w=["r2"])
            for c in range(NT):
                ACT(lambda e, c=c: e.activation(out=junk[:, 0:256], in_=numsb[:, c, 0:256], func=AF.Square,
                                                scale=r2[:, c:c + 1], accum_out=ss3[:, c:c + 1]),
                    r=["A1", "r2"], w=[("ss3", c)])
            ACT(lambda e: e.activation(out=ln3, in_=ss3, func=AF.Ln, scale=1.0 / 256, bias=cfc(C_EPS, 1)),
                r=[("ss3", c) for c in range(NT)] + ["CF"], w=["ln3"])
            ACT(lambda e: e.activation(out=r3, in_=ln3, func=AF.Exp, scale=-0.5), r=["ln3"], w=["r3a"])
            DVE(lambda e: e.tensor_tensor(out=r3, in0=r3, in1=r2, op=ALU.mult), r=["r3a", "r2"], w=["r3"])
            for c in range(NT):
                DVE(lambda e, c=c: e.scalar_tensor_tensor(
                    out=numsb[:, c, 0:256], in0=numsb[:, c, 0:256], scalar=r3[:, c:c + 1], in1=hwbc,
                    op0=ALU.mult, op1=ALU.mult),
                    r=["A1", "r3", "hwbc"], w=[("hn", c)])
                DVE(lambda e, c=c: e.tensor_tensor(out=og[:, c, :], in0=numsb[:, c, 0:256], in1=og[:, c, :],
                                                   op=ALU.mult),
                    r=[("hn", c), ("og", (c // 2) * 2)], w=[("gated", c)])
            for cc in range(2):
                for t0 in range(0, NT, 4):
                    bank = 6 + cnt["pj"] % 2
                    cnt["pj"] += 1
                    pv = PS[bank][:].bitcast(BF16)
                    for tt_ in range(4):
                        t = t0 + tt_
                        PE(lambda e, pv=pv, t=t, tt_=tt_, cc=cc: e.transpose(
                            out=pv[:, tt_ * 128:(tt_ + 1) * 128], in_=og[:, t, cc * 128:(cc + 1) * 128],
                            identity=ident[:]),
                           r=[("gated", t), "ident"], w=[("ps", bank)])
                    ACT(lambda e, pv=pv, t0=t0, cc=cc: e.activation(out=htT[:, cc, t0 * 128:(t0 + 4) * 128],
                                                                    in_=pv[:, 0:512], func=AF.Copy),
                        r=[("ps", bank)], w=[("htT", cc, t0)])
            wout_s, wkout = w_next(("m_wo", hd))
            wout = wout_s[:, 0:2048].rearrange("p (c n) -> p c n", c=2)
            for t in range(NT):
                for n in range(2):
                    bank = 6 + cnt["pj"] % 2
                    cnt["pj"] += 1
                    for cc in range(2):
                        PE(lambda e, bank=bank, cc=cc, t=t, n=n: e.matmul(
                            PS[bank][:], lhsT=htT[:, cc, t * 128:(t + 1) * 128],
                            rhs=wout[:, cc, n * 512:(n + 1) * 512], start=(cc == 0), stop=(cc == 1)),
                           r=[wkout, ("htT", cc, (t // 4) * 4)], w=[("ps", bank)])
                    DVE(lambda e, bank=bank, t=t, n=n: e.tensor_tensor(
                        out=X[:, t, n * 512:(n + 1) * 512], in0=PS[bank][:],
                        in1=X[:, t, n * 512:(n + 1) * 512], op=ALU.add),
                        r=[("ps", bank), ("X", t)], w=[("X", t)])

        for hd in range(4):
            m_head(hd)

    def emit_final(s, do_norm):
        if do_norm:
            gb = load_gbc(4)
            emit_stats()
            for t in range(NT):
                DVE(lambda e, t=t: e.scalar_tensor_tensor(out=X[:, t, :], in0=X[:, t, :], scalar=rstd[:, t:t + 1],
                                                          in1=gbc[:, gb, :], op0=ALU.mult, op1=ALU.mult),
                    r=[("X", t), "rstd", ("gbc", gb)], w=[("X", t)])
        for t0 in range(0, NT, 4):
            SPDMA(out_d[s, t0 * 128:(t0 + 4) * 128, :].rearrange("(t p) d -> p t d", p=128), X[:, t0:t0 + 4, :],
                  r=[("X", t) for t in range(t0, t0 + 4)], key=("o", t0 // 4))

    for s in range(nseq):
        for t0 in range(0, NT, 4):
            SPDMA(X[:, t0:t0 + 4, :], x_d[s, t0 * 128:(t0 + 4) * 128, :].rearrange("(t p) d -> p t d", p=128),
                  w=[("X", t) for t in range(t0, t0 + 4)], key=("x", t0 // 4))
        if "attn" in phases:
            emit_norm(0)
            emit_attn()
        if "mlp0" in phases:
            emit_norm(1)
            emit_mlp(0)
        if "mlstm" in phases:
            emit_norm(2)
            emit_mlstm()
        if "mlp1" in phases:
            emit_norm(3)
            emit_mlp(1)
        if "dbg_hnT" in phases:
            emit_norm(1)
            SPDMA(dbg_d, hnT[:].rearrange("p a b -> p (a b)"), r=HN_ALL, key=("c", 5))
        emit_final(s, "final" in phases)

    dl = S.simulate()
    assert dl is None, f"deadlock in schedule: {dl}"
    S.finalize(nc, stack, {"sp": [("o", i) for i in range(4)] + ([("c", 5)] if "dbg_hnT" in phases else []) + ([("c", 6), ("c", 7), ("c", 8), ("c", 9)] if dbgh else [])})
    stack.close()
    return nc


def _t5_bucket_table():
    d = np.arange(256)
    max_exact = 16
    dd = np.maximum(d, 1).astype(np.float32)
    large = max_exact + (np.log(dd / max_exact) / math.log(128 / max_exact) * (32 - max_exact)).astype(np.int32)
    large = np.minimum(large, 31)
    return np.where(d < max_exact, d, large)


def host_consts(inp):
    f32 = np.float32
    cf = np.zeros((128, NCF), f32)
    s_idx = np.arange(128)[:, None]
    j_idx = np.arange(128)[None, :]
    cf[:, C_MASK:C_MASK + 128] = (s_idx <= j_idx).astype(f32)
    cf[:, C_ONES:C_ONES + 128] = 1.0
    rel = np.asarray(inp["rel_bias"], f32)
    cf[:, C_CB:C_CB + 8] = rel[31][None, :]
    for k, nm in enumerate(("attn_lambda_q1", "attn_lambda_k1", "attn_lambda_q2", "attn_lambda_k2")):
        cf[:, C_LQK + 64 * k:C_LQK + 64 * (k + 1)] = np.asarray(inp[nm], f32)[0][None, :]
    cf[:, C_SUBW:C_SUBW + 128] = np.asarray(inp["attn_subln"], f32)[0][None, :]
    cf[:, C_GB:C_GB + 4] = np.asarray(inp["mlstm_b_i"], f32)[0][None, :]
    cf[:, C_GB + 4:C_GB + 8] = np.asarray(inp["mlstm_b_f"], f32)[0][None, :]
    cw = np.asarray(inp["mlstm_conv_w"], f32)[0]
    cf[:, C_CW:C_CW + 32] = cw.reshape(4, 8, 128).transpose(2, 1, 0).reshape(128, 32)
    cf[:, C_CBIAS:C_CBIAS + 8] = np.asarray(inp["mlstm_conv_b"], f32)[0].reshape(8, 128).T
    cf[:, C_EPS] = EPS
    cf[:, C_ONE1] = 1.0
    bt = _t5_bucket_table()
    kk = np.arange(128)[:, None]
    qq = np.arange(256)[None, :]
    dist = qq - kk
    idx = bt[np.clip(dist, 0, 255)]
    tt = rel[idx]
    tt = np.where((dist >= 0)[:, :, None], tt, f32(NEG))
    tt = np.ascontiguousarray(tt.transpose(0, 2, 1)).reshape(128, 8 * 256).astype(f32)
    gains = np.stack([np.asarray(inp["attn_norm"], f32)[0], np.asarray(inp["mlp_norm"], f32)[0],
                      np.asarray(inp["mlstm_norm"], f32)[0], np.asarray(inp["mlp_norm"], f32)[1],
                      np.asarray(inp["final_norm"], f32)], 0)
    gbc = np.ascontiguousarray(np.broadcast_to(gains[:, None, :], (5, 128, D))).astype(f32)
    hwbc = np.ascontiguousarray(np.broadcast_to(np.asarray(inp["mlstm_head_norm"], f32)[0][None, :], (128, D)))
    ident = np.eye(128, dtype=f32).astype(ml_dtypes.bfloat16)
    return dict(cf32=cf, tt=tt, gbc=gbc, hwbc=hwbc, ident=ident)


def make_in_maps(inp, ncores, nseq):
    c = host_consts(inp)
    f32 = np.float32
    shared = dict(
        a_w_in=np.ascontiguousarray(np.asarray(inp["attn_w_in"], f32)[0]),
        a_w_out=np.ascontiguousarray(np.asarray(inp["attn_w_out"], f32)[0]),
        m_w_in=np.ascontiguousarray(np.asarray(inp["mlstm_w_in"], f32)[0]),
        m_w_out=np.ascontiguousarray(np.asarray(inp["mlstm_w_out"], f32)[0]),
        w1=np.ascontiguousarray(np.asarray(inp["mlp_w1"], f32)),
        w2=np.ascontiguousarray(np.asarray(inp["mlp_w2"], f32)),
        **c,
    )
    x = np.asarray(inp["x"], f32)
    maps = []
    for i in range(ncores):
        m = dict(shared)
        m["x"] = np.ascontiguousarray(x[i * nseq:(i + 1) * nseq])
        maps.append(m)
    return maps


_NC_CACHE = {}


def kernel(**inputs):
    nseq = 16 // NCORES
    if "full" not in _NC_CACHE:
        _NC_CACHE["full"] = build(nseq=nseq)
    nc = _NC_CACHE["full"]
    in_maps = make_in_maps(inputs, NCORES, nseq)
    res = run_bass_kernel_spmd(nc, in_maps, core_ids=list(range(NCORES)))
    out = np.concatenate([np.asarray(r["out"]) for r in res.results], axis=0)
    return out.astype(np.float32)
```

```python
import math
from contextlib import ExitStack

import numpy as np
import ml_dtypes

import concourse.bass as bass
import concourse.mybir as mybir
from concourse.bass_utils import run_bass_kernel_spmd

F32 = mybir.dt.float32
BF16 = mybir.dt.bfloat16
AF = mybir.ActivationFunctionType
ALU = mybir.AluOpType
AX = mybir.AxisListType

NCORES = 8
SEQ = 2048
D = 1024
NT = SEQ // 128
EPS = 1e-6
NEG = -30000.0
LAMBDA_INIT = 0.8 - 0.6 * math.exp(-0.3 * 0)

C_MASK = 0
C_ONES = 128
C_CB = 256
C_LQK = 264
C_SUBW = 520
C_GB = 648
C_CW = 656
C_CBIAS = 688
C_EPS = 696
C_ONE1 = 697
NCF = 704


class Op:
    __slots__ = ("eng", "fn", "deps", "dma_key", "dma_cnt", "inc", "ticket", "idx")

    def __init__(self, eng, fn, dma_key):
        self.eng = eng
        self.fn = fn
        self.deps = []
        self.dma_key = dma_key
        self.dma_cnt = 0
        self.inc = False
        self.ticket = 0
        self.idx = 0


class Sched:
    ENGS = ("pe", "act", "dve", "pool", "sp")

    def __init__(self):
        self.ops = {e: [] for e in self.ENGS}
        self.res = {}
        self.dma_counts = {}
        self.barrier_ops = None
        self.passed = set()

    def _st(self, k):
        st = self.res.get(k)
        if st is None:
            st = self.res[k] = [None, {}]
        return st

    def snapshot(self, keys):
        out = []
        for k in keys:
            st = self._st(k)
            if st[0] is not None:
                out.append(st[0])
            out.extend(st[1].values())
        return out

    def add(self, eng, fn, reads=(), writes=(), war=(), dma_key=None, after=()):
        op = Op(eng, fn, dma_key)
        op.idx = len(self.ops[eng])
        deps = {}

        def dep(o):
            if o is None or o is op:
                return
            key = ("dma", id(o)) if o.dma_key is not None else o.eng
            cur = deps.get(key)
            if cur is None or o.idx > cur.idx:
                deps[key] = o

        for r in reads:
            st = self._st(r)
            dep(st[0])
        for w in writes:
            st = self._st(w)
            dep(st[0])
            for o in st[1].values():
                dep(o)
        for w in war:
            st = self._st(w)
            for o in st[1].values():
                dep(o)
        for o in after:
            dep(o)
        if self.barrier_ops is not None and eng not in self.passed:
            for o in self.barrier_ops:
                dep(o)
            self.passed.add(eng)
        for r in reads:
            st = self._st(r)
            rk = ("dma", id(op)) if dma_key is not None else eng
            st[1][rk] = op
        for w in writes:
            self.res[w] = [op, {}]
        if dma_key is not None:
            c = self.dma_counts.get(dma_key, 0) + 1
            self.dma_counts[dma_key] = c
            op.dma_cnt = c
        op.deps = list(deps.values())
        self.ops[eng].append(op)
        return op

    def barrier(self):
        ops = []
        for e in ("pe", "act", "dve"):
            if self.ops[e]:
                ops.append(self.ops[e][-1])
        self.barrier_ops = ops
        self.passed = set()

    def simulate(self):
        done = set()
        ptr = {e: 0 for e in self.ENGS}
        total = sum(len(v) for v in self.ops.values())
        n = 0
        while n < total:
            progressed = False
            for e in self.ENGS:
                while ptr[e] < len(self.ops[e]):
                    op = self.ops[e][ptr[e]]
                    if all(id(d) in done for d in op.deps):
                        done.add(id(op))
                        ptr[e] += 1
                        n += 1
                        progressed = True
                    else:
                        break
            if not progressed:
                return {e: (ptr[e], len(self.ops[e])) for e in self.ENGS}
        return None

    def finalize(self, nc, stack, final_waits):
        for e in self.ENGS:
            for op in self.ops[e]:
                for d in op.deps:
                    if d.dma_key is None:
                        if not (d.eng == "pe" and op.eng == "pe"):
                            d.inc = True
        for e in self.ENGS:
            n = 0
            for op in self.ops[e]:
                if op.inc and op.dma_key is None:
                    n += 1
                    op.ticket = n
        esem = {e: stack.enter_context(nc.semaphore("s_" + e)) for e in self.ENGS}
        dsem = {}
        for k in self.dma_counts:
            dsem[k] = stack.enter_context(nc.semaphore("d_" + "_".join(str(x) for x in k)))
        block = stack.enter_context(nc.Block())
        sched = self

        def replay(e, eng):
            waited = {}
            for op in sched.ops[e]:
                for d in op.deps:
                    if d.dma_key is not None:
                        sem, val, sk = dsem[d.dma_key], 16 * d.dma_cnt, ("d", d.dma_key)
                    else:
                        if d.eng == "pe" and e == "pe":
                            continue
                        sem, val, sk = esem[d.eng], d.ticket, ("e", d.eng)
                    if waited.get(sk, 0) >= val:
                        continue
                    waited[sk] = val
                    eng.wait_ge(sem, val)
                ins = op.fn(eng)
                if op.dma_key is not None:
                    ins.then_inc(dsem[op.dma_key], 16)
                elif op.inc:
                    ins.then_inc(esem[e], 1)
            for k in final_waits.get(e, ()):
                eng.wait_ge(dsem[k], 16 * sched.dma_counts[k])

        @block.tensor
        def _(eng):
            replay("pe", eng)

        @block.scalar
        def _(eng):
            replay("act", eng)

        @block.vector
        def _(eng):
            replay("dve", eng)

        @block.gpsimd
        def _(eng):
            replay("pool", eng)

        @block.sync
        def _(eng):
            replay("sp", eng)


def build(nseq=2, phases=("attn", "mlp0", "mlstm", "mlp1", "final")):
    nc = bass.Bass("TRN2", target_bir_lowering=False)
    x_d = nc.dram_tensor("x", [nseq, SEQ, D], F32, kind="ExternalInput").ap()
    out_d = nc.dram_tensor("out", [nseq, SEQ, D], F32, kind="ExternalOutput").ap()
    a_w_in = nc.dram_tensor("a_w_in", [D, 3072], F32, kind="ExternalInput").ap()
    a_w_out = nc.dram_tensor("a_w_out", [D, D], F32, kind="ExternalInput").ap()
    m_w_in = nc.dram_tensor("m_w_in", [D, 3080], F32, kind="ExternalInput").ap()
    m_w_out = nc.dram_tensor("m_w_out", [D, D], F32, kind="ExternalInput").ap()
    w1_d = nc.dram_tensor("w1", [2, D, 4096], F32, kind="ExternalInput").ap()
    w2_d = nc.dram_tensor("w2", [2, 4096, D], F32, kind="ExternalInput").ap()
    cf_d = nc.dram_tensor("cf32", [128, NCF], F32, kind="ExternalInput").ap()
    id_d = nc.dram_tensor("ident", [128, 128], BF16, kind="ExternalInput").ap()
    tt_d = nc.dram_tensor("tt", [128, 8 * 256], F32, kind="ExternalInput").ap()
    gbc_d = nc.dram_tensor("gbc", [5, 128, D], F32, kind="ExternalInput").ap()
    hw_d = nc.dram_tensor("hwbc", [128, D], F32, kind="ExternalInput").ap()
    dbg_d = nc.dram_tensor("dbg", [128, 8 * SEQ], BF16, kind="ExternalOutput").ap() if "dbg_hnT" in phases else None
    dbgh = [int(p[8:]) for p in phases if p.startswith("dbg_otok")]
    dbo_d = nc.dram_tensor("dbo", [128, NT * 128], BF16, kind="ExternalOutput").ap() if dbgh else None
    dbq_d = nc.dram_tensor("dbq", [128, 3 * SEQ], BF16, kind="ExternalOutput").ap() if dbgh else None

    S = Sched()
    stack = ExitStack()
    sb = lambda name, shape, dt: stack.enter_context(nc.sbuf_tensor(name, shape, dt))
    X = sb("X", [128, NT, D], F32)
    hnT = sb("hnT", [128, 8, SEQ], BF16)
    WS = sb("WS", [128, 4, 4096], BF16)
    ARENA_B = 60 * 1024
    arena = sb("arena", [128, ARENA_B // 2], BF16)
    CF = sb("CF", [128, NCF], F32)
    ident = sb("identsb", [128, 128], BF16)
    xn = sb("xn", [128, 2, D], BF16)
    junk = sb("junk", [128, D], BF16)
    gbc = sb("gbcs", [128, 2, D], F32)
    ssq = sb("ssq", [128, NT], F32)
    rstd = sb("rstd", [128, NT], F32)
    lnt = sb("lnt", [128, NT], F32)
    PS = [stack.enter_context(nc.psum_tensor(f"ps{i}", [128, 512], F32)) for i in range(8)]

    def carve(off, dt, *dims):
        n = 1
        for d_ in dims:
            n *= d_
        nb = n * (4 if dt == F32 else 2)
        assert off % 4 == 0 and off + nb <= ARENA_B, (off, nb)
        ap = arena[:, off // 2:(off + nb) // 2]
        if dt == F32:
            ap = ap.bitcast(F32)
        if len(dims) == 2:
            ap = ap.rearrange("p (a b) -> p a b", a=dims[0])
        elif len(dims) == 3:
            ap = ap.rearrange("p (a b c) -> p a b c", a=dims[0], b=dims[1])
        return ap, off + ((nb + 3) // 4) * 4

    PE = lambda fn, r=(), w=(): S.add("pe", fn, r, w)
    ACT = lambda fn, r=(), w=(): S.add("act", fn, r, w)
    DVE = lambda fn, r=(), w=(): S.add("dve", fn, r, w)

    def SPDMA(out, in_, r=(), w=(), key=None):
        return S.add("sp", lambda e: e.dma_start(out=out, in_=in_), r, w, dma_key=key)

    cfc = lambda c0, n: CF[:, c0:c0 + n]

    plan = []
    for s in range(nseq):
        if "attn" in phases:
            for h in range(8):
                plan.append(("a_qkv", h))
                if h % 2 == 1:
                    plan.append(("a_wo", h // 2))
        if "mlp0" in phases:
            for g in range(8):
                plan.append(("w1", 0, g))
                plan.append(("w2", 0, g))
        if "mlstm" in phases:
            plan.append(("m_g",))
            for hd in range(4):
                plan.append(("m_qk", hd))
                plan.append(("m_v", hd))
                plan.append(("m_o", hd))
                plan.append(("m_wo", hd))
        if "mlp1" in phases:
            for g in range(8):
                plan.append(("w1", 1, g))
                plan.append(("w2", 1, g))

    wstate = {"issued": 0, "next": 0}

    def w_issue(i):
        tag = plan[i]
        slot = i % 4
        dst = WS[:, slot, :]
        kind = tag[0]
        parts = []
        if kind == "a_qkv":
            h = tag[1]
            for t_ in range(3):
                src = a_w_in[:, t_ * 1024 + h * 128:t_ * 1024 + (h + 1) * 128].rearrange("(c p) n -> p c n", p=128)
                o = dst[:, 0:8 * 384].rearrange("p (c t n) -> p c t n", c=8, t=3)[:, :, t_, :]
                parts.append((o, src))
        elif kind == "a_wo":
            hp = tag[1]
            src = a_w_out[hp * 256:(hp + 1) * 256, :].rearrange("(c p) n -> p c n", p=128)
            parts.append((dst[:, 0:2048].rearrange("p (c n) -> p c n", c=2), src))
        elif kind == "w1":
            _, l, g = tag
            src = w1_d[l, :, g * 512:(g + 1) * 512].rearrange("(c p) n -> p c n", p=128)
            parts.append((dst[:, 0:4096].rearrange("p (c n) -> p c n", c=8), src))
        elif kind == "w2":
            _, l, g = tag
            src = w2_d[l, g * 512:(g + 1) * 512, :].rearrange("(c p) n -> p c n", p=128)
            parts.append((dst[:, 0:4096].rearrange("p (c n) -> p c n", c=4), src))
        elif kind == "m_g":
            src = m_w_in[:, 3072:3080].rearrange("(c p) n -> p c n", p=128)
            parts.append((dst[:, 0:64].rearrange("p (c n) -> p c n", c=8), src))
        elif kind == "m_qk":
            hd = tag[1]
            for t_ in range(2):
                src = m_w_in[:, t_ * 512 + hd * 128:t_ * 512 + (hd + 1) * 128].rearrange("(c p) n -> p c n", p=128)
                o = dst[:, 0:2048].rearrange("p (c t n) -> p c t n", c=8, t=2)[:, :, t_, :]
                parts.append((o, src))
        elif kind == "m_v":
            hd = tag[1]
            src = m_w_in[:, 1024 + hd * 256:1024 + (hd + 1) * 256].rearrange("(c p) n -> p c n", p=128)
            parts.append((dst[:, 0:2048].rearrange("p (c n) -> p c n", c=8), src))
        elif kind == "m_o":
            hd = tag[1]
            src = m_w_in[:, 2048 + hd * 256:2048 + (hd + 1) * 256].rearrange("(c p) n -> p c n", p=128)
            parts.append((dst[:, 0:2048].rearrange("p (c n) -> p c n", c=8), src))
        elif kind == "m_wo":
            hd = tag[1]
            src = m_w_out[hd * 256:(hd + 1) * 256, :].rearrange("(c p) n -> p c n", p=128)
            parts.append((dst[:, 0:2048].rearrange("p (c n) -> p c n", c=2), src))
        else:
            raise ValueError(kind)
        snap = S.snapshot([("ws", slot, p_) for p_ in range(3)])
        for p_, (o, src) in enumerate(parts):
            S.add("pool", lambda e, o=o, src=src: e.dma_start(out=o, in_=src), (), [("ws", slot, p_)],
                  dma_key=("ws", slot, p_), after=snap)

    def w_next(tag):
        i = wstate["next"]
        assert plan[i] == tag, (plan[i], tag)
        while wstate["issued"] < min(len(plan), i + 3):
            w_issue(wstate["issued"])
            wstate["issued"] += 1
        wstate["next"] = i + 1
        slot = i % 4
        return WS[:, slot, :], ("ws", slot, 0)

    SPDMA(CF[:], cf_d, w=["CF"], key=("c", 0))
    SPDMA(ident[:], id_d, w=["ident"], key=("c", 1))
    gbstate = {"n": 0}

    def load_gbc(idx):
        b = gbstate["n"] % 2
        gbstate["n"] += 1
        SPDMA(gbc[:, b, :], gbc_d[idx], w=[("gbc", b)], key=("g", b))
        return b

    def emit_stats():
        for t in range(NT):
            ACT(lambda e, t=t: e.activation(out=junk[:], in_=X[:, t, :], func=AF.Square,
                                            accum_out=ssq[:, t:t + 1]),
                r=[("X", t)], w=[("ssq", t)])
        ACT(lambda e: e.activation(out=lnt[:], in_=ssq[:], func=AF.Ln, scale=1.0 / D, bias=cfc(C_EPS, 1)),
            r=[("ssq", t) for t in range(NT)] + ["CF"], w=["lnt"])
        ACT(lambda e: e.activation(out=rstd[:], in_=lnt[:], func=AF.Exp, scale=-0.5),
            r=["lnt"], w=["rstd"])

    def emit_norm(gidx):
        gb = load_gbc(gidx)
        emit_stats()
        for t in range(NT):
            b = t % 2
            DVE(lambda e, t=t, b=b: e.scalar_tensor_tensor(out=xn[:, b, :], in0=X[:, t, :],
                                                           scalar=rstd[:, t:t + 1], in1=gbc[:, gb, :],
                                                           op0=ALU.mult, op1=ALU.mult),
                r=[("X", t), "rstd", ("gbc", gb)], w=[("xn", b)])
            bank = 6 + (t % 2)
            pv = PS[bank][:].bitcast(BF16)
            for c in range(8):
                PE(lambda e, c=c, b=b, pv=pv: e.transpose(out=pv[:, c * 128:(c + 1) * 128],
                                                          in_=xn[:, b, c * 128:(c + 1) * 128],
                                                          identity=ident[:]),
                   r=[("xn", b), "ident"], w=[("ps", bank)])
            o = hnT[:, :, t * 128:(t + 1) * 128]
            i_ = pv.rearrange("p (c n) -> p c n", c=8)
            if t % 2 == 0:
                ACT(lambda e, o=o, i_=i_: e.activation(out=o, in_=i_, func=AF.Copy),
                    r=[("ps", bank)], w=[("hnT", t)])
            else:
                DVE(lambda e, o=o, i_=i_: e.tensor_copy(out=o, in_=i_),
                    r=[("ps", bank)], w=[("hnT", t)])

    HN_ALL = [("hnT", t) for t in range(NT)]

    def emit_mlp(layer):
        S.barrier()
        off = 0
        h1T, off = carve(off, BF16, 2, 4, SEQ)
        rtmp, off = carve(off, F32, 2, 512)
        cnt = {"a": 0, "b": 0, "r": 0}

        def g1(g, tq, w1v, wk1):
            hb = g % 2
            for c in range(4):
                bank = cnt["a"] % 4
                cnt["a"] += 1
                for d_ in range(8):
                    PE(lambda e, bank=bank, c=c, d_=d_: e.matmul(
                        PS[bank][:], lhsT=w1v[:, d_, c * 128:(c + 1) * 128],
                        rhs=hnT[:, d_, tq * 512:(tq + 1) * 512], start=(d_ == 0), stop=(d_ == 7)),
                       r=[wk1] + [("hnT", t) for t in range(tq * 4, tq * 4 + 4)], w=[("ps", bank)])
                rb = cnt["r"] % 2
                cnt["r"] += 1
                ACT(lambda e, bank=bank, rb=rb: e.activation(out=rtmp[:, rb, :], in_=PS[bank][:], func=AF.Relu),
                    r=[("ps", bank)], w=[("rtmp", rb)])
                ACT(lambda e, rb=rb, c=c, hb=hb: e.activation(out=h1T[:, hb, c, tq * 512:(tq + 1) * 512],
                                                              in_=rtmp[:, rb, :], func=AF.Square),
                    r=[("rtmp", rb)], w=[("h1T", hb, c, tq)])

        def g2(g, tq, w2v, wk2):
            hb = g % 2
            for t in range(tq * 4, tq * 4 + 4):
                for n in range(2):
                    bank = 4 + cnt["b"] % 4
                    cnt["b"] += 1
                    for c in range(4):
                        PE(lambda e, bank=bank, c=c, t=t, n=n: e.matmul(
                            PS[bank][:], lhsT=h1T[:, hb, c, t * 128:(t + 1) * 128],
                            rhs=w2v[:, c, n * 512:(n + 1) * 512], start=(c == 0), stop=(c == 3)),
                           r=[wk2, ("h1T", hb, c, tq)], w=[("ps", bank)])
                    DVE(lambda e, bank=bank, t=t, n=n: e.tensor_tensor(
                        out=X[:, t, n * 512:(n + 1) * 512], in0=PS[bank][:],
                        in1=X[:, t, n * 512:(n + 1) * 512], op=ALU.add),
                        r=[("ps", bank), ("X", t)], w=[("X", t)])

        for g in range(8):
            w1s, wk1 = w_next(("w1", layer, g))
            w2s, wk2 = w_next(("w2", layer, g))
            w1v = w1s[:, 0:4096].rearrange("p (c n) -> p c n", c=8)
            w2v = w2s[:, 0:4096].rearrange("p (c n) -> p c n", c=4)
            g1(g, 0, w1v, wk1)
            for tq in range(1, 4):
                g1(g, tq, w1v, wk1)
                g2(g, tq - 1, w2v, wk2)
            g2(g, 3, w2v, wk2)

    def emit_attn():
        S.barrier()
        off = 0
        TT, off = carve(off, F32, 8, 256)
        qTm, off = carve(off, BF16, 2, SEQ)
        kT, off = carve(off, BF16, SEQ)
        V, off = carve(off, BF16, NT, 130)
        PT, off = carve(off, BF16, 6, 512)
        osb, off = carve(off, F32, 2, 4, 128)
        t1sb, off = carve(off, F32, 4, 128)
        otok, off = carve(off, BF16, NT, 128)
        oTp, off = carve(off, BF16, 2, SEQ)
        subw, off = carve(off, F32, 128)
        sm, off = carve(off, F32, 64)
        lqk, off = carve(off, F32, 2, 64)
        accsb, off = carve(off, F32, 2, 1032)
        SPDMA(TT.rearrange("p a b -> p (a b)"), tt_d, w=["TT"], key=("c", 2))
        for k in range(2):
            DVE(lambda e, k=k: e.tensor_tensor(out=lqk[:, k, :], in0=cfc(C_LQK + 128 * k, 64),
                                               in1=cfc(C_LQK + 128 * k + 64, 64), op=ALU.mult),
                r=["CF"], w=[("lqk", k)])
            DVE(lambda e, k=k: e.reduce_sum(out=sm[:, k:k + 1], in_=lqk[:, k, :], axis=AX.X),
                r=[("lqk", k)], w=[("sm", k)])
        ACT(lambda e: e.activation(out=sm[:, 2:4], in_=sm[:, 0:2], func=AF.Exp),
            r=[("sm", 0), ("sm", 1)], w=[("sm", 2)])
        DVE(lambda e: e.scalar_tensor_tensor(out=sm[:, 4:5], in0=sm[:, 2:3], scalar=float(LAMBDA_INIT),
                                             in1=sm[:, 3:4], op0=ALU.add, op1=ALU.subtract),
            r=[("sm", 2)], w=["lam"])
        DVE(lambda e: e.tensor_scalar(out=subw, in0=cfc(C_SUBW, 128), scalar1=float(1.0 - LAMBDA_INIT),
                                      scalar2=None, op0=ALU.mult),
            r=["CF"], w=["subw"])
        DVE(lambda e: e.memset(V[:, :, 128:130], 1.0), w=["Vones"])
        for h_ in range(8):
            DVE(lambda e, h_=h_: e.tensor_scalar(out=TT[:, h_, :], in0=TT[:, h_, :], scalar1=cfc(C_CB + h_, 1),
                                                 scalar2=None, op0=ALU.subtract),
                r=["TT", "CF"], w=["TT"])
        DVE(lambda e: e.memset(qTm[64:128, 0, :], 0.0), w=["qz0"])
        DVE(lambda e: e.memset(qTm[0:64, 1, :], 0.0), w=["qz1"])
        lam = sm[:, 4:5]
        cnt = {"st": 0, "pt": 0, "nt": 0, "pj": 0, "ab": 0}
        STB = (0, 1, 2, 6, 7)
        ACCA = (3, 4)
        ACCB = 5

        def acc(m, qs):
            if qs < 3:
                return PS[ACCA[m]][:, qs * 129:(qs + 1) * 129], ("ps", ACCA[m])
            return PS[ACCB][:, m * 129:(m + 1) * 129], ("ps", ACCB)

        def attn_head(h):
            wsl, wk = w_next(("a_qkv", h))
            Wv_ = wsl[:, 0:8 * 384].rearrange("p (c t n) -> p c t n", c=8, t=3)
            for which in (0, 1):
                for tq in range(4):
                    bank = 6 + cnt["pj"] % 2
                    cnt["pj"] += 1
                    for d_ in range(8):
                        PE(lambda e, bank=bank, d_=d_, which=which, tq=tq: e.matmul(
                            PS[bank][:], lhsT=Wv_[:, d_, which, :], rhs=hnT[:, d_, tq * 512:(tq + 1) * 512],
                            start=(d_ == 0), stop=(d_ == 7)),
                           r=[(wk[0], wk[1], which)] + [("hnT", t) for t in range(tq * 4, tq * 4 + 4)], w=[("ps", bank)])
                    if which == 0:
                        DVE(lambda e, bank=bank, tq=tq: e.tensor_scalar(
                            out=qTm[0:64, 0, tq * 512:(tq + 1) * 512], in0=PS[bank][0:64, :], scalar1=0.125,
                            scalar2=None, op0=ALU.mult),
                            r=[("ps", bank)], w=[("qk", 0)])
                        DVE(lambda e, bank=bank, tq=tq: e.tensor_scalar(
                            out=qTm[64:128, 1, tq * 512:(tq + 1) * 512], in0=PS[bank][64:128, :], scalar1=0.125,
                            scalar2=None, op0=ALU.mult),
                            r=[("ps", bank), ("qk", 0)], w=[("qk", 0)])
                    else:
                        DVE(lambda e, bank=bank, tq=tq: e.tensor_copy(
                            out=kT[:, tq * 512:(tq + 1) * 512], in_=PS[bank][:]),
                            r=[("ps", bank)], w=[("qk", 1)])
            for t0 in range(0, NT, 4):
                bank = 6 + cnt["pj"] % 2
                cnt["pj"] += 1
                for tt_ in range(4):
                    t = t0 + tt_
                    for d_ in range(8):
                        PE(lambda e, bank=bank, d_=d_, t=t, tt_=tt_: e.matmul(
                            PS[bank][:, tt_ * 128:(tt_ + 1) * 128], lhsT=hnT[:, d_, t * 128:(t + 1) * 128],
                            rhs=Wv_[:, d_, 2, :], start=(d_ == 0), stop=(d_ == 7)),
                           r=[(wk[0], wk[1], 2), ("hnT", t)], w=[("ps", bank)])
                DVE(lambda e, bank=bank, t0=t0: e.tensor_copy(
                    out=V[:, t0:t0 + 4, 0:128], in_=PS[bank][:].rearrange("p (a b) -> p a b", a=4)),
                    r=[("ps", bank)], w=["V"])
            tiles = [(j, i, m) for j in range(4) for i in range(4 * j + 4) for m in range(2)]
            LA = 4
            info = {}

            def stage_a(n):
                j, i, m = tiles[n]
                qb0 = max(4 * j, i)
                q0 = qb0 * 128
                N = (4 * j + 4) * 128 - q0
                d0 = qb0 - i
                bank = STB[cnt["st"] % 5]
                cnt["st"] += 1
                PE(lambda e: e.matmul(
                    PS[bank][:, 0:N], lhsT=kT[:, i * 128:(i + 1) * 128],
                    rhs=qTm[:, m, q0:q0 + N], start=True, stop=True),
                   r=[("qk", 0), ("qk", 1), "qz0", "qz1"], w=[("ps", bank)])
                pb = cnt["pt"] % 6
                cnt["pt"] += 1
                if d0 < 2:
                    tc0 = 0 if d0 == 0 else 128
                    n1 = min(256 - tc0, N)
                    DVE(lambda e: e.tensor_tensor(
                        out=PS[bank][:, 0:n1], in0=PS[bank][:, 0:n1], in1=TT[:, h, tc0:tc0 + n1], op=ALU.add),
                        r=[("ps", bank), "TT"], w=[("ps", bank)])
                ACT(lambda e: e.activation(out=PT[:, pb, 0:N], in_=PS[bank][:, 0:N], func=AF.Exp),
                    r=[("ps", bank)], w=[("PT", pb)])
                info[n] = (pb, q0, qb0)

            def stage_b(n):
                j, i, m = tiles[n]
                pb, q0, qb0 = info[n]
                for qs in range(qb0 - 4 * j, 4):
                    qb = 4 * j + qs
                    col = qb * 128 - q0
                    a_ap, a_key = acc(m, qs)
                    first = (i == 0) and ((qs == 0) or (qs == 3 and m == 0))
                    PE(lambda e, a_ap=a_ap, col=col, first=first, qb=qb: e.matmul(
                        a_ap, lhsT=PT[:, pb, col:col + 128], rhs=V[:, i, 0:129],
                        start=first, stop=(i == qb), skip_group_check=True),
                       r=[("PT", pb), "V", "Vones"], w=[a_key])
                if i == 4 * j + 3 and m == 1:
                    finalize(j)

            pending = []

            def finalize(j):
                ab = cnt["ab"] % 2
                cnt["ab"] += 1
                DVE(lambda e: e.tensor_copy(out=accsb[:, ab, 0:387], in_=PS[ACCA[0]][:, 0:387]),
                    r=[("ps", ACCA[0])], w=[("accsb", ab, 0)])
                DVE(lambda e: e.tensor_copy(out=accsb[:, ab, 387:774], in_=PS[ACCA[1]][:, 0:387]),
                    r=[("ps", ACCA[1])], w=[("accsb", ab, 1)])
                DVE(lambda e: e.tensor_copy(out=accsb[:, ab, 774:1032], in_=PS[ACCB][:, 0:258]),
                    r=[("ps", ACCB)], w=[("accsb", ab, 2)])
                akeys = [("accsb", ab, k_) for k_ in range(3)]

                def sacc(m, qs):
                    o_ = (m * 387 + qs * 129) if qs < 3 else (774 + m * 129)
                    return accsb[:, ab, o_:o_ + 129]

                sb0 = 8 + ab * 24
                for qs in range(4):
                    a0 = sacc(0, qs)
                    a1 = sacc(1, qs)
                    c0 = sb0 + qs * 3
                    DVE(lambda e, a0=a0, c0=c0: e.reciprocal(out=sm[:, c0:c0 + 1], in_=a0[:, 128:129]),
                        r=akeys, w=[("smq", ab, qs, 0)])
                    DVE(lambda e, a1=a1, c0=c0: e.reciprocal(out=sm[:, c0 + 1:c0 + 2], in_=a1[:, 128:129]),
                        r=akeys, w=[("smq", ab, qs, 1)])
                    DVE(lambda e, c0=c0: e.tensor_tensor(out=sm[:, c0 + 2:c0 + 3], in0=sm[:, c0 + 1:c0 + 2],
                                                         in1=lam, op=ALU.mult),
                        r=[("smq", ab, qs, 1), "lam"], w=[("smq", ab, qs, 2)])
                    DVE(lambda e, a1=a1, c0=c0, qs=qs: e.tensor_scalar(
                        out=t1sb[:, qs, :], in0=a1[:, 0:128], scalar1=sm[:, c0 + 2:c0 + 3], scalar2=None,
                        op0=ALU.mult),
                        r=akeys + [("smq", ab, qs, 2)], w=[("t1", qs)])
                    DVE(lambda e, a0=a0, c0=c0, qs=qs: e.scalar_tensor_tensor(
                        out=osb[:, ab, qs, :], in0=a0[:, 0:128], scalar=sm[:, c0:c0 + 1], in1=t1sb[:, qs, :],
                        op0=ALU.mult, op1=ALU.subtract),
                        r=akeys + [("smq", ab, qs, 0), ("t1", qs)], w=[("osb", ab, qs)])
                    DVE(lambda e, qs=qs: e.scalar_tensor_tensor(
                        out=t1sb[:, qs, :], in0=osb[:, ab, qs, :], scalar=1.0, in1=osb[:, ab, qs, :],
                        op0=ALU.mult, op1=ALU.mult, accum_out=sm[:, sb0 + 12 + qs:sb0 + 13 + qs]),
                        r=[("osb", ab, qs), ("t1", qs)], w=[("t1", qs), ("ss2", ab, qs)])

                def part2():
                    ACT(lambda e: e.activation(out=sm[:, sb0 + 16:sb0 + 20], in_=sm[:, sb0 + 12:sb0 + 16], func=AF.Ln,
                                               scale=1.0 / 128, bias=cfc(C_EPS, 1)),
                        r=[("ss2", ab, q_) for q_ in range(4)] + ["CF"], w=[("ln2", ab)])
                    ACT(lambda e: e.activation(out=sm[:, sb0 + 20:sb0 + 24], in_=sm[:, sb0 + 16:sb0 + 20],
                                               func=AF.Exp, scale=-0.5),
                        r=[("ln2", ab)], w=[("rs2", ab)])
                    for qs in range(4):
                        t = 4 * j + qs
                        DVE(lambda e, qs=qs, t=t: e.scalar_tensor_tensor(
                            out=otok[:, t, :], in0=osb[:, ab, qs, :], scalar=sm[:, sb0 + 20 + qs:sb0 + 21 + qs],
                            in1=subw, op0=ALU.mult, op1=ALU.mult),
                            r=[("osb", ab, qs), ("rs2", ab), "subw"], w=[("otok", t)])

                pending.append([8, part2])

            def tick():
                for p_ in list(pending):
                    p_[0] -= 1
                    if p_[0] <= 0:
                        pending.remove(p_)
                        p_[1]()

            for n in range(len(tiles) + LA):
                if n < len(tiles):
                    stage_a(n)
                if n >= LA:
                    stage_b(n - LA)
                tick()
            while pending:
                tick()
            if dbgh and h == dbgh[0]:
                SPDMA(dbo_d, otok.rearrange("p a b -> p (a b)"), r=[("otok", t) for t in range(NT)], key=("c", 6))
                SPDMA(dbq_d[:, 0:SEQ], qTm[:, 0, :], r=[("qk", 0)], key=("c", 7))
                SPDMA(dbq_d[:, SEQ:2 * SEQ], kT, r=[("qk", 1)], key=("c", 8))
                SPDMA(dbq_d[:, 2 * SEQ:2 * SEQ + NT * 128].rearrange("p (a b) -> p a b", a=NT), V[:, :, 0:128], r=["V"], key=("c", 9))
            hh = h % 2
            for t0 in range(0, NT, 4):
                bank = 6 + cnt["pj"] % 2
                cnt["pj"] += 1
                pv = PS[bank][:].bitcast(BF16)
                for tt_ in range(4):
                    t = t0 + tt_
                    PE(lambda e, pv=pv, t=t, tt_=tt_: e.transpose(out=pv[:, tt_ * 128:(tt_ + 1) * 128],
                                                                  in_=otok[:, t, :], identity=ident[:]),
                       r=[("otok", t), "ident"], w=[("ps", bank)])
                ACT(lambda e, pv=pv, t0=t0, hh=hh: e.activation(out=oTp[:, hh, t0 * 128:(t0 + 4) * 128],
                                                                in_=pv[:, 0:512], func=AF.Copy),
                    r=[("ps", bank)], w=[("oTp", hh, t0)])
            if hh == 1:
                wsl2, wk2 = w_next(("a_wo", h // 2))
                wo = wsl2[:, 0:2048].rearrange("p (c n) -> p c n", c=2)
                for t in range(NT):
                    for n in range(2):
                        bank = 6 + cnt["pj"] % 2
                        cnt["pj"] += 1
                        for c in range(2):
                            PE(lambda e, bank=bank, c=c, t=t, n=n: e.matmul(
                                PS[bank][:], lhsT=oTp[:, c, t * 128:(t + 1) * 128],
                                rhs=wo[:, c, n * 512:(n + 1) * 512], start=(c == 0), stop=(c == 1)),
                               r=[wk2, ("oTp", c, (t // 4) * 4)], w=[("ps", bank)])
                        DVE(lambda e, bank=bank, t=t, n=n: e.tensor_tensor(
                            out=X[:, t, n * 512:(n + 1) * 512], in0=PS[bank][:],
                            in1=X[:, t, n * 512:(n + 1) * 512], op=ALU.add),
                            r=[("ps", bank), ("X", t)], w=[("X", t)])

        for h in range(8):
            attn_head(h)

    def emit_mlstm():
        S.barrier()
        off = 0
        A1_off = off
        raw, off = carve(off, F32, SEQ + 16)
        cacc, off = carve(off, F32, SEQ)
        numsb, _ = carve(A1_off, F32, NT, 257)
        assert NT * 257 * 4 <= off - A1_off
        qT, off = carve(off, BF16, SEQ)
        kT, off = carve(off, BF16, SEQ)
        vaug, off = carve(off, BF16, NT, 258)
        og, off = carve(off, BF16, NT, 256)
        htT, _ = carve(A1_off, BF16, 2, SEQ)
        hwbc, off = carve(off, F32, 256)
        G, off = carve(off, F32, NT, 8)
        nl, off = carve(off, F32, NT, 4)
        fq, off = carve(off, F32, NT, 4)
        fk, off = carve(off, F32, NT, 4)
        gg, off = carve(off, F32, NT, 4)
        eb, off = carve(off, F32, NT, 4)
        tmpa, off = carve(off, F32, NT, 4)
        tmpb, off = carve(off, F32, NT, 4)
        CTf, off = carve(off, F32, 2, 258)
        CTb, off = carve(off, BF16, NT, 258)
        kpp, off = carve(off, BF16, 2, 128)
        scsb, off = carve(off, BF16, 2, 128)
        dd, off = carve(off, F32, NT)
        r2, off = carve(off, F32, NT)
        ss3, off = carve(off, F32, NT)
        ln3, off = carve(off, F32, NT)
        r3, off = carve(off, F32, NT)
        maskT = cfc(C_MASK, 128)
        ones = cfc(C_ONES, 128)
        cnt = {"pj": 0, "k": 0}
        wsl, wk = w_next(("m_g",))
        wg = wsl[:, 0:64].rearrange("p (c n) -> p c n", c=8)
        for t in range(NT):
            for d_ in range(8):
                PE(lambda e, t=t, d_=d_: e.matmul(PS[0][:, t * 8:(t + 1) * 8], lhsT=hnT[:, d_, t * 128:(t + 1) * 128],
                                                  rhs=wg[:, d_, :], start=(d_ == 0), stop=(d_ == 7)),
                   r=[wk, ("hnT", t)], w=[("ps", 0)])
        Gv = PS[0][:, 0:128].rearrange("p (t n) -> p t n", t=NT)
        for t in range(NT):
            pass
        DVE(lambda e: e.tensor_tensor(out=G, in0=Gv, in1=cfc(C_GB, 8).unsqueeze(1).to_broadcast([128, NT, 8]),
                                      op=ALU.add),
            r=[("ps", 0), "CF"], w=["G"])
        ACT(lambda e: e.activation(out=tmpa, in_=G[:, :, 4:8], func=AF.Exp, scale=-1.0), r=["G"], w=["tmpa"])
        ACT(lambda e: e.activation(out=nl, in_=tmpa, func=AF.Ln, bias=cfc(C_ONE1, 1)), r=["tmpa", "CF"], w=["nl"])
        nlf = nl.rearrange("p t h -> p (t h)")
        PE(lambda e: e.matmul(PS[1][:, 0:64], lhsT=maskT, rhs=nlf, start=True, stop=True),
           r=["nl", "CF"], w=[("ps", 1)])
        PE(lambda e: e.matmul(PS[1][:, 64:128], lhsT=ones, rhs=nlf, start=True, stop=True),
           r=["nl", "CF"], w=[("ps", 1)])
        cum = PS[1][:, 0:64].rearrange("p (t h) -> p t h", t=NT)
        tot = PS[1][:, 64:128].rearrange("p (t h) -> p t h", t=NT)
        ACT(lambda e: e.activation(out=fq, in_=cum, func=AF.Exp, scale=-1.0), r=[("ps", 1)], w=["fq0"])
        DVE(lambda e: e.tensor_scalar(out=fq, in0=fq, scalar1=float(128 ** -0.5), scalar2=None, op0=ALU.mult),
            r=["fq0"], w=["fq"])
        DVE(lambda e: e.tensor_tensor(out=tmpa, in0=cum, in1=G[:, :, 0:4], op=ALU.add),
            r=[("ps", 1), "G", "nl"], w=["tmpa2"])
        ACT(lambda e: e.activation(out=fk, in_=tmpa, func=AF.Exp), r=["tmpa2"], w=["fk"])
        DVE(lambda e: e.tensor_tensor(out=tmpb, in0=tmpa, in1=tot, op=ALU.subtract),
            r=["tmpa2", ("ps", 1)], w=["tmpb"])
        ACT(lambda e: e.activation(out=gg, in_=tmpb, func=AF.Exp), r=["tmpb"], w=["gg"])
        ACT(lambda e: e.activation(out=eb, in_=tot, func=AF.Exp, scale=-1.0), r=[("ps", 1)], w=["eb"])
        DVE(lambda e: e.memset(vaug[:, :, 256:258], 1.0), w=["vones"])

        def m_head(hd):
            wqk_s, wkq = w_next(("m_qk", hd))
            wqk = wqk_s[:, 0:2048].rearrange("p (c t n) -> p c t n", c=8, t=2)
            S.add("sp", lambda e, hd=hd: e.dma_start(out=hwbc, in_=hw_d[:, hd * 256:(hd + 1) * 256]),
                  (), ["hwbc"], dma_key=("c", 3))
            DVE(lambda e: e.memset(raw[:, 0:3], 0.0), r=[], w=["A1", "rawz"])
            for which, dst in ((0, qT), (1, kT)):
                ch = which * 4 + hd
                for tq in range(4):
                    bank = 6 + cnt["pj"] % 2
                    cnt["pj"] += 1
                    for d_ in range(8):
                        PE(lambda e, bank=bank, d_=d_, which=which, tq=tq: e.matmul(
                            PS[bank][:], lhsT=wqk[:, d_, which, :], rhs=hnT[:, d_, tq * 512:(tq + 1) * 512],
                            start=(d_ == 0), stop=(d_ == 7)),
                           r=[(wkq[0], wkq[1], which)] + [("hnT", t) for t in range(tq * 4, tq * 4 + 4)], w=[("ps", bank)])
                    ACT(lambda e, bank=bank, tq=tq: e.activation(
                        out=raw[:, 3 + tq * 512:3 + (tq + 1) * 512], in_=PS[bank][:], func=AF.Copy),
                        r=[("ps", bank), "rawz"], w=["A1"])
                cw = lambda k_, ch=ch: cfc(C_CW + ch * 4 + k_, 1)
                DVE(lambda e, cw=cw: e.tensor_scalar(out=cacc, in0=raw[:, 3:3 + SEQ], scalar1=cw(3), scalar2=None,
                                                     op0=ALU.mult),
                    r=["A1", "rawz", "CF"], w=["cacc"])
                for k_ in (2, 1, 0):
                    DVE(lambda e, cw=cw, k_=k_: e.scalar_tensor_tensor(
                        out=cacc, in0=raw[:, k_:k_ + SEQ], scalar=cw(k_), in1=cacc, op0=ALU.mult, op1=ALU.add),
                        r=["A1", "rawz", "cacc"], w=["cacc"])
                ACT(lambda e, dst=dst, ch=ch: e.activation(out=dst, in_=cacc, func=AF.Silu,
                                                           bias=cfc(C_CBIAS + ch, 1)),
                    r=["cacc", "CF"], w=[("mqk", which)])
            wv_s, wkv = w_next(("m_v", hd))
            wv = wv_s[:, 0:2048].rearrange("p (c n) -> p c n", c=8)
            for t0 in range(0, NT, 2):
                bank = 6 + cnt["pj"] % 2
                cnt["pj"] += 1
                for tt_ in range(2):
                    t = t0 + tt_
                    for d_ in range(8):
                        PE(lambda e, bank=bank, d_=d_, t=t, tt_=tt_: e.matmul(
                            PS[bank][:, tt_ * 256:(tt_ + 1) * 256], lhsT=hnT[:, d_, t * 128:(t + 1) * 128],
                            rhs=wv[:, d_, :], start=(d_ == 0), stop=(d_ == 7)),
                           r=[wkv, ("hnT", t)], w=[("ps", bank)])
                DVE(lambda e, bank=bank, t0=t0: e.tensor_copy(
                    out=vaug[:, t0:t0 + 2, 0:256], in_=PS[bank][:].rearrange("p (a b) -> p a b", a=2)),
                    r=[("ps", bank)], w=["vaug"])
            wo_s, wko = w_next(("m_o", hd))
            wo_ = wo_s[:, 0:2048].rearrange("p (c n) -> p c n", c=8)
            for t0 in range(0, NT, 2):
                bank = 6 + cnt["pj"] % 2
                cnt["pj"] += 1
                for tt_ in range(2):
                    t = t0 + tt_
                    for d_ in range(8):
                        PE(lambda e, bank=bank, d_=d_, t=t, tt_=tt_: e.matmul(
                            PS[bank][:, tt_ * 256:(tt_ + 1) * 256], lhsT=hnT[:, d_, t * 128:(t + 1) * 128],
                            rhs=wo_[:, d_, :], start=(d_ == 0), stop=(d_ == 7)),
                           r=[wko, ("hnT", t)], w=[("ps", bank)])
                ACT(lambda e, bank=bank, t0=t0: e.activation(
                    out=og[:, t0:t0 + 2, :], in_=PS[bank][:].rearrange("p (a b) -> p a b", a=2), func=AF.Sigmoid),
                    r=[("ps", bank)], w=[("og", t0)])
            DK = [("dsb", c) for c in range(NT)]
            for c in range(NT - 1):
                cs = slice(c * 128, (c + 1) * 128)
                sb_ = cnt["k"] % 2
                cnt["k"] += 1
                tb = 2 + sb_
                db = 4 + sb_
                pvb = PS[tb][:].bitcast(BF16)
                PE(lambda e, cs=cs, pvb=pvb: e.transpose(out=pvb[:, 0:128], in_=kT[:, cs], identity=ident[:]),
                   r=[("mqk", 1), "ident"], w=[("ps", tb)])
                ACT(lambda e, c=c, sb_=sb_, pvb=pvb: e.activation(
                    out=kpp[:, sb_, :], in_=pvb[:, 0:128], func=AF.Copy, scale=gg[:, c, hd:hd + 1]),
                    r=[("ps", tb), "gg"], w=[("kpp", sb_)])
                PE(lambda e, c=c, sb_=sb_, db=db: e.matmul(PS[db][:, 0:257], lhsT=kpp[:, sb_, :],
                                                            rhs=vaug[:, c, 0:257], start=True, stop=True),
                   r=[("kpp", sb_), "vaug", "vones"], w=[("ps", db)])
                S.add("dve", lambda e, c=c, db=db: e.tensor_copy(out=numsb[:, c, :], in_=PS[db][:, 0:257]),
                      [("ps", db), "A1"], [("dsb", c)], war=["A1"])
            ACT(lambda e: e.activation(out=CTb[:, 1, 0:257], in_=numsb[:, 0, :], func=AF.Copy),
                r=[("dsb", 0)], w=[("CTb", 1)])
            for c in range(1, NT - 1):
                src = numsb[:, 0, :] if c == 1 else CTf[:, (c - 1) % 2, 0:257]
                skey = ("dsb", 0) if c == 1 else ("CTf", (c - 1) % 2)
                DVE(lambda e, c=c, src=src: e.scalar_tensor_tensor(
                    out=CTf[:, c % 2, 0:257], in0=src, scalar=eb[:, c, hd:hd + 1], in1=numsb[:, c, :],
                    op0=ALU.mult, op1=ALU.add),
                    r=[skey, ("dsb", c), "eb"], w=[("CTf", c % 2)])
                ACT(lambda e, c=c: e.activation(out=CTb[:, c + 1, 0:257], in_=CTf[:, c % 2, 0:257], func=AF.Copy),
                    r=[("CTf", c % 2)], w=[("CTb", c + 1)])
            for c in range(NT):
                cs = slice(c * 128, (c + 1) * 128)
                sb_ = c % 2
                stb = 2 + sb_
                nb_ = 4 + sb_
                PE(lambda e, cs=cs, stb=stb: e.matmul(PS[stb][:, 0:128], lhsT=kT[:, cs], rhs=qT[:, cs],
                                                      start=True, stop=True),
                   r=[("mqk", 0), ("mqk", 1)], w=[("ps", stb)])
                DVE(lambda e, c=c, sb_=sb_, stb=stb: e.scalar_tensor_tensor(
                    out=scsb[:, sb_, :], in0=PS[stb][:, 0:128], scalar=fk[:, c, hd:hd + 1], in1=maskT,
                    op0=ALU.mult, op1=ALU.mult),
                    r=[("ps", stb), "fk", "CF"], w=[("scsb", sb_)])
                PE(lambda e, c=c, sb_=sb_, nb_=nb_: e.matmul(PS[nb_][:, 0:257], lhsT=scsb[:, sb_, :],
                                                              rhs=vaug[:, c, 0:257], start=True, stop=(c == 0)),
                   r=[("scsb", sb_), "vaug", "vones"], w=[("ps", nb_)])
                if c > 0:
                    PE(lambda e, c=c, cs=cs, nb_=nb_: e.matmul(PS[nb_][:, 0:257], lhsT=qT[:, cs],
                                                                rhs=CTb[:, c, 0:257], start=False, stop=True),
                       r=[("mqk", 0), ("CTb", c)], w=[("ps", nb_)])
                ACT(lambda e, c=c, nb_=nb_: e.activation(out=numsb[:, c, :], in_=PS[nb_][:, 0:257], func=AF.Copy),
                    r=[("ps", nb_), "A1"], w=[("dsb", c)])
            den = numsb[:, :, 256]
            DVE(lambda e, hd=hd: e.tensor_tensor(out=dd, in0=den, in1=fq[:, :, hd], op=ALU.mult),
                r=DK + ["A1", "fq"], w=["dd"])
            DVE(lambda e: e.scalar_tensor_tensor(out=dd, in0=dd, scalar=-1.0, in1=dd, op0=ALU.mult, op1=ALU.max),
                r=["dd"], w=["dd"])
            DVE(lambda e: e.tensor_scalar(out=dd, in0=dd, scalar1=1.0, scalar2=None, op0=ALU.max),
                r=["dd"], w=["dd"])
            DVE(lambda e: e.reciprocal(out=dd, in_=dd), r=["dd"], w=["dd"])
            DVE(lambda e, hd=hd: e.tensor_tensor(out=r2, in0=dd, in1=fq[:, :, hd], op=ALU.mult),
                r=["dd", "fq"], w=["r2"])
            for c in range(NT):
                ACT(lambda e, c=c: e.activation(out=junk[:, 0:256], in_=numsb[:, c, 0:256], func=AF.Square,
                                                scale=r2[:, c:c + 1], accum_out=ss3[:, c:c + 1]),
                    r=[("dsb", c), "A1", "r2"], w=[("ss3", c)])
            ACT(lambda e: e.activation(out=ln3, in_=ss3, func=AF.Ln, scale=1.0 / 256, bias=cfc(C_EPS, 1)),
                r=[("ss3", c) for c in range(NT)] + ["CF"], w=["ln3"])
            ACT(lambda e: e.activation(out=r3, in_=ln3, func=AF.Exp, scale=-0.5), r=["ln3"], w=["r3a"])
            DVE(lambda e: e.tensor_tensor(out=r3, in0=r3, in1=r2, op=ALU.mult), r=["r3a", "r2"], w=["r3"])
            for c in range(NT):
                DVE(lambda e, c=c: e.scalar_tensor_tensor(
                    out=numsb[:, c, 0:256], in0=numsb[:, c, 0:256], scalar=r3[:, c:c + 1], in1=hwbc,
                    op0=ALU.mult, op1=ALU.mult),
                    r=[("dsb", c), "A1", "r3", "hwbc"], w=[("hn", c), ("dsb", c)])
                DVE(lambda e, c=c: e.tensor_tensor(out=og[:, c, :], in0=numsb[:, c, 0:256], in1=og[:, c, :],
                                                   op=ALU.mult),
                    r=[("hn", c), ("dsb", c), "A1", ("og", (c // 2) * 2)], w=[("gated", c)])
            for cc in range(2):
                for t0 in range(0, NT, 4):
                    bank = 6 + cnt["pj"] % 2
                    cnt["pj"] += 1
                    pv = PS[bank][:].bitcast(BF16)
                    for tt_ in range(4):
                        t = t0 + tt_
                        PE(lambda e, pv=pv, t=t, tt_=tt_, cc=cc: e.transpose(
                            out=pv[:, tt_ * 128:(tt_ + 1) * 128], in_=og[:, t, cc * 128:(cc + 1) * 128],
                            identity=ident[:]),
                           r=[("gated", t), "ident"], w=[("ps", bank)])
                    ACT(lambda e, pv=pv, t0=t0, cc=cc: e.activation(out=htT[:, cc, t0 * 128:(t0 + 4) * 128],
                                                                    in_=pv[:, 0:512], func=AF.Copy),
                        r=[("ps", bank), "A1"] + [("gated", t_) for t_ in range(NT)], w=[("htT", cc, t0)] + DK)
            wout_s, wkout = w_next(("m_wo", hd))
            wout = wout_s[:, 0:2048].rearrange("p (c n) -> p c n", c=2)
            for t in range(NT):
                for n in range(2):
                    bank = 6 + cnt["pj"] % 2
                    cnt["pj"] += 1
                    for cc in range(2):
                        PE(lambda e, bank=bank, cc=cc, t=t, n=n: e.matmul(
                            PS[bank][:], lhsT=htT[:, cc, t * 128:(t + 1) * 128],
                            rhs=wout[:, cc, n * 512:(n + 1) * 512], start=(cc == 0), stop=(cc == 1)),
                           r=[wkout, ("htT", cc, (t // 4) * 4), "A1"], w=[("ps", bank)])
                    DVE(lambda e, bank=bank, t=t, n=n: e.tensor_tensor(
                        out=X[:, t, n * 512:(n + 1) * 512], in0=PS[bank][:],
                        in1=X[:, t, n * 512:(n + 1) * 512], op=ALU.add),
                        r=[("ps", bank), ("X", t)], w=[("X", t)])

        for hd in range(4):
            m_head(hd)

    def emit_final(s, do_norm):
        if do_norm:
            gb = load_gbc(4)
            emit_stats()
            for t in range(NT):
                DVE(lambda e, t=t: e.scalar_tensor_tensor(out=X[:, t, :], in0=X[:, t, :], scalar=rstd[:, t:t + 1],
                                                          in1=gbc[:, gb, :], op0=ALU.mult, op1=ALU.mult),
                    r=[("X", t), "rstd", ("gbc", gb)], w=[("X", t)])
        for t0 in range(0, NT, 4):
            SPDMA(out_d[s, t0 * 128:(t0 + 4) * 128, :].rearrange("(t p) d -> p t d", p=128), X[:, t0:t0 + 4, :],
                  r=[("X", t) for t in range(t0, t0 + 4)], key=("o", t0 // 4))

    for s in range(nseq):
        for t0 in range(0, NT, 4):
            SPDMA(X[:, t0:t0 + 4, :], x_d[s, t0 * 128:(t0 + 4) * 128, :].rearrange("(t p) d -> p t d", p=128),
                  w=[("X", t) for t in range(t0, t0 + 4)], key=("x", t0 // 4))
        if "attn" in phases:
            emit_norm(0)
            emit_attn()
        if "mlp0" in phases:
            emit_norm(1)
            emit_mlp(0)
        if "mlstm" in phases:
            emit_norm(2)
            emit_mlstm()
        if "mlp1" in phases:
            emit_norm(3)
            emit_mlp(1)
        if "dbg_hnT" in phases:
            emit_norm(1)
            SPDMA(dbg_d, hnT[:].rearrange("p a b -> p (a b)"), r=HN_ALL, key=("c", 5))
        emit_final(s, "final" in phases)

    dl = S.simulate()
    assert dl is None, f"deadlock in schedule: {dl}"
    S.finalize(nc, stack, {"sp": [("o", i) for i in range(4)] + ([("c", 5)] if "dbg_hnT" in phases else []) + ([("c", 6), ("c", 7), ("c", 8), ("c", 9)] if dbgh else [])})
    stack.close()
    return nc


def _t5_bucket_table():
    d = np.arange(256)
    max_exact = 16
    dd = np.maximum(d, 1).astype(np.float32)
    large = max_exact + (np.log(dd / max_exact) / math.log(128 / max_exact) * (32 - max_exact)).astype(np.int32)
    large = np.minimum(large, 31)
    return np.where(d < max_exact, d, large)


def host_consts(inp):
    f32 = np.float32
    cf = np.zeros((128, NCF), f32)
    s_idx = np.arange(128)[:, None]
    j_idx = np.arange(128)[None, :]
    cf[:, C_MASK:C_MASK + 128] = (s_idx <= j_idx).astype(f32)
    cf[:, C_ONES:C_ONES + 128] = 1.0
    rel = np.asarray(inp["rel_bias"], f32)
    cf[:, C_CB:C_CB + 8] = rel[31][None, :]
    for k, nm in enumerate(("attn_lambda_q1", "attn_lambda_k1", "attn_lambda_q2", "attn_lambda_k2")):
        cf[:, C_LQK + 64 * k:C_LQK + 64 * (k + 1)] = np.asarray(inp[nm], f32)[0][None, :]
    cf[:, C_SUBW:C_SUBW + 128] = np.asarray(inp["attn_subln"], f32)[0][None, :]
    cf[:, C_GB:C_GB + 4] = np.asarray(inp["mlstm_b_i"], f32)[0][None, :]
    cf[:, C_GB + 4:C_GB + 8] = np.asarray(inp["mlstm_b_f"], f32)[0][None, :]
    cw = np.asarray(inp["mlstm_conv_w"], f32)[0]
    cf[:, C_CW:C_CW + 32] = cw.reshape(4, 8, 128).transpose(2, 1, 0).reshape(128, 32)
    cf[:, C_CBIAS:C_CBIAS + 8] = np.asarray(inp["mlstm_conv_b"], f32)[0].reshape(8, 128).T
    cf[:, C_EPS] = EPS
    cf[:, C_ONE1] = 1.0
    bt = _t5_bucket_table()
    kk = np.arange(128)[:, None]
    qq = np.arange(256)[None, :]
    dist = qq - kk
    idx = bt[np.clip(dist, 0, 255)]
    tt = rel[idx]
    tt = np.where((dist >= 0)[:, :, None], tt, f32(NEG))
    tt = np.ascontiguousarray(tt.transpose(0, 2, 1)).reshape(128, 8 * 256).astype(f32)
    gains = np.stack([np.asarray(inp["attn_norm"], f32)[0], np.asarray(inp["mlp_norm"], f32)[0],
                      np.asarray(inp["mlstm_norm"], f32)[0], np.asarray(inp["mlp_norm"], f32)[1],
                      np.asarray(inp["final_norm"], f32)], 0)
    gbc = np.ascontiguousarray(np.broadcast_to(gains[:, None, :], (5, 128, D))).astype(f32)
    hwbc = np.ascontiguousarray(np.broadcast_to(np.asarray(inp["mlstm_head_norm"], f32)[0][None, :], (128, D)))
    ident = np.eye(128, dtype=f32).astype(ml_dtypes.bfloat16)
    return dict(cf32=cf, tt=tt, gbc=gbc, hwbc=hwbc, ident=ident)


def make_in_maps(inp, ncores, nseq):
    c = host_consts(inp)
    f32 = np.float32
    shared = dict(
        a_w_in=np.ascontiguousarray(np.asarray(inp["attn_w_in"], f32)[0]),
        a_w_out=np.ascontiguousarray(np.asarray(inp["attn_w_out"], f32)[0]),
        m_w_in=np.ascontiguousarray(np.asarray(inp["mlstm_w_in"], f32)[0]),
        m_w_out=np.ascontiguousarray(np.asarray(inp["mlstm_w_out"], f32)[0]),
        w1=np.ascontiguousarray(np.asarray(inp["mlp_w1"], f32)),
        w2=np.ascontiguousarray(np.asarray(inp["mlp_w2"], f32)),
        **c,
    )
    x = np.asarray(inp["x"], f32)
    maps = []
    for i in range(ncores):
        m = dict(shared)
        m["x"] = np.ascontiguousarray(x[i * nseq:(i + 1) * nseq])
        maps.append(m)
    return maps


_NC_CACHE = {}


def kernel(**inputs):
    nseq = 16 // NCORES
    if "full" not in _NC_CACHE:
        _NC_CACHE["full"] = build(nseq=nseq)
    nc = _NC_CACHE["full"]
    in_maps = make_in_maps(inputs, NCORES, nseq)
    res = run_bass_kernel_spmd(nc, in_maps, core_ids=list(range(NCORES)))
    out = np.concatenate([np.asarray(r["out"]) for r in res.results], axis=0)
    return out.astype(np.float32)
```

```python
import math
from contextlib import ExitStack

import numpy as np
import ml_dtypes

import concourse.bass as bass
import concourse.mybir as mybir
from concourse.bass_utils import run_bass_kernel_spmd

F32 = mybir.dt.float32
BF16 = mybir.dt.bfloat16
AF = mybir.ActivationFunctionType
ALU = mybir.AluOpType
AX = mybir.AxisListType

NCORES = 8
SEQ = 2048
D = 1024
NT = SEQ // 128
EPS = 1e-6
NEG = -30000.0
LAMBDA_INIT = 0.8 - 0.6 * math.exp(-0.3 * 0)

C_MASK = 0
C_ONES = 128
C_CB = 256
C_LQK = 264
C_SUBW = 520
C_GB = 648
C_CW = 656
C_CBIAS = 688
C_EPS = 696
C_ONE1 = 697
NCF = 704


class Op:
    __slots__ = ("eng", "fn", "deps", "dma_key", "dma_cnt", "inc", "ticket", "idx")

    def __init__(self, eng, fn, dma_key):
        self.eng = eng
        self.fn = fn
        self.deps = []
        self.dma_key = dma_key
        self.dma_cnt = 0
        self.inc = False
        self.ticket = 0
        self.idx = 0


class Sched:
    ENGS = ("pe", "act", "dve", "pool", "sp")

    def __init__(self):
        self.ops = {e: [] for e in self.ENGS}
        self.res = {}
        self.dma_counts = {}
        self.barrier_ops = None
        self.passed = set()

    def _st(self, k):
        st = self.res.get(k)
        if st is None:
            st = self.res[k] = [None, {}]
        return st

    def snapshot(self, keys):
        out = []
        for k in keys:
            st = self._st(k)
            if st[0] is not None:
                out.append(st[0])
            out.extend(st[1].values())
        return out

    def add(self, eng, fn, reads=(), writes=(), war=(), dma_key=None, after=()):
        op = Op(eng, fn, dma_key)
        op.idx = len(self.ops[eng])
        deps = {}

        def dep(o):
            if o is None or o is op:
                return
            key = ("dma", id(o)) if o.dma_key is not None else o.eng
            cur = deps.get(key)
            if cur is None or o.idx > cur.idx:
                deps[key] = o

        for r in reads:
            st = self._st(r)
            dep(st[0])
        for w in writes:
            st = self._st(w)
            dep(st[0])
            for o in st[1].values():
                dep(o)
        for w in war:
            st = self._st(w)
            for o in st[1].values():
                dep(o)
        for o in after:
            dep(o)
        if self.barrier_ops is not None and eng not in self.passed:
            for o in self.barrier_ops:
                dep(o)
            self.passed.add(eng)
        for r in reads:
            st = self._st(r)
            rk = ("dma", id(op)) if dma_key is not None else eng
            st[1][rk] = op
        for w in writes:
            self.res[w] = [op, {}]
        if dma_key is not None:
            c = self.dma_counts.get(dma_key, 0) + 1
            self.dma_counts[dma_key] = c
            op.dma_cnt = c
        op.deps = list(deps.values())
        self.ops[eng].append(op)
        return op

    def barrier(self):
        ops = []
        for e in ("pe", "act", "dve"):
            if self.ops[e]:
                ops.append(self.ops[e][-1])
        self.barrier_ops = ops
        self.passed = set()

    def simulate(self):
        done = set()
        ptr = {e: 0 for e in self.ENGS}
        total = sum(len(v) for v in self.ops.values())
        n = 0
        while n < total:
            progressed = False
            for e in self.ENGS:
                while ptr[e] < len(self.ops[e]):
                    op = self.ops[e][ptr[e]]
                    if all(id(d) in done for d in op.deps):
                        done.add(id(op))
                        ptr[e] += 1
                        n += 1
                        progressed = True
                    else:
                        break
            if not progressed:
                return {e: (ptr[e], len(self.ops[e])) for e in self.ENGS}
        return None

    def finalize(self, nc, stack, final_waits):
        for e in self.ENGS:
            for op in self.ops[e]:
                for d in op.deps:
                    if d.dma_key is None:
                        if not (d.eng == "pe" and op.eng == "pe"):
                            d.inc = True
        for e in self.ENGS:
            n = 0
            for op in self.ops[e]:
                if op.inc and op.dma_key is None:
                    n += 1
                    op.ticket = n
        esem = {e: stack.enter_context(nc.semaphore("s_" + e)) for e in self.ENGS}
        dsem = {}
        for k in self.dma_counts:
            dsem[k] = stack.enter_context(nc.semaphore("d_" + "_".join(str(x) for x in k)))
        block = stack.enter_context(nc.Block())
        sched = self

        def replay(e, eng):
            waited = {}
            for op in sched.ops[e]:
                for d in op.deps:
                    if d.dma_key is not None:
                        sem, val, sk = dsem[d.dma_key], 16 * d.dma_cnt, ("d", d.dma_key)
                    else:
                        if d.eng == "pe" and e == "pe":
                            continue
                        sem, val, sk = esem[d.eng], d.ticket, ("e", d.eng)
                    if waited.get(sk, 0) >= val:
                        continue
                    waited[sk] = val
                    eng.wait_ge(sem, val)
                ins = op.fn(eng)
                if op.dma_key is not None:
                    ins.then_inc(dsem[op.dma_key], 16)
                elif op.inc:
                    ins.then_inc(esem[e], 1)
            for k in final_waits.get(e, ()):
                eng.wait_ge(dsem[k], 16 * sched.dma_counts[k])

        @block.tensor
        def _(eng):
            replay("pe", eng)

        @block.scalar
        def _(eng):
            replay("act", eng)

        @block.vector
        def _(eng):
            replay("dve", eng)

        @block.gpsimd
        def _(eng):
            replay("pool", eng)

        @block.sync
        def _(eng):
            replay("sp", eng)


def build(nseq=2, phases=("attn", "mlp0", "mlstm", "mlp1", "final")):
    nc = bass.Bass("TRN2", target_bir_lowering=False)
    x_d = nc.dram_tensor("x", [nseq, SEQ, D], F32, kind="ExternalInput").ap()
    out_d = nc.dram_tensor("out", [nseq, SEQ, D], F32, kind="ExternalOutput").ap()
    a_w_in = nc.dram_tensor("a_w_in", [D, 3072], F32, kind="ExternalInput").ap()
    a_w_out = nc.dram_tensor("a_w_out", [D, D], F32, kind="ExternalInput").ap()
    m_w_in = nc.dram_tensor("m_w_in", [D, 3080], F32, kind="ExternalInput").ap()
    m_w_out = nc.dram_tensor("m_w_out", [D, D], F32, kind="ExternalInput").ap()
    w1_d = nc.dram_tensor("w1", [2, D, 4096], F32, kind="ExternalInput").ap()
    w2_d = nc.dram_tensor("w2", [2, 4096, D], F32, kind="ExternalInput").ap()
    cf_d = nc.dram_tensor("cf32", [128, NCF], F32, kind="ExternalInput").ap()
    id_d = nc.dram_tensor("ident", [128, 128], BF16, kind="ExternalInput").ap()
    tt_d = nc.dram_tensor("tt", [128, 8 * 256], F32, kind="ExternalInput").ap()
    gbc_d = nc.dram_tensor("gbc", [5, 128, D], F32, kind="ExternalInput").ap()
    hw_d = nc.dram_tensor("hwbc", [128, D], F32, kind="ExternalInput").ap()
    dbg_d = nc.dram_tensor("dbg", [128, 8 * SEQ], BF16, kind="ExternalOutput").ap() if "dbg_hnT" in phases else None
    dbgh = [int(p[8:]) for p in phases if p.startswith("dbg_otok")]
    dbo_d = nc.dram_tensor("dbo", [128, NT * 128], BF16, kind="ExternalOutput").ap() if dbgh else None
    dbq_d = nc.dram_tensor("dbq", [128, 3 * SEQ], BF16, kind="ExternalOutput").ap() if dbgh else None

    S = Sched()
    stack = ExitStack()
    sb = lambda name, shape, dt: stack.enter_context(nc.sbuf_tensor(name, shape, dt))
    X = sb("X", [128, NT, D], F32)
    hnT = sb("hnT", [128, 8, SEQ], BF16)
    WS = sb("WS", [128, 4, 4096], BF16)
    ARENA_B = 60 * 1024
    arena = sb("arena", [128, ARENA_B // 2], BF16)
    CF = sb("CF", [128, NCF], F32)
    ident = sb("identsb", [128, 128], BF16)
    xn = sb("xn", [128, 2, D], BF16)
    junk = sb("junk", [128, D], BF16)
    gbc = sb("gbcs", [128, 2, D], F32)
    ssq = sb("ssq", [128, NT], F32)
    rstd = sb("rstd", [128, NT], F32)
    lnt = sb("lnt", [128, NT], F32)
    PS = [stack.enter_context(nc.psum_tensor(f"ps{i}", [128, 512], F32)) for i in range(8)]

    def carve(off, dt, *dims):
        n = 1
        for d_ in dims:
            n *= d_
        nb = n * (4 if dt == F32 else 2)
        assert off % 4 == 0 and off + nb <= ARENA_B, (off, nb)
        ap = arena[:, off // 2:(off + nb) // 2]
        if dt == F32:
            ap = ap.bitcast(F32)
        if len(dims) == 2:
            ap = ap.rearrange("p (a b) -> p a b", a=dims[0])
        elif len(dims) == 3:
            ap = ap.rearrange("p (a b c) -> p a b c", a=dims[0], b=dims[1])
        return ap, off + ((nb + 3) // 4) * 4

    PE = lambda fn, r=(), w=(): S.add("pe", fn, r, w)
    ACT = lambda fn, r=(), w=(): S.add("act", fn, r, w)
    DVE = lambda fn, r=(), w=(): S.add("dve", fn, r, w)

    def SPDMA(out, in_, r=(), w=(), key=None):
        return S.add("sp", lambda e: e.dma_start(out=out, in_=in_), r, w, dma_key=key)

    cfc = lambda c0, n: CF[:, c0:c0 + n]

    plan = []
    for s in range(nseq):
        if "attn" in phases:
            for h in range(8):
                plan.append(("a_qkv", h))
                if h % 2 == 1:
                    plan.append(("a_wo", h // 2))
        if "mlp0" in phases:
            for g in range(8):
                plan.append(("w1", 0, g))
                plan.append(("w2", 0, g))
        if "mlstm" in phases:
            plan.append(("m_g",))
            for hd in range(4):
                plan.append(("m_qk", hd))
                plan.append(("m_v", hd))
                plan.append(("m_o", hd))
                plan.append(("m_wo", hd))
        if "mlp1" in phases:
            for g in range(8):
                plan.append(("w1", 1, g))
                plan.append(("w2", 1, g))

    wstate = {"issued": 0, "next": 0}

    def w_issue(i):
        tag = plan[i]
        slot = i % 4
        dst = WS[:, slot, :]
        kind = tag[0]
        parts = []
        if kind == "a_qkv":
            h = tag[1]
            for t_ in range(3):
                src = a_w_in[:, t_ * 1024 + h * 128:t_ * 1024 + (h + 1) * 128].rearrange("(c p) n -> p c n", p=128)
                o = dst[:, 0:8 * 384].rearrange("p (c t n) -> p c t n", c=8, t=3)[:, :, t_, :]
                parts.append((o, src))
        elif kind == "a_wo":
            hp = tag[1]
            src = a_w_out[hp * 256:(hp + 1) * 256, :].rearrange("(c p) n -> p c n", p=128)
            parts.append((dst[:, 0:2048].rearrange("p (c n) -> p c n", c=2), src))
        elif kind == "w1":
            _, l, g = tag
            src = w1_d[l, :, g * 512:(g + 1) * 512].rearrange("(c p) n -> p c n", p=128)
            parts.append((dst[:, 0:4096].rearrange("p (c n) -> p c n", c=8), src))
        elif kind == "w2":
            _, l, g = tag
            src = w2_d[l, g * 512:(g + 1) * 512, :].rearrange("(c p) n -> p c n", p=128)
            parts.append((dst[:, 0:4096].rearrange("p (c n) -> p c n", c=4), src))
        elif kind == "m_g":
            src = m_w_in[:, 3072:3080].rearrange("(c p) n -> p c n", p=128)
            parts.append((dst[:, 0:64].rearrange("p (c n) -> p c n", c=8), src))
        elif kind == "m_qk":
            hd = tag[1]
            for t_ in range(2):
                src = m_w_in[:, t_ * 512 + hd * 128:t_ * 512 + (hd + 1) * 128].rearrange("(c p) n -> p c n", p=128)
                o = dst[:, 0:2048].rearrange("p (c t n) -> p c t n", c=8, t=2)[:, :, t_, :]
                parts.append((o, src))
        elif kind == "m_v":
            hd = tag[1]
            src = m_w_in[:, 1024 + hd * 256:1024 + (hd + 1) * 256].rearrange("(c p) n -> p c n", p=128)
            parts.append((dst[:, 0:2048].rearrange("p (c n) -> p c n", c=8), src))
        elif kind == "m_o":
            hd = tag[1]
            src = m_w_in[:, 2048 + hd * 256:2048 + (hd + 1) * 256].rearrange("(c p) n -> p c n", p=128)
            parts.append((dst[:, 0:2048].rearrange("p (c n) -> p c n", c=8), src))
        elif kind == "m_wo":
            hd = tag[1]
            src = m_w_out[hd * 256:(hd + 1) * 256, :].rearrange("(c p) n -> p c n", p=128)
            parts.append((dst[:, 0:2048].rearrange("p (c n) -> p c n", c=2), src))
        else:
            raise ValueError(kind)
        snap = S.snapshot([("ws", slot, p_) for p_ in range(3)])
        for p_, (o, src) in enumerate(parts):
            S.add("pool", lambda e, o=o, src=src: e.dma_start(out=o, in_=src), (), [("ws", slot, p_)],
                  dma_key=("ws", slot, p_), after=snap)

    def w_next(tag):
        i = wstate["next"]
        assert plan[i] == tag, (plan[i], tag)
        while wstate["issued"] < min(len(plan), i + 3):
            w_issue(wstate["issued"])
            wstate["issued"] += 1
        wstate["next"] = i + 1
        slot = i % 4
        return WS[:, slot, :], ("ws", slot, 0)

    SPDMA(CF[:], cf_d, w=["CF"], key=("c", 0))
    SPDMA(ident[:], id_d, w=["ident"], key=("c", 1))
    gbstate = {"n": 0}

    def load_gbc(idx):
        b = gbstate["n"] % 2
        gbstate["n"] += 1
        SPDMA(gbc[:, b, :], gbc_d[idx], w=[("gbc", b)], key=("g", b))
        return b

    def emit_stats():
        for t in range(NT):
            ACT(lambda e, t=t: e.activation(out=junk[:], in_=X[:, t, :], func=AF.Square,
                                            accum_out=ssq[:, t:t + 1]),
                r=[("X", t)], w=[("ssq", t)])
        ACT(lambda e: e.activation(out=lnt[:], in_=ssq[:], func=AF.Ln, scale=1.0 / D, bias=cfc(C_EPS, 1)),
            r=[("ssq", t) for t in range(NT)] + ["CF"], w=["lnt"])
        ACT(lambda e: e.activation(out=rstd[:], in_=lnt[:], func=AF.Exp, scale=-0.5),
            r=["lnt"], w=["rstd"])

    def emit_norm(gidx):
        gb = load_gbc(gidx)
        emit_stats()
        for t in range(NT):
            b = t % 2
            DVE(lambda e, t=t, b=b: e.scalar_tensor_tensor(out=xn[:, b, :], in0=X[:, t, :],
                                                           scalar=rstd[:, t:t + 1], in1=gbc[:, gb, :],
                                                           op0=ALU.mult, op1=ALU.mult),
                r=[("X", t), "rstd", ("gbc", gb)], w=[("xn", b)])
            bank = 6 + (t % 2)
            pv = PS[bank][:].bitcast(BF16)
            for c in range(8):
                PE(lambda e, c=c, b=b, pv=pv: e.transpose(out=pv[:, c * 128:(c + 1) * 128],
                                                          in_=xn[:, b, c * 128:(c + 1) * 128],
                                                          identity=ident[:]),
                   r=[("xn", b), "ident"], w=[("ps", bank)])
            o = hnT[:, :, t * 128:(t + 1) * 128]
            i_ = pv.rearrange("p (c n) -> p c n", c=8)
            if t % 2 == 0:
                ACT(lambda e, o=o, i_=i_: e.activation(out=o, in_=i_, func=AF.Copy),
                    r=[("ps", bank)], w=[("hnT", t)])
            else:
                DVE(lambda e, o=o, i_=i_: e.tensor_copy(out=o, in_=i_),
                    r=[("ps", bank)], w=[("hnT", t)])

    HN_ALL = [("hnT", t) for t in range(NT)]

    def emit_mlp(layer):
        S.barrier()
        off = 0
        h1T, off = carve(off, BF16, 2, 4, SEQ)
        rtmp, off = carve(off, F32, 2, 512)
        cnt = {"a": 0, "b": 0, "r": 0}

        def g1(g, tq, w1v, wk1):
            hb = g % 2
            for c in range(4):
                bank = cnt["a"] % 4
                cnt["a"] += 1
                for d_ in range(8):
                    PE(lambda e, bank=bank, c=c, d_=d_: e.matmul(
                        PS[bank][:], lhsT=w1v[:, d_, c * 128:(c + 1) * 128],
                        rhs=hnT[:, d_, tq * 512:(tq + 1) * 512], start=(d_ == 0), stop=(d_ == 7)),
                       r=[wk1] + [("hnT", t) for t in range(tq * 4, tq * 4 + 4)], w=[("ps", bank)])
                rb = cnt["r"] % 2
                cnt["r"] += 1
                ACT(lambda e, bank=bank, rb=rb: e.activation(out=rtmp[:, rb, :], in_=PS[bank][:], func=AF.Relu),
                    r=[("ps", bank)], w=[("rtmp", rb)])
                ACT(lambda e, rb=rb, c=c, hb=hb: e.activation(out=h1T[:, hb, c, tq * 512:(tq + 1) * 512],
                                                              in_=rtmp[:, rb, :], func=AF.Square),
                    r=[("rtmp", rb)], w=[("h1T", hb, c, tq)])

        def g2(g, tq, w2v, wk2):
            hb = g % 2
            for t in range(tq * 4, tq * 4 + 4):
                for n in range(2):
                    bank = 4 + cnt["b"] % 4
                    cnt["b"] += 1
                    for c in range(4):
                        PE(lambda e, bank=bank, c=c, t=t, n=n: e.matmul(
                            PS[bank][:], lhsT=h1T[:, hb, c, t * 128:(t + 1) * 128],
                            rhs=w2v[:, c, n * 512:(n + 1) * 512], start=(c == 0), stop=(c == 3)),
                           r=[wk2, ("h1T", hb, c, tq)], w=[("ps", bank)])
                    DVE(lambda e, bank=bank, t=t, n=n: e.tensor_tensor(
                        out=X[:, t, n * 512:(n + 1) * 512], in0=PS[bank][:],
                        in1=X[:, t, n * 512:(n + 1) * 512], op=ALU.add),
                        r=[("ps", bank), ("X", t)], w=[("X", t)])

        for g in range(8):
            w1s, wk1 = w_next(("w1", layer, g))
            w2s, wk2 = w_next(("w2", layer, g))
            w1v = w1s[:, 0:4096].rearrange("p (c n) -> p c n", c=8)
            w2v = w2s[:, 0:4096].rearrange("p (c n) -> p c n", c=4)
            g1(g, 0, w1v, wk1)
            for tq in range(1, 4):
                g1(g, tq, w1v, wk1)
                g2(g, tq - 1, w2v, wk2)
            g2(g, 3, w2v, wk2)

    def emit_attn():
        S.barrier()
        off = 0
        TT, off = carve(off, F32, 8, 256)
        qTm, off = carve(off, BF16, 2, SEQ)
        kT, off = carve(off, BF16, SEQ)
        V, off = carve(off, BF16, NT, 130)
        PT, off = carve(off, BF16, 6, 512)
        osb, off = carve(off, F32, 2, 4, 128)
        t1sb, off = carve(off, F32, 4, 128)
        otok, off = carve(off, BF16, NT, 128)
        oTp, off = carve(off, BF16, 2, SEQ)
        subw, off = carve(off, F32, 128)
        sm, off = carve(off, F32, 64)
        lqk, off = carve(off, F32, 2, 64)
        accsb, off = carve(off, F32, 2, 1032)
        SPDMA(TT.rearrange("p a b -> p (a b)"), tt_d, w=["TT"], key=("c", 2))
        for k in range(2):
            DVE(lambda e, k=k: e.tensor_tensor(out=lqk[:, k, :], in0=cfc(C_LQK + 128 * k, 64),
                                               in1=cfc(C_LQK + 128 * k + 64, 64), op=ALU.mult),
                r=["CF"], w=[("lqk", k)])
            DVE(lambda e, k=k: e.reduce_sum(out=sm[:, k:k + 1], in_=lqk[:, k, :], axis=AX.X),
                r=[("lqk", k)], w=[("sm", k)])
        ACT(lambda e: e.activation(out=sm[:, 2:4], in_=sm[:, 0:2], func=AF.Exp),
            r=[("sm", 0), ("sm", 1)], w=[("sm", 2)])
        DVE(lambda e: e.scalar_tensor_tensor(out=sm[:, 4:5], in0=sm[:, 2:3], scalar=float(LAMBDA_INIT),
                                             in1=sm[:, 3:4], op0=ALU.add, op1=ALU.subtract),
            r=[("sm", 2)], w=["lam"])
        DVE(lambda e: e.tensor_scalar(out=subw, in0=cfc(C_SUBW, 128), scalar1=float(1.0 - LAMBDA_INIT),
                                      scalar2=None, op0=ALU.mult),
            r=["CF"], w=["subw"])
        DVE(lambda e: e.memset(V[:, :, 128:130], 1.0), w=["Vones"])
        for h_ in range(8):
            DVE(lambda e, h_=h_: e.tensor_scalar(out=TT[:, h_, :], in0=TT[:, h_, :], scalar1=cfc(C_CB + h_, 1),
                                                 scalar2=None, op0=ALU.subtract),
                r=["TT", "CF"], w=["TT"])
        DVE(lambda e: e.memset(qTm[64:128, 0, :], 0.0), w=["qz0"])
        DVE(lambda e: e.memset(qTm[0:64, 1, :], 0.0), w=["qz1"])
        lam = sm[:, 4:5]
        cnt = {"st": 0, "pt": 0, "nt": 0, "pj": 0, "ab": 0}
        STB = (0, 1, 2, 6, 7)
        ACCA = (3, 4)
        ACCB = 5

        def acc(m, qs):
            if qs < 3:
                return PS[ACCA[m]][:, qs * 129:(qs + 1) * 129], ("ps", ACCA[m])
            return PS[ACCB][:, m * 129:(m + 1) * 129], ("ps", ACCB)

        def attn_head(h):
            wsl, wk = w_next(("a_qkv", h))
            Wv_ = wsl[:, 0:8 * 384].rearrange("p (c t n) -> p c t n", c=8, t=3)
            for which in (0, 1):
                for tq in range(4):
                    bank = 6 + cnt["pj"] % 2
                    cnt["pj"] += 1
                    for d_ in range(8):
                        PE(lambda e, bank=bank, d_=d_, which=which, tq=tq: e.matmul(
                            PS[bank][:], lhsT=Wv_[:, d_, which, :], rhs=hnT[:, d_, tq * 512:(tq + 1) * 512],
                            start=(d_ == 0), stop=(d_ == 7)),
                           r=[(wk[0], wk[1], which)] + [("hnT", t) for t in range(tq * 4, tq * 4 + 4)], w=[("ps", bank)])
                    if which == 0:
                        DVE(lambda e, bank=bank, tq=tq: e.tensor_scalar(
                            out=qTm[0:64, 0, tq * 512:(tq + 1) * 512], in0=PS[bank][0:64, :], scalar1=0.125,
                            scalar2=None, op0=ALU.mult),
                            r=[("ps", bank)], w=[("qk", 0)])
                        DVE(lambda e, bank=bank, tq=tq: e.tensor_scalar(
                            out=qTm[64:128, 1, tq * 512:(tq + 1) * 512], in0=PS[bank][64:128, :], scalar1=0.125,
                            scalar2=None, op0=ALU.mult),
                            r=[("ps", bank), ("qk", 0)], w=[("qk", 0)])
                    else:
                        DVE(lambda e, bank=bank, tq=tq: e.tensor_copy(
                            out=kT[:, tq * 512:(tq + 1) * 512], in_=PS[bank][:]),
                            r=[("ps", bank)], w=[("qk", 1)])
            for t0 in range(0, NT, 4):
                bank = 6 + cnt["pj"] % 2
                cnt["pj"] += 1
                for tt_ in range(4):
                    t = t0 + tt_
                    for d_ in range(8):
                        PE(lambda e, bank=bank, d_=d_, t=t, tt_=tt_: e.matmul(
                            PS[bank][:, tt_ * 128:(tt_ + 1) * 128], lhsT=hnT[:, d_, t * 128:(t + 1) * 128],
                            rhs=Wv_[:, d_, 2, :], start=(d_ == 0), stop=(d_ == 7)),
                           r=[(wk[0], wk[1], 2), ("hnT", t)], w=[("ps", bank)])
                DVE(lambda e, bank=bank, t0=t0: e.tensor_copy(
                    out=V[:, t0:t0 + 4, 0:128], in_=PS[bank][:].rearrange("p (a b) -> p a b", a=4)),
                    r=[("ps", bank)], w=["V"])
            tiles = [(j, i, m) for j in range(4) for i in range(4 * j + 4) for m in range(2)]
            LA = 4
            info = {}

            def stage_a(n):
                j, i, m = tiles[n]
                qb0 = max(4 * j, i)
                q0 = qb0 * 128
                N = (4 * j + 4) * 128 - q0
                d0 = qb0 - i
                bank = STB[cnt["st"] % 5]
                cnt["st"] += 1
                PE(lambda e: e.matmul(
                    PS[bank][:, 0:N], lhsT=kT[:, i * 128:(i + 1) * 128],
                    rhs=qTm[:, m, q0:q0 + N], start=True, stop=True),
                   r=[("qk", 0), ("qk", 1), "qz0", "qz1"], w=[("ps", bank)])
                pb = cnt["pt"] % 6
                cnt["pt"] += 1
                if d0 < 2:
                    tc0 = 0 if d0 == 0 else 128
                    n1 = min(256 - tc0, N)
                    DVE(lambda e: e.tensor_tensor(
                        out=PS[bank][:, 0:n1], in0=PS[bank][:, 0:n1], in1=TT[:, h, tc0:tc0 + n1], op=ALU.add),
                        r=[("ps", bank), "TT"], w=[("ps", bank)])
                ACT(lambda e: e.activation(out=PT[:, pb, 0:N], in_=PS[bank][:, 0:N], func=AF.Exp),
                    r=[("ps", bank)], w=[("PT", pb)])
                info[n] = (pb, q0, qb0)

            def stage_b(n):
                j, i, m = tiles[n]
                pb, q0, qb0 = info[n]
                for qs in range(qb0 - 4 * j, 4):
                    qb = 4 * j + qs
                    col = qb * 128 - q0
                    a_ap, a_key = acc(m, qs)
                    first = (i == 0) and ((qs == 0) or (qs == 3 and m == 0))
                    PE(lambda e, a_ap=a_ap, col=col, first=first, qb=qb: e.matmul(
                        a_ap, lhsT=PT[:, pb, col:col + 128], rhs=V[:, i, 0:129],
                        start=first, stop=(i == qb), skip_group_check=True),
                       r=[("PT", pb), "V", "Vones"], w=[a_key])
                if i == 4 * j + 3 and m == 1:
                    finalize(j)

            pending = []

            def finalize(j):
                ab = cnt["ab"] % 2
                cnt["ab"] += 1
                DVE(lambda e: e.tensor_copy(out=accsb[:, ab, 0:387], in_=PS[ACCA[0]][:, 0:387]),
                    r=[("ps", ACCA[0])], w=[("accsb", ab, 0)])
                DVE(lambda e: e.tensor_copy(out=accsb[:, ab, 387:774], in_=PS[ACCA[1]][:, 0:387]),
                    r=[("ps", ACCA[1])], w=[("accsb", ab, 1)])
                DVE(lambda e: e.tensor_copy(out=accsb[:, ab, 774:1032], in_=PS[ACCB][:, 0:258]),
                    r=[("ps", ACCB)], w=[("accsb", ab, 2)])
                akeys = [("accsb", ab, k_) for k_ in range(3)]

                def sacc(m, qs):
                    o_ = (m * 387 + qs * 129) if qs < 3 else (774 + m * 129)
                    return accsb[:, ab, o_:o_ + 129]

                sb0 = 8 + ab * 24

                def part1(qs):
                    a0 = sacc(0, qs)
                    a1 = sacc(1, qs)
                    c0 = sb0 + qs * 3
                    DVE(lambda e, a0=a0, c0=c0: e.reciprocal(out=sm[:, c0:c0 + 1], in_=a0[:, 128:129]),
                        r=akeys, w=[("smq", ab, qs, 0)])
                    DVE(lambda e, a1=a1, c0=c0: e.reciprocal(out=sm[:, c0 + 1:c0 + 2], in_=a1[:, 128:129]),
                        r=akeys, w=[("smq", ab, qs, 1)])
                    DVE(lambda e, c0=c0: e.tensor_tensor(out=sm[:, c0 + 2:c0 + 3], in0=sm[:, c0 + 1:c0 + 2],
                                                         in1=lam, op=ALU.mult),
                        r=[("smq", ab, qs, 1), "lam"], w=[("smq", ab, qs, 2)])
                    DVE(lambda e, a1=a1, c0=c0, qs=qs: e.tensor_scalar(
                        out=t1sb[:, qs, :], in0=a1[:, 0:128], scalar1=sm[:, c0 + 2:c0 + 3], scalar2=None,
                        op0=ALU.mult),
                        r=akeys + [("smq", ab, qs, 2)], w=[("t1", qs)])
                    DVE(lambda e, a0=a0, c0=c0, qs=qs: e.scalar_tensor_tensor(
                        out=osb[:, ab, qs, :], in0=a0[:, 0:128], scalar=sm[:, c0:c0 + 1], in1=t1sb[:, qs, :],
                        op0=ALU.mult, op1=ALU.subtract),
                        r=akeys + [("smq", ab, qs, 0), ("t1", qs)], w=[("osb", ab, qs)])
                    DVE(lambda e, qs=qs: e.scalar_tensor_tensor(
                        out=t1sb[:, qs, :], in0=osb[:, ab, qs, :], scalar=1.0, in1=osb[:, ab, qs, :],
                        op0=ALU.mult, op1=ALU.mult, accum_out=sm[:, sb0 + 12 + qs:sb0 + 13 + qs]),
                        r=[("osb", ab, qs), ("t1", qs)], w=[("t1", qs), ("ss2", ab, qs)])

                def part2():
                    ACT(lambda e: e.activation(out=sm[:, sb0 + 16:sb0 + 20], in_=sm[:, sb0 + 12:sb0 + 16], func=AF.Ln,
                                               scale=1.0 / 128, bias=cfc(C_EPS, 1)),
                        r=[("ss2", ab, q_) for q_ in range(4)] + ["CF"], w=[("ln2", ab)])
                    ACT(lambda e: e.activation(out=sm[:, sb0 + 20:sb0 + 24], in_=sm[:, sb0 + 16:sb0 + 20],
                                               func=AF.Exp, scale=-0.5),
                        r=[("ln2", ab)], w=[("rs2", ab)])
                    for qs in range(4):
                        t = 4 * j + qs
                        DVE(lambda e, qs=qs, t=t: e.scalar_tensor_tensor(
                            out=otok[:, t, :], in0=osb[:, ab, qs, :], scalar=sm[:, sb0 + 20 + qs:sb0 + 21 + qs],
                            in1=subw, op0=ALU.mult, op1=ALU.mult),
                            r=[("osb", ab, qs), ("rs2", ab), "subw"], w=[("otok", t)])

                for qs_ in range(4):
                    pending.append([1 + qs_, (lambda qs_=qs_: part1(qs_))])
                pending.append([9, part2])

            def tick():
                for p_ in list(pending):
                    p_[0] -= 1
                    if p_[0] <= 0:
                        pending.remove(p_)
                        p_[1]()

            for n in range(len(tiles) + LA):
                if n < len(tiles):
                    stage_a(n)
                if n >= LA:
                    stage_b(n - LA)
                tick()
            while pending:
                tick()
            if dbgh and h == dbgh[0]:
                SPDMA(dbo_d, otok.rearrange("p a b -> p (a b)"), r=[("otok", t) for t in range(NT)], key=("c", 6))
                SPDMA(dbq_d[:, 0:SEQ], qTm[:, 0, :], r=[("qk", 0)], key=("c", 7))
                SPDMA(dbq_d[:, SEQ:2 * SEQ], kT, r=[("qk", 1)], key=("c", 8))
                SPDMA(dbq_d[:, 2 * SEQ:2 * SEQ + NT * 128].rearrange("p (a b) -> p a b", a=NT), V[:, :, 0:128], r=["V"], key=("c", 9))
            hh = h % 2
            for t0 in range(0, NT, 4):
                bank = 6 + cnt["pj"] % 2
                cnt["pj"] += 1
                pv = PS[bank][:].bitcast(BF16)
                for tt_ in range(4):
                    t = t0 + tt_
                    PE(lambda e, pv=pv, t=t, tt_=tt_: e.transpose(out=pv[:, tt_ * 128:(tt_ + 1) * 128],
                                                                  in_=otok[:, t, :], identity=ident[:]),
                       r=[("otok", t), "ident"], w=[("ps", bank)])
                ACT(lambda e, pv=pv, t0=t0, hh=hh: e.activation(out=oTp[:, hh, t0 * 128:(t0 + 4) * 128],
                                                                in_=pv[:, 0:512], func=AF.Copy),
                    r=[("ps", bank)], w=[("oTp", hh, t0)])
            if hh == 1:
                wsl2, wk2 = w_next(("a_wo", h // 2))
                wo = wsl2[:, 0:2048].rearrange("p (c n) -> p c n", c=2)
                for t in range(NT):
                    for n in range(2):
                        bank = 6 + cnt["pj"] % 2
                        cnt["pj"] += 1
                        for c in range(2):
                            PE(lambda e, bank=bank, c=c, t=t, n=n: e.matmul(
                                PS[bank][:], lhsT=oTp[:, c, t * 128:(t + 1) * 128],
                                rhs=wo[:, c, n * 512:(n + 1) * 512], start=(c == 0), stop=(c == 1)),
                               r=[wk2, ("oTp", c, (t // 4) * 4)], w=[("ps", bank)])
                        DVE(lambda e, bank=bank, t=t, n=n: e.tensor_tensor(
                            out=X[:, t, n * 512:(n + 1) * 512], in0=PS[bank][:],
                            in1=X[:, t, n * 512:(n + 1) * 512], op=ALU.add),
                            r=[("ps", bank), ("X", t)], w=[("X", t)])

        for h in range(8):
            attn_head(h)

    def emit_mlstm():
        S.barrier()
        off = 0
        A1_off = off
        raw, off = carve(off, F32, SEQ + 16)
        cacc, off = carve(off, F32, SEQ)
        numsb, _ = carve(A1_off, F32, NT, 257)
        assert NT * 257 * 4 <= off - A1_off
        qT, off = carve(off, BF16, SEQ)
        kT, off = carve(off, BF16, SEQ)
        vaug, off = carve(off, BF16, NT, 258)
        og, off = carve(off, BF16, NT, 256)
        htT, _ = carve(A1_off, BF16, 2, SEQ)
        hwbc, off = carve(off, F32, 256)
        G, off = carve(off, F32, NT, 8)
        nl, off = carve(off, F32, NT, 4)
        fq, off = carve(off, F32, NT, 4)
        fk, off = carve(off, F32, NT, 4)
        gg, off = carve(off, F32, NT, 4)
        eb, off = carve(off, F32, NT, 4)
        tmpa, off = carve(off, F32, NT, 4)
        tmpb, off = carve(off, F32, NT, 4)
        CTf, off = carve(off, F32, 2, 258)
        CTb, off = carve(off, BF16, NT, 258)
        kpp, off = carve(off, BF16, 3, 128)
        scsb, off = carve(off, BF16, 3, 128)
        dd, off = carve(off, F32, NT)
        r2, off = carve(off, F32, NT)
        ss3, off = carve(off, F32, NT)
        ln3, off = carve(off, F32, NT)
        r3, off = carve(off, F32, NT)
        maskT = cfc(C_MASK, 128)
        ones = cfc(C_ONES, 128)
        cnt = {"pj": 0, "k": 0}
        wsl, wk = w_next(("m_g",))
        wg = wsl[:, 0:64].rearrange("p (c n) -> p c n", c=8)
        for t in range(NT):
            for d_ in range(8):
                PE(lambda e, t=t, d_=d_: e.matmul(PS[0][:, t * 8:(t + 1) * 8], lhsT=hnT[:, d_, t * 128:(t + 1) * 128],
                                                  rhs=wg[:, d_, :], start=(d_ == 0), stop=(d_ == 7)),
                   r=[wk, ("hnT", t)], w=[("ps", 0)])
        Gv = PS[0][:, 0:128].rearrange("p (t n) -> p t n", t=NT)
        for t in range(NT):
            pass
        DVE(lambda e: e.tensor_tensor(out=G, in0=Gv, in1=cfc(C_GB, 8).unsqueeze(1).to_broadcast([128, NT, 8]),
                                      op=ALU.add),
            r=[("ps", 0), "CF"], w=["G"])
        ACT(lambda e: e.activation(out=tmpa, in_=G[:, :, 4:8], func=AF.Exp, scale=-1.0), r=["G"], w=["tmpa"])
        ACT(lambda e: e.activation(out=nl, in_=tmpa, func=AF.Ln, bias=cfc(C_ONE1, 1)), r=["tmpa", "CF"], w=["nl"])
        nlf = nl.rearrange("p t h -> p (t h)")
        PE(lambda e: e.matmul(PS[1][:, 0:64], lhsT=maskT, rhs=nlf, start=True, stop=True),
           r=["nl", "CF"], w=[("ps", 1)])
        PE(lambda e: e.matmul(PS[1][:, 64:128], lhsT=ones, rhs=nlf, start=True, stop=True),
           r=["nl", "CF"], w=[("ps", 1)])
        cum = PS[1][:, 0:64].rearrange("p (t h) -> p t h", t=NT)
        tot = PS[1][:, 64:128].rearrange("p (t h) -> p t h", t=NT)
        ACT(lambda e: e.activation(out=fq, in_=cum, func=AF.Exp, scale=-1.0), r=[("ps", 1)], w=["fq0"])
        DVE(lambda e: e.tensor_scalar(out=fq, in0=fq, scalar1=float(128 ** -0.5), scalar2=None, op0=ALU.mult),
            r=["fq0"], w=["fq"])
        DVE(lambda e: e.tensor_tensor(out=tmpa, in0=cum, in1=G[:, :, 0:4], op=ALU.add),
            r=[("ps", 1), "G", "nl"], w=["tmpa2"])
        ACT(lambda e: e.activation(out=fk, in_=tmpa, func=AF.Exp), r=["tmpa2"], w=["fk"])
        DVE(lambda e: e.tensor_tensor(out=tmpb, in0=tmpa, in1=tot, op=ALU.subtract),
            r=["tmpa2", ("ps", 1)], w=["tmpb"])
        ACT(lambda e: e.activation(out=gg, in_=tmpb, func=AF.Exp), r=["tmpb"], w=["gg"])
        ACT(lambda e: e.activation(out=eb, in_=tot, func=AF.Exp, scale=-1.0), r=[("ps", 1)], w=["eb"])
        DVE(lambda e: e.memset(vaug[:, :, 256:258], 1.0), w=["vones"])

        def m_head(hd):
            wqk_s, wkq = w_next(("m_qk", hd))
            wqk = wqk_s[:, 0:2048].rearrange("p (c t n) -> p c t n", c=8, t=2)
            S.add("sp", lambda e, hd=hd: e.dma_start(out=hwbc, in_=hw_d[:, hd * 256:(hd + 1) * 256]),
                  (), ["hwbc"], dma_key=("c", 3))
            DVE(lambda e: e.memset(raw[:, 0:3], 0.0), r=[], w=["A1", "rawz"])
            for which, dst in ((0, qT), (1, kT)):
                ch = which * 4 + hd
                for tq in range(4):
                    bank = 6 + cnt["pj"] % 2
                    cnt["pj"] += 1
                    for d_ in range(8):
                        PE(lambda e, bank=bank, d_=d_, which=which, tq=tq: e.matmul(
                            PS[bank][:], lhsT=wqk[:, d_, which, :], rhs=hnT[:, d_, tq * 512:(tq + 1) * 512],
                            start=(d_ == 0), stop=(d_ == 7)),
                           r=[(wkq[0], wkq[1], which)] + [("hnT", t) for t in range(tq * 4, tq * 4 + 4)], w=[("ps", bank)])
                    ACT(lambda e, bank=bank, tq=tq: e.activation(
                        out=raw[:, 3 + tq * 512:3 + (tq + 1) * 512], in_=PS[bank][:], func=AF.Copy),
                        r=[("ps", bank), "rawz"], w=["A1"])
                cw = lambda k_, ch=ch: cfc(C_CW + ch * 4 + k_, 1)
                DVE(lambda e, cw=cw: e.tensor_scalar(out=cacc, in0=raw[:, 3:3 + SEQ], scalar1=cw(3), scalar2=None,
                                                     op0=ALU.mult),
                    r=["A1", "rawz", "CF"], w=["cacc"])
                for k_ in (2, 1, 0):
                    DVE(lambda e, cw=cw, k_=k_: e.scalar_tensor_tensor(
                        out=cacc, in0=raw[:, k_:k_ + SEQ], scalar=cw(k_), in1=cacc, op0=ALU.mult, op1=ALU.add),
                        r=["A1", "rawz", "cacc"], w=["cacc"])
                ACT(lambda e, dst=dst, ch=ch: e.activation(out=dst, in_=cacc, func=AF.Silu,
                                                           bias=cfc(C_CBIAS + ch, 1)),
                    r=["cacc", "CF"], w=[("mqk", which)])
            wv_s, wkv = w_next(("m_v", hd))
            wv = wv_s[:, 0:2048].rearrange("p (c n) -> p c n", c=8)
            for t0 in range(0, NT, 2):
                bank = 6 + cnt["pj"] % 2
                cnt["pj"] += 1
                for tt_ in range(2):
                    t = t0 + tt_
                    for d_ in range(8):
                        PE(lambda e, bank=bank, d_=d_, t=t, tt_=tt_: e.matmul(
                            PS[bank][:, tt_ * 256:(tt_ + 1) * 256], lhsT=hnT[:, d_, t * 128:(t + 1) * 128],
                            rhs=wv[:, d_, :], start=(d_ == 0), stop=(d_ == 7)),
                           r=[wkv, ("hnT", t)], w=[("ps", bank)])
                DVE(lambda e, bank=bank, t0=t0: e.tensor_copy(
                    out=vaug[:, t0:t0 + 2, 0:256], in_=PS[bank][:].rearrange("p (a b) -> p a b", a=2)),
                    r=[("ps", bank)], w=["vaug"])
            wo_s, wko = w_next(("m_o", hd))
            wo_ = wo_s[:, 0:2048].rearrange("p (c n) -> p c n", c=8)
            for t0 in range(0, NT, 2):
                bank = 6 + cnt["pj"] % 2
                cnt["pj"] += 1
                for tt_ in range(2):
                    t = t0 + tt_
                    for d_ in range(8):
                        PE(lambda e, bank=bank, d_=d_, t=t, tt_=tt_: e.matmul(
                            PS[bank][:, tt_ * 256:(tt_ + 1) * 256], lhsT=hnT[:, d_, t * 128:(t + 1) * 128],
                            rhs=wo_[:, d_, :], start=(d_ == 0), stop=(d_ == 7)),
                           r=[wko, ("hnT", t)], w=[("ps", bank)])
                ACT(lambda e, bank=bank, t0=t0: e.activation(
                    out=og[:, t0:t0 + 2, :], in_=PS[bank][:].rearrange("p (a b) -> p a b", a=2), func=AF.Sigmoid),
                    r=[("ps", bank)], w=[("og", t0)])
                for tt_ in range(2):
                    DVE(lambda e, t=t0 + tt_: e.tensor_tensor(out=og[:, t, :], in0=og[:, t, :], in1=hwbc, op=ALU.mult),
                        r=[("og", t0), "hwbc"], w=[("og", t0)])
            DK = [("dsb", c) for c in range(NT)]
            def p1a(c):
                cs = slice(c * 128, (c + 1) * 128)
                sb_ = c % 3
                tb = (2, 3, 0)[sb_]
                pvb = PS[tb][:].bitcast(BF16)
                PE(lambda e: e.transpose(out=pvb[:, 0:128], in_=kT[:, cs], identity=ident[:]),
                   r=[("mqk", 1), "ident"], w=[("ps", tb)])
                ACT(lambda e: e.activation(out=kpp[:, sb_, :], in_=pvb[:, 0:128], func=AF.Copy,
                                           scale=gg[:, c, hd:hd + 1]),
                    r=[("ps", tb), "gg"], w=[("kpp", sb_)])

            def p1b(c):
                sb_ = c % 3
                db = (4, 5, 1)[sb_]
                PE(lambda e: e.matmul(PS[db][:, 0:257], lhsT=kpp[:, sb_, :], rhs=vaug[:, c, 0:257],
                                      start=True, stop=True),
                   r=[("kpp", sb_), "vaug", "vones"], w=[("ps", db)])
                S.add("dve", lambda e: e.tensor_copy(out=numsb[:, c, :], in_=PS[db][:, 0:257]),
                      [("ps", db), "A1"], [("dsb", c)], war=["A1"])

            for c in range(NT - 1 + 2):
                if c < NT - 1:
                    p1a(c)
                if c >= 2:
                    p1b(c - 2)
            ACT(lambda e: e.activation(out=CTb[:, 1, 0:257], in_=numsb[:, 0, :], func=AF.Copy),
                r=[("dsb", 0)], w=[("CTb", 1)])
            for c in range(1, NT - 1):
                src = numsb[:, 0, :] if c == 1 else CTf[:, (c - 1) % 2, 0:257]
                skey = ("dsb", 0) if c == 1 else ("CTf", (c - 1) % 2)
                DVE(lambda e, c=c, src=src: e.scalar_tensor_tensor(
                    out=CTf[:, c % 2, 0:257], in0=src, scalar=eb[:, c, hd:hd + 1], in1=numsb[:, c, :],
                    op0=ALU.mult, op1=ALU.add),
                    r=[skey, ("dsb", c), "eb"], w=[("CTf", c % 2)])
                ACT(lambda e, c=c: e.activation(out=CTb[:, c + 1, 0:257], in_=CTf[:, c % 2, 0:257], func=AF.Copy),
                    r=[("CTf", c % 2)], w=[("CTb", c + 1)])
            def p2a(c):
                cs = slice(c * 128, (c + 1) * 128)
                sb_ = c % 3
                stb = (2, 3, 0)[sb_]
                PE(lambda e: e.matmul(PS[stb][:, 0:128], lhsT=kT[:, cs], rhs=qT[:, cs], start=True, stop=True),
                   r=[("mqk", 0), ("mqk", 1)], w=[("ps", stb)])
                DVE(lambda e: e.scalar_tensor_tensor(
                    out=scsb[:, sb_, :], in0=PS[stb][:, 0:128], scalar=fk[:, c, hd:hd + 1], in1=maskT,
                    op0=ALU.mult, op1=ALU.mult),
                    r=[("ps", stb), "fk", "CF"], w=[("scsb", sb_)])

            def p2b(c):
                cs = slice(c * 128, (c + 1) * 128)
                sb_ = c % 3
                nb_ = (4, 5, 1)[sb_]
                PE(lambda e: e.matmul(PS[nb_][:, 0:257], lhsT=scsb[:, sb_, :], rhs=vaug[:, c, 0:257],
                                      start=True, stop=(c == 0)),
                   r=[("scsb", sb_), "vaug", "vones"], w=[("ps", nb_)])
                if c > 0:
                    PE(lambda e: e.matmul(PS[nb_][:, 0:257], lhsT=qT[:, cs], rhs=CTb[:, c, 0:257],
                                          start=False, stop=True),
                       r=[("mqk", 0), ("CTb", c)], w=[("ps", nb_)])
                ACT(lambda e: e.activation(out=numsb[:, c, :], in_=PS[nb_][:, 0:257], func=AF.Copy),
                    r=[("ps", nb_), "A1"], w=[("dsb", c)])

            for c in range(NT + 2):
                if c < NT:
                    p2a(c)
                if c >= 2:
                    p2b(c - 2)
            den = numsb[:, :, 256]
            DVE(lambda e, hd=hd: e.tensor_tensor(out=dd, in0=den, in1=fq[:, :, hd], op=ALU.mult),
                r=DK + ["A1", "fq"], w=["dd"])
            DVE(lambda e: e.scalar_tensor_tensor(out=dd, in0=dd, scalar=-1.0, in1=dd, op0=ALU.mult, op1=ALU.max),
                r=["dd"], w=["dd"])
            DVE(lambda e: e.tensor_scalar(out=dd, in0=dd, scalar1=1.0, scalar2=None, op0=ALU.max),
                r=["dd"], w=["dd"])
            DVE(lambda e: e.reciprocal(out=dd, in_=dd), r=["dd"], w=["dd"])
            DVE(lambda e, hd=hd: e.tensor_tensor(out=r2, in0=dd, in1=fq[:, :, hd], op=ALU.mult),
                r=["dd", "fq"], w=["r2"])
            for c in range(NT):
                ACT(lambda e, c=c: e.activation(out=junk[:, 0:256], in_=numsb[:, c, 0:256], func=AF.Square,
                                                scale=r2[:, c:c + 1], accum_out=ss3[:, c:c + 1]),
                    r=[("dsb", c), "A1", "r2"], w=[("ss3", c)])
            ACT(lambda e: e.activation(out=ln3, in_=ss3, func=AF.Ln, scale=1.0 / 256, bias=cfc(C_EPS, 1)),
                r=[("ss3", c) for c in range(NT)] + ["CF"], w=["ln3"])
            ACT(lambda e: e.activation(out=r3, in_=ln3, func=AF.Exp, scale=-0.5), r=["ln3"], w=["r3a"])
            DVE(lambda e: e.tensor_tensor(out=r3, in0=r3, in1=r2, op=ALU.mult), r=["r3a", "r2"], w=["r3"])
            for c in range(NT):
                DVE(lambda e, c=c: e.scalar_tensor_tensor(
                    out=og[:, c, :], in0=numsb[:, c, 0:256], scalar=r3[:, c:c + 1], in1=og[:, c, :],
                    op0=ALU.mult, op1=ALU.mult),
                    r=[("dsb", c), "A1", "r3", ("og", (c // 2) * 2)], w=[("gated", c)])
            for cc in range(2):
                for t0 in range(0, NT, 4):
                    bank = 6 + cnt["pj"] % 2
                    cnt["pj"] += 1
                    pv = PS[bank][:].bitcast(BF16)
                    for tt_ in range(4):
                        t = t0 + tt_
                        PE(lambda e, pv=pv, t=t, tt_=tt_, cc=cc: e.transpose(
                            out=pv[:, tt_ * 128:(tt_ + 1) * 128], in_=og[:, t, cc * 128:(cc + 1) * 128],
                            identity=ident[:]),
                           r=[("gated", t), "ident"], w=[("ps", bank)])
                    ACT(lambda e, pv=pv, t0=t0, cc=cc: e.activation(out=htT[:, cc, t0 * 128:(t0 + 4) * 128],
                                                                    in_=pv[:, 0:512], func=AF.Copy),
                        r=[("ps", bank), "A1"] + [("gated", t_) for t_ in range(NT)], w=[("htT", cc, t0)] + DK)
            wout_s, wkout = w_next(("m_wo", hd))
            wout = wout_s[:, 0:2048].rearrange("p (c n) -> p c n", c=2)
            for t in range(NT):
                for n in range(2):
                    bank = 6 + cnt["pj"] % 2
                    cnt["pj"] += 1
                    for cc in range(2):
                        PE(lambda e, bank=bank, cc=cc, t=t, n=n: e.matmul(
                            PS[bank][:], lhsT=htT[:, cc, t * 128:(t + 1) * 128],
                            rhs=wout[:, cc, n * 512:(n + 1) * 512], start=(cc == 0), stop=(cc == 1)),
                           r=[wkout, ("htT", cc, (t // 4) * 4), "A1"], w=[("ps", bank)])
                    DVE(lambda e, bank=bank, t=t, n=n: e.tensor_tensor(
                        out=X[:, t, n * 512:(n + 1) * 512], in0=PS[bank][:],
                        in1=X[:, t, n * 512:(n + 1) * 512], op=ALU.add),
                        r=[("ps", bank), ("X", t)], w=[("X", t)])

        for hd in range(4):
            m_head(hd)

    def emit_final(s, do_norm):
        if do_norm:
            gb = load_gbc(4)
            emit_stats()
            for t in range(NT):
                DVE(lambda e, t=t: e.scalar_tensor_tensor(out=X[:, t, :], in0=X[:, t, :], scalar=rstd[:, t:t + 1],
                                                          in1=gbc[:, gb, :], op0=ALU.mult, op1=ALU.mult),
                    r=[("X", t), "rstd", ("gbc", gb)], w=[("X", t)])
        for t0 in range(0, NT, 4):
            SPDMA(out_d[s, t0 * 128:(t0 + 4) * 128, :].rearrange("(t p) d -> p t d", p=128), X[:, t0:t0 + 4, :],
                  r=[("X", t) for t in range(t0, t0 + 4)], key=("o", t0 // 4))

    for s in range(nseq):
        for t0 in range(0, NT, 4):
            SPDMA(X[:, t0:t0 + 4, :], x_d[s, t0 * 128:(t0 + 4) * 128, :].rearrange("(t p) d -> p t d", p=128),
                  w=[("X", t) for t in range(t0, t0 + 4)], key=("x", t0 // 4))
        if "attn" in phases:
            emit_norm(0)
            emit_attn()
        if "mlp0" in phases:
            emit_norm(1)
            emit_mlp(0)
        if "mlstm" in phases:
            emit_norm(2)
            emit_mlstm()
        if "mlp1" in phases:
            emit_norm(3)
            emit_mlp(1)
        if "dbg_hnT" in phases:
            emit_norm(1)
            SPDMA(dbg_d, hnT[:].rearrange("p a b -> p (a b)"), r=HN_ALL, key=("c", 5))
        emit_final(s, "final" in phases)

    dl = S.simulate()
    assert dl is None, f"deadlock in schedule: {dl}"
    S.finalize(nc, stack, {"sp": [("o", i) for i in range(4)] + ([("c", 5)] if "dbg_hnT" in phases else []) + ([("c", 6), ("c", 7), ("c", 8), ("c", 9)] if dbgh else [])})
    stack.close()
    return nc


def _t5_bucket_table():
    d = np.arange(256)
    max_exact = 16
    dd = np.maximum(d, 1).astype(np.float32)
    large = max_exact + (np.log(dd / max_exact) / math.log(128 / max_exact) * (32 - max_exact)).astype(np.int32)
    large = np.minimum(large, 31)
    return np.where(d < max_exact, d, large)


def host_consts(inp):
    f32 = np.float32
    cf = np.zeros((128, NCF), f32)
    s_idx = np.arange(128)[:, None]
    j_idx = np.arange(128)[None, :]
    cf[:, C_MASK:C_MASK + 128] = (s_idx <= j_idx).astype(f32)
    cf[:, C_ONES:C_ONES + 128] = 1.0
    rel = np.asarray(inp["rel_bias"], f32)
    cf[:, C_CB:C_CB + 8] = rel[31][None, :]
    for k, nm in enumerate(("attn_lambda_q1", "attn_lambda_k1", "attn_lambda_q2", "attn_lambda_k2")):
        cf[:, C_LQK + 64 * k:C_LQK + 64 * (k + 1)] = np.asarray(inp[nm], f32)[0][None, :]
    cf[:, C_SUBW:C_SUBW + 128] = np.asarray(inp["attn_subln"], f32)[0][None, :]
    cf[:, C_GB:C_GB + 4] = np.asarray(inp["mlstm_b_i"], f32)[0][None, :]
    cf[:, C_GB + 4:C_GB + 8] = np.asarray(inp["mlstm_b_f"], f32)[0][None, :]
    cw = np.asarray(inp["mlstm_conv_w"], f32)[0]
    cf[:, C_CW:C_CW + 32] = cw.reshape(4, 8, 128).transpose(2, 1, 0).reshape(128, 32)
    cf[:, C_CBIAS:C_CBIAS + 8] = np.asarray(inp["mlstm_conv_b"], f32)[0].reshape(8, 128).T
    cf[:, C_EPS] = EPS
    cf[:, C_ONE1] = 1.0
    bt = _t5_bucket_table()
    kk = np.arange(128)[:, None]
    qq = np.arange(256)[None, :]
    dist = qq - kk
    idx = bt[np.clip(dist, 0, 255)]
    tt = rel[idx]
    tt = np.where((dist >= 0)[:, :, None], tt, f32(NEG))
    tt = np.ascontiguousarray(tt.transpose(0, 2, 1)).reshape(128, 8 * 256).astype(f32)
    gains = np.stack([np.asarray(inp["attn_norm"], f32)[0], np.asarray(inp["mlp_norm"], f32)[0],
                      np.asarray(inp["mlstm_norm"], f32)[0], np.asarray(inp["mlp_norm"], f32)[1],
                      np.asarray(inp["final_norm"], f32)], 0)
    gbc = np.ascontiguousarray(np.broadcast_to(gains[:, None, :], (5, 128, D))).astype(f32)
    hwbc = np.ascontiguousarray(np.broadcast_to(np.asarray(inp["mlstm_head_norm"], f32)[0][None, :], (128, D)))
    ident = np.eye(128, dtype=f32).astype(ml_dtypes.bfloat16)
    return dict(cf32=cf, tt=tt, gbc=gbc, hwbc=hwbc, ident=ident)


def make_in_maps(inp, ncores, nseq):
    c = host_consts(inp)
    f32 = np.float32
    shared = dict(
        a_w_in=np.ascontiguousarray(np.asarray(inp["attn_w_in"], f32)[0]),
        a_w_out=np.ascontiguousarray(np.asarray(inp["attn_w_out"], f32)[0]),
        m_w_in=np.ascontiguousarray(np.asarray(inp["mlstm_w_in"], f32)[0]),
        m_w_out=np.ascontiguousarray(np.asarray(inp["mlstm_w_out"], f32)[0]),
        w1=np.ascontiguousarray(np.asarray(inp["mlp_w1"], f32)),
        w2=np.ascontiguousarray(np.asarray(inp["mlp_w2"], f32)),
        **c,
    )
    x = np.asarray(inp["x"], f32)
    maps = []
    for i in range(ncores):
        m = dict(shared)
        m["x"] = np.ascontiguousarray(x[i * nseq:(i + 1) * nseq])
        maps.append(m)
    return maps


_NC_CACHE = {}


def kernel(**inputs):
    nseq = 16 // NCORES
    if "full" not in _NC_CACHE:
        _NC_CACHE["full"] = build(nseq=nseq)
    nc = _NC_CACHE["full"]
    in_maps = make_in_maps(inputs, NCORES, nseq)
    res = run_bass_kernel_spmd(nc, in_maps, core_ids=list(range(NCORES)))
    out = np.concatenate([np.asarray(r["out"]) for r in res.results], axis=0)
    return out.astype(np.float32)
```
